# Optimizing a Trainium2 kernel written in Bass

```python
import math
import jax
import jax.numpy as jnp
from jax import lax
import numpy as np

D_MODEL = 1024
BATCH = 2
SEQ = 16384
DEPTH = 4
DEC_BATCH = 8
DEC_SEQ = 32
PAST_LEN = 4096

CHUNK = 64
N_META = 16
Q_BLOCK = 128
N_MIXERS = 3
NORM_EPS = 1e-6

DA_HEADS = 4
DA_DH = 64
DA_DV = 2 * DA_DH
DA_QK = 2 * DA_HEADS * DA_DH
DA_V = DA_HEADS * DA_DV
DA_IN = 2 * DA_QK + 2 * DA_V

RET_HEADS = 4
RET_DK = D_MODEL // RET_HEADS
RET_DV = 2 * RET_DK
RET_QK = RET_HEADS * RET_DK
RET_V = RET_HEADS * RET_DV
RET_IN = 2 * RET_QK + 2 * RET_V
ROPE_BASE = 10000.0

SB_HEADS = 4
SB_DH = 128
SB_W = SB_HEADS * SB_DH
SB_IN = 4 * SB_W

kernel_name = 'hybrid_stream_diff_ret_stick_step'


def rmsnorm(x, g):
    xf = x.astype(jnp.float32)
    y = xf * lax.rsqrt(jnp.mean(xf * xf, axis=-1, keepdims=True) + NORM_EPS)
    return (y * g.astype(jnp.float32)).astype(x.dtype)


def chunk_index(pos):
    return jnp.floor_divide(pos, CHUNK)


def prompt_blocks(l):
    n = l - N_META
    blocks = [(0, N_META, N_META)]
    for s in range(0, n, Q_BLOCK):
        e = N_META + min(s + Q_BLOCK, n)
        blocks.append((N_META + s, e, e))
    return blocks


def run_blocks(block_fn, q, q_pos, k_all, v_all, k_pos, blocks):
    outs = [block_fn(q[:, a:c], q_pos[a:c], k_all[:, :e], v_all[:, :e], k_pos[:e])
            for (a, c, e) in blocks]
    return jnp.concatenate(outs, axis=1)


def reverse_cumsum(x):
    k = x.shape[-1]
    pad = (-k) % Q_BLOCK
    xp = jnp.pad(x, [(0, 0)] * (x.ndim - 1) + [(pad, 0)])
    nb = (k + pad) // Q_BLOCK
    xb = xp.reshape(*x.shape[:-1], nb, Q_BLOCK)
    idx = jnp.arange(Q_BLOCK)
    upper = (idx[:, None] >= idx[None, :]).astype(x.dtype)
    within = jnp.einsum('...nj,ji->...ni', xb, upper, precision=lax.Precision.HIGHEST)
    nidx = jnp.arange(nb)
    later = (nidx[:, None] > nidx[None, :]).astype(x.dtype)
    tail = jnp.einsum('...m,mn->...n', jnp.sum(xb, axis=-1), later, precision=lax.Precision.HIGHEST)
    out = (within + tail[..., None]).reshape(*x.shape[:-1], nb * Q_BLOCK)
    return out[..., pad:]


def rotary(x, pos):
    half = x.shape[-1] // 2
    inv = ROPE_BASE ** (-jnp.linspace(0.0, 1.0, half, dtype=jnp.float32))
    ang = pos.astype(jnp.float32)[:, None] * inv[None, :]
    cos = jnp.cos(ang)[None, :, None, :]
    sin = jnp.sin(ang)[None, :, None, :]
    xf = x.astype(jnp.float32)
    x1, x2 = xf[..., :half], xf[..., half:]
    return jnp.concatenate([x1 * cos - x2 * sin, x1 * sin + x2 * cos], axis=-1).astype(x.dtype)


def diff_attention_mixer(xn, q_pos, k_pos, blocks, k_past, v_past, w_in, lam_q1, lam_k1, lam_q2, lam_k2,
                         subln_g, w_out, lam_init):
    b, l, _ = xn.shape
    proj = jnp.einsum('bld,de->ble', xn, w_in)
    q, k, v, gate = jnp.split(proj, [DA_QK, 2 * DA_QK, 2 * DA_QK + DA_V], axis=-1)
    q = q.reshape(b, l, 2 * DA_HEADS, DA_DH) * (DA_DH ** -0.5)
    k = k.reshape(b, l, 2 * DA_HEADS, DA_DH)
    v = v.reshape(b, l, DA_HEADS, DA_DV)
    k_all = k if k_past is None else jnp.concatenate([k_past, k], axis=1)
    v_all = v if v_past is None else jnp.concatenate([v_past, v], axis=1)
    lam = (jnp.exp(jnp.sum(lam_q1.astype(jnp.float32) * lam_k1.astype(jnp.float32)))
           - jnp.exp(jnp.sum(lam_q2.astype(jnp.float32) * lam_k2.astype(jnp.float32))) + lam_init)

    def block(qb, pb, kb, vb, kpb):
        nq, nk = qb.shape[1], kb.shape[1]
        s = jnp.einsum('bqnd,bknd->bnqk', qb, kb).astype(jnp.float32)
        visible = chunk_index(kpb)[None, :] <= chunk_index(pb)[:, None]
        s = jnp.where(visible, s, -jnp.inf).reshape(b, DA_HEADS, 2, nq, nk)
        e = jnp.exp(s - jnp.max(s, axis=-1, keepdims=True))
        denom = jnp.moveaxis(jnp.sum(e, axis=-1), 3, 1)
        o = jnp.einsum('bhmqk,bkhe->bqhme', e.astype(vb.dtype), vb).astype(jnp.float32)
        o = o / denom[..., None]
        return o[:, :, :, 0] - lam * o[:, :, :, 1]

    o = run_blocks(block, q, q_pos, k_all, v_all, k_pos, blocks)
    o = rmsnorm(o.astype(xn.dtype), subln_g) * (1.0 - lam_init)
    o = o.reshape(b, l, DA_V) * jax.nn.silu(gate)
    return jnp.einsum('ble,ed->bld', o, w_out), k, v


def retention_chunk(state, q, k, v, log_gamma):
    c = q.shape[1]
    idx = jnp.arange(c, dtype=jnp.float32)
    rel = idx[:, None] - idx[None, :]
    decay = jnp.where(rel >= 0, jnp.exp(jnp.maximum(rel, 0.0)[None] * log_gamma[:, None, None]), 0.0)
    scores = jnp.einsum('bihd,bjhd->bhij', q, k) * decay
    o_inner = jnp.einsum('bhij,bjhe->bihe', scores, v)
    q_dec = q * jnp.exp((idx + 1.0)[:, None] * log_gamma[None, :])[None, :, :, None]
    o_cross = jnp.einsum('bihd,bhde->bihe', q_dec, state)
    k_dec = k * jnp.exp((c - 1.0 - idx)[:, None] * log_gamma[None, :])[None, :, :, None]
    new_state = (jnp.exp(c * log_gamma)[None, :, None, None] * state
                 + jnp.einsum('bjhd,bjhe->bhde', k_dec, v))
    return new_state, o_inner + o_cross


def retention_mixer(xn, pos, state, w_in, gn_g, w_out):
    b, l, _ = xn.shape
    proj = jnp.einsum('bld,de->ble', xn, w_in)
    q, k, v, gate = jnp.split(proj, [RET_QK, 2 * RET_QK, 2 * RET_QK + RET_V], axis=-1)
    q = rotary(q.reshape(b, l, RET_HEADS, RET_DK), pos)
    k = rotary(k.reshape(b, l, RET_HEADS, RET_DK), pos) * (RET_DK ** -0.5)
    v = v.reshape(b, l, RET_HEADS, RET_DV)
    log_gamma = jnp.log(1.0 - 2.0 ** (-5.0 - jnp.arange(RET_HEADS, dtype=jnp.float32)))
    qf, kf, vf = q.astype(jnp.float32), k.astype(jnp.float32), v.astype(jnp.float32)
    if state is None:
        pad = (-l) % CHUNK
        nc = (l + pad) // CHUNK

        def to_chunks(a):
            a = jnp.pad(a, [(0, 0), (pad, 0), (0, 0), (0, 0)])
            return jnp.moveaxis(a.reshape(b, nc, CHUNK, *a.shape[2:]), 1, 0)

        s0 = jnp.zeros((b, RET_HEADS, RET_DK, RET_DV), jnp.float32)
        s_fin, o = lax.scan(lambda s, xs: retention_chunk(s, xs[0], xs[1], xs[2], log_gamma),
                            s0, (to_chunks(qf), to_chunks(kf), to_chunks(vf)))
        o = jnp.moveaxis(o, 0, 1).reshape(b, nc * CHUNK, RET_HEADS, RET_DV)[:, pad:]
    else:
        s_fin, o = retention_chunk(state.astype(jnp.float32), qf, kf, vf, log_gamma)
    o = rmsnorm(o.astype(xn.dtype), gn_g)
    o = o.reshape(b, l, RET_V) * jax.nn.silu(gate)
    return jnp.einsum('ble,ed->bld', o, w_out), s_fin.astype(xn.dtype)


def stick_breaking_mixer(xn, q_pos, k_pos, blocks, k_past, v_past, w_in, w_out):
    b, l, _ = xn.shape
    proj = jnp.einsum('bld,de->ble', xn, w_in)
    q, k, v, gate = jnp.split(proj, 4, axis=-1)
    q = q.reshape(b, l, SB_HEADS, SB_DH) * (SB_DH ** -0.5)
    k = k.reshape(b, l, SB_HEADS, SB_DH)
    v = v.reshape(b, l, SB_HEADS, SB_DH)
    k_all = k if k_past is None else jnp.concatenate([k_past, k], axis=1)
    v_all = v if v_past is None else jnp.concatenate([v_past, v], axis=1)

    def block(qb, pb, kb, vb, kpb):
        z = jnp.einsum('bqhd,bkhd->bhqk', qb, kb).astype(jnp.float32)
        earlier = kpb[None, :] < pb[:, None]
        log_keep = jnp.where(earlier, -jax.nn.softplus(z), 0.0)
        tail = reverse_cumsum(log_keep)
        a = jnp.exp(jnp.where(earlier, z + tail, -jnp.inf))
        return jnp.einsum('bhqk,bkhd->bqhd', a.astype(vb.dtype), vb)

    o = run_blocks(block, q, q_pos, k_all, v_all, k_pos, blocks)
    o = o.reshape(b, l, SB_W) * jax.nn.silu(gate)
    return jnp.einsum('ble,ed->bld', o, w_out), k, v


def setup_inputs(seed: int = 0) -> dict:
    key = jax.random.key(seed)
    keys = jax.random.split(key, 48)
    counter = [0]

    def nxt():
        kk = keys[counter[0]]
        counter[0] += 1
        return kk

    def nrm(shape, scale):
        return scale * jax.random.normal(nxt(), shape, jnp.float32)

    def gain(shape):
        return 1.0 + 0.02 * jax.random.normal(nxt(), shape, jnp.float32)

    d = D_MODEL
    return {
        'x_prompt': nrm((BATCH, SEQ, d), 1.0),
        'x_sample': nrm((DEC_BATCH, DEC_SEQ, d), 1.0),
        'cache_k_l0': nrm((DEC_BATCH, PAST_LEN, 2 * DA_HEADS, DA_DH), 1.0),
        'cache_v_l0': nrm((DEC_BATCH, PAST_LEN, DA_HEADS, DA_DV), 1.0),
        'state_ret_l1': nrm((DEC_BATCH, RET_HEADS, RET_DK, RET_DV), 0.5),
        'cache_k_l2': nrm((DEC_BATCH, PAST_LEN, SB_HEADS, SB_DH), 1.0),
        'cache_v_l2': nrm((DEC_BATCH, PAST_LEN, SB_HEADS, SB_DH), 1.0),
        'cache_k_l3': nrm((DEC_BATCH, PAST_LEN, 2 * DA_HEADS, DA_DH), 1.0),
        'cache_v_l3': nrm((DEC_BATCH, PAST_LEN, DA_HEADS, DA_DV), 1.0),
        'meta_tokens': nrm((N_META, d), 1.0),
        'norm_g_0': gain((d,)),
        'w_in_0': nrm((d, DA_IN), d ** -0.5),
        'lam_q1_0': nrm((DA_DH,), 0.1),
        'lam_k1_0': nrm((DA_DH,), 0.1),
        'lam_q2_0': nrm((DA_DH,), 0.1),
        'lam_k2_0': nrm((DA_DH,), 0.1),
        'subln_g_0': gain((DA_DV,)),
        'w_out_0': nrm((DA_V, d), DA_V ** -0.5),
        'norm_g_1': gain((d,)),
        'w_in_1': nrm((d, RET_IN), d ** -0.5),
        'gn_g_1': gain((RET_HEADS, RET_DV)),
        'w_out_1': nrm((RET_V, d), RET_V ** -0.5),
        'norm_g_2': gain((d,)),
        'w_in_2': nrm((d, SB_IN), d ** -0.5),
        'w_out_2': nrm((SB_W, d), SB_W ** -0.5),
        'norm_g_3': gain((d,)),
        'w_in_3': nrm((d, DA_IN), d ** -0.5),
        'lam_q1_3': nrm((DA_DH,), 0.1),
        'lam_k1_3': nrm((DA_DH,), 0.1),
        'lam_q2_3': nrm((DA_DH,), 0.1),
        'lam_k2_3': nrm((DA_DH,), 0.1),
        'subln_g_3': gain((DA_DV,)),
        'w_out_3': nrm((DA_V, d), DA_V ** -0.5),
        'norm_g_final': gain((d,)),
    }


def reference(x_prompt, x_sample, cache_k_l0, cache_v_l0, state_ret_l1, cache_k_l2, cache_v_l2,
              cache_k_l3, cache_v_l3, meta_tokens,
              norm_g_0, w_in_0, lam_q1_0, lam_k1_0, lam_q2_0, lam_k2_0, subln_g_0, w_out_0,
              norm_g_1, w_in_1, gn_g_1, w_out_1,
              norm_g_2, w_in_2, w_out_2,
              norm_g_3, w_in_3, lam_q1_3, lam_k1_3, lam_q2_3, lam_k2_3, subln_g_3, w_out_3,
              norm_g_final):
    b = x_prompt.shape[0]
    meta = jnp.broadcast_to(meta_tokens.astype(x_prompt.dtype)[None], (b, N_META, D_MODEL))
    xp = jnp.concatenate([meta, x_prompt], axis=1)
    xs = x_sample
    l_p = xp.shape[1]
    s_len = xs.shape[1]
    past = cache_k_l0.shape[1]
    pos_p = jnp.arange(l_p, dtype=jnp.int32) - N_META
    pos_s = past + jnp.arange(s_len, dtype=jnp.int32)
    kpos_s = jnp.arange(past + s_len, dtype=jnp.int32)
    blocks_p = prompt_blocks(l_p)
    blocks_s = [(0, s_len, past + s_len)]

    layers = [
        (norm_g_0, w_in_0, w_out_0, (lam_q1_0, lam_k1_0, lam_q2_0, lam_k2_0, subln_g_0), (cache_k_l0, cache_v_l0)),
        (norm_g_1, w_in_1, w_out_1, (gn_g_1,), (state_ret_l1,)),
        (norm_g_2, w_in_2, w_out_2, (), (cache_k_l2, cache_v_l2)),
        (norm_g_3, w_in_3, w_out_3, (lam_q1_3, lam_k1_3, lam_q2_3, lam_k2_3, subln_g_3), (cache_k_l3, cache_v_l3)),
    ]
    new_states = []
    for i in range(DEPTH):
        norm_g, w_in, w_out, extra, cache = layers[i]
        kind = i % N_MIXERS
        hp = rmsnorm(xp, norm_g)
        hs = rmsnorm(xs, norm_g)
        if kind == 0:
            lam_init = 0.8 - 0.6 * math.exp(-0.3 * i)
            op, kp, vp = diff_attention_mixer(hp, pos_p, pos_p, blocks_p, None, None, w_in, *extra, w_out, lam_init)
            os_, ks_, vs_ = diff_attention_mixer(hs, pos_s, kpos_s, blocks_s, cache[0], cache[1], w_in, *extra,
                                                 w_out, lam_init)
            new_states.append((kp, vp, ks_, vs_))
        elif kind == 1:
            op, sp = retention_mixer(hp, pos_p, None, w_in, extra[0], w_out)
            os_, ss = retention_mixer(hs, pos_s, cache[0], w_in, extra[0], w_out)
            new_states.append((sp, ss))
        else:
            op, kp, vp = stick_breaking_mixer(hp, pos_p, pos_p, blocks_p, None, None, w_in, w_out)
            os_, ks_, vs_ = stick_breaking_mixer(hs, pos_s, kpos_s, blocks_s, cache[0], cache[1], w_in, w_out)
            new_states.append((kp, vp, ks_, vs_))
        xp = xp + op
        xs = xs + os_

    y_prompt = rmsnorm(xp, norm_g_final)[:, N_META:]
    y_sample = rmsnorm(xs, norm_g_final)
    k0_p, v0_p, k0_s, v0_s = new_states[0]
    ret1_p, ret1_s = new_states[1]
    k2_p, v2_p, k2_s, v2_s = new_states[2]
    k3_p, v3_p, k3_s, v3_s = new_states[3]
    return (y_prompt, y_sample, k0_p, v0_p, k0_s, v0_s, ret1_p, ret1_s,
            k2_p, v2_p, k2_s, v2_s, k3_p, v3_p, k3_s, v3_s)
```

```python
import contextlib
import math
import numpy as np
import concourse.bass as bass
import concourse.mybir as mybir
from concourse.bass_utils import run_bass_kernel_spmd

F32 = mybir.dt.float32
BF16 = mybir.dt.bfloat16
AF = mybir.ActivationFunctionType
ALU = mybir.AluOpType

SEQ = 16384
PAST = 4096
EPS = 1e-6
STOP = 0


class R:
    __slots__ = ("w", "rd")

    def __init__(self):
        self.w = None
        self.rd = {}


class KB:
    EPOCH = 30000
    NDMA = 40

    def __init__(self, nc):
        self.nc = nc
        self.es = contextlib.ExitStack()
        self.h = {"pe": nc.tensor, "act": nc.scalar, "dve": nc.vector, "pool": nc.gpsimd, "sp": nc.sync}
        self.sem = {}
        self.cnt = {}
        self.seen = {e: {} for e in self.h}
        self.nsem = 0
        for e in self.h:
            self._new_epoch(e)
        self.dsem = [self._mksem("d%d" % i) for i in range(self.NDMA)]
        self.duse = [0] * self.NDMA
        self.dnext = 0
        self.csems = []
        self.ninst = 0
        self.stopped = False
        KBREF[0] = self

    def _mksem(self, name):
        self.nsem += 1
        return self.es.enter_context(self.nc.semaphore("s%s_%d" % (name, self.nsem)))

    def _new_epoch(self, e):
        self.sem[e] = self._mksem(e)
        self.cnt[e] = 0

    def sb(self, name, shape, dt, es=None):
        return (es or self.es).enter_context(self.nc.sbuf_tensor("s_" + name, list(shape), dt))

    def ps(self, name, shape, dt):
        return self.es.enter_context(self.nc.psum_tensor("p_" + name, list(shape), dt))

    def _deps(self, eng, reads, writes, is_dma):
        need = {}

        def add(t, war=False):
            if t is None:
                return
            s, v, te = t
            if (not is_dma) and te == eng and (war or eng == "pe"):
                return
            k = id(s)
            if k not in need or need[k][1] < v:
                need[k] = (s, v)

        for r in reads:
            add(r.w)
        for r in writes:
            add(r.w)
            for t in r.rd.values():
                add(t, True)
        seen = self.seen[eng]
        for k, (s, v) in need.items():
            if seen.get(k, 0) < v:
                self.h[eng].wait_ge(s, v)
                seen[k] = v

    def op(self, eng, fn, reads=(), writes=()):
        if self.stopped:
            return None
        self._deps(eng, reads, writes, False)
        if self.cnt[eng] >= self.EPOCH:
            self._new_epoch(eng)
        inst = fn(self.h[eng])
        self.cnt[eng] += 1
        self.ninst += 1
        inst.then_inc(self.sem[eng], 1)
        tok = (self.sem[eng], self.cnt[eng], eng)
        for r in writes:
            r.w = tok
            r.rd = {}
        for r in reads:
            r.rd[eng] = tok
        return inst

    def _dtok(self, q):
        i = self.dnext
        self.dnext = (self.dnext + 1) % self.NDMA
        s = self.dsem[i]
        pv = 16 * self.duse[i]
        if pv and self.seen[q].get(id(s), 0) < pv:
            self.h[q].wait_ge(s, pv)
            self.seen[q][id(s)] = pv
        self.duse[i] += 1
        return i, s, 16 * self.duse[i]

    def dma(self, q, out, in_, reads=(), writes=()):
        if self.stopped:
            return None
        self._deps(q, reads, writes, True)
        i, s, v = self._dtok(q)
        inst = self.h[q].dma_start(out=out, in_=in_)
        inst.then_inc(s, 16)
        self.ninst += 1
        tok = (s, v, None)
        for r in writes:
            r.w = tok
            r.rd = {}
        for r in reads:
            r.rd[("d", i)] = tok
        return inst

    def coll(self, kind, ins, outs, groups, reads=(), writes=()):
        if self.stopped:
            return None
        q = "pool"
        self._deps(q, reads, writes, True)
        if not self.csems:
            self.csems.append(self._mksem("cc"))
            self.ccnt = 0
        s = self.csems[0]
        inst = self.nc.gpsimd.collective_compute(kind, ALU.bypass, replica_groups=groups, ins=ins, outs=outs)
        inst.then_inc(s, 1)
        self.ccnt += 1
        tok = (s, self.ccnt, None)
        for r in writes:
            r.w = tok
            r.rd = {}
        for r in reads:
            r.rd[("c", 0)] = tok

    def barrier(self):
        if self.stopped:
            return
        toks = [(self.sem[e], self.cnt[e]) for e in ("pe", "act", "dve", "pool") if self.cnt[e]]
        toks += [(s, 16 * self.duse[i]) for i, s in enumerate(self.dsem) if self.duse[i]]
        toks += [(s, self.ccnt) for s in self.csems]
        for e in self.h:
            for s, v in toks:
                if s is self.sem[e] and e != "sp":
                    continue
                if self.seen[e].get(id(s), 0) < v:
                    self.h[e].wait_ge(s, v)
                    self.seen[e][id(s)] = v


class StopBuild(Exception):
    pass


CKN = [0]
KBREF = [None]


def ck(tag):
    CKN[0] += 1
    if STOP >= 10 and CKN[0] == STOP - 9:
        print('STOP at', tag)
        KBREF[0].barrier()
        KBREF[0].stopped = True


class Grp:
    pass


class KT_:
    pass


def build(SEQ, PAST):
    NF = SEQ // 128
    NGF = NF // 4
    NP = 16 + SEQ
    NT = NP + 128
    NCB = PAST // 128
    NVT = max(NF + 1, 4 * NCB + 4)
    assert 4 * PAST <= NP
    NCH = [4, 16, 4, 4]
    WIN_COLS = [512, 1536, 512, 512]
    LAM_INIT = {0: 0.8 - 0.6 * math.exp(0.0), 3: 0.8 - 0.6 * math.exp(-0.9)}

    nc = bass.Bass("TRN2", target_bir_lowering=False, dynamic_dma_scratch_size=4096)

    def din(name, shape, dt=F32):
        return nc.dram_tensor(name, list(shape), dt, kind="ExternalInput").ap()

    def dout(name, shape, dt=F32):
        return nc.dram_tensor(name, list(shape), dt, kind="ExternalOutput").ap()

    xin = din("xin", [NT, 1024])
    win = [din("win%d" % L, [1024, WIN_COLS[L]]) for L in range(4)]
    ngin = [din("ng%d" % L, [128, 8]) for L in range(4)]
    wout = [din("wout%d" % L, [NCH[L] * 128, 1024]) for L in range(4)]
    lamin = {L: din("lam%d" % L, [1, 256]) for L in (0, 3)}
    subg = {L: din("subg%d" % L, [1, 128]) for L in (0, 3)}
    gn1 = din("gn1", [1, 512])
    ngf = din("ngf", [1, 1024])
    ckin = {L: din("ck%d" % L, [128, 4 * PAST]) for L in (0, 2, 3)}
    cvin = {L: din("cv%d" % L, [4, PAST, 128]) for L in (0, 2, 3)}
    st1 = din("st1", [4, 256, 512])
    cstin = din("cst", [128, 776])
    cosin = din("cos", [128, NT])
    sinin = din("sin", [128, NT])

    yo = dout("y", [NT, 1024])
    kTo = {L: dout("kT%d" % L, [128, NT]) for L in (0, 2, 3)}
    vo = {L: dout("v%d" % L, [NT, 128]) for L in (0, 2, 3)}
    retp = dout("retp", [256, 512])
    rets = dout("rets", [4, 256, 512])

    xs = nc.dram_tensor("xs", [NT, 1024], F32).ap()
    NGRP = NGF + 2
    bounce = [nc.dram_tensor("bnc%d" % L, [NGRP, NCH[L] // 4 * 128, 512], BF16) for L in range(4)]
    gath = [nc.dram_tensor("gth%d" % L, [NGRP, NCH[L] * 128, 512], BF16) for L in range(4)]

    kb = KB(nc)
    op = kb.op
    dma = kb.dma

    groups = []
    g = Grp(); g.idx = 0; g.c0 = 0; g.tiles = [(0, 16)]; g.kind = "meta"; g.vt0 = 0; groups.append(g)
    for i in range(NGF):
        g = Grp(); g.idx = 1 + i; g.c0 = 16 + 512 * i; g.tiles = [(128 * t, 128) for t in range(4)]
        g.kind = "frame"; g.vt0 = 1 + 4 * i; groups.append(g)
    g = Grp(); g.idx = 1 + NGF; g.c0 = NP; g.tiles = [(32 * t, 32) for t in range(4)]; g.kind = "sample"
    g.vt0 = 4 * NCB; groups.append(g)
    for g in groups:
        g.ncols = sum(n for _, n in g.tiles)

    with kb.es:
        cst = kb.sb("cst", [128, 776], F32); rcst = R()
        identb = kb.sb("identb", [128, 128], BF16)
        negUb = kb.sb("negUb", [128, 128], BF16)
        negOb = kb.sb("negOb", [128, 128], BF16)
        onesf = kb.sb("onesf", [128, 128], F32)
        rcb = R()
        epst = kb.sb("epst", [128, 1], F32)
        onet = kb.sb("onet", [128, 1], F32)
        xt = [kb.sb("xt%d" % i, [128, 1024], F32) for i in range(2)]; rxt = [R(), R()]
        sqj = kb.sb("sqj", [128, 1024], BF16); rsqj = R()
        xh = kb.sb("xh", [128, 1024], BF16); rxh = R()
        ss = [kb.sb("ss%d" % i, [128, 4], F32) for i in range(2)]; rss = [R(), R()]
        xnT = kb.sb("xnT", [128, 8, 512], BF16); rxnT = R()
        qT = kb.sb("qT", [128, 2, 512], BF16); rqT = R(); rqTp = [R(), R()]; rsgp = [R(), R()]
        kst = kb.sb("kst", [128, 512], F32); rkst = R()
        vst = [kb.sb("vst%d" % i, [128, 128], F32) for i in range(2)]; rvst = [R(), R()]
        ogT = kb.sb("ogT", [128, 4, 512], BF16); rogT = R()
        oga = [kb.sb("oga%d" % i, [128, 16, 128], BF16) for i in range(2)]; roga = [R(), R()]
        TF = [kb.sb("tf%d" % i, [128, 512], F32) for i in range(6)]; rTF = [R() for _ in range(6)]
        TF6 = kb.sb("tf6", [128, 512], F32); rTF6 = R()
        TH = [kb.sb("th%d" % i, [128, 512], BF16) for i in range(5)]; rTH = [R() for _ in range(5)]
        wst = [kb.sb("wst%d" % i, [128, 1024], F32) for i in range(2)]; rwst = [R(), R()]
        sg = kb.sb("sg", [128, 4, 512], F32); rsg = R()
        rd = kb.sb("rd", [128, 8], F32); rrd = R()
        lamt = kb.sb("lamt", [128, 8], F32); rlam = R()
        lrow = kb.sb("lrow", [1, 264], F32); rlrow = R()
        Gt = kb.sb("Gt", [128, 512], F32); rGt = R()
        grow = wst[0]; rgrow = rwst[0]
        B = [kb.ps("pb%d" % i, [128, 512], F32) for i in range(7)]; rB = [R() for _ in range(7)]
        TRb = kb.ps("trb", [128, 8, 128], BF16); rTR = R()
        rxs = [R() for _ in groups]
        rgath = [[R() for _ in groups] for _ in range(4)]
        rbnc = [[R() for _ in groups] for _ in range(4)]
        rko = R(); rvo = R(); ryo = R()

        ident = cst[:, 0:128]
        Mstr = cst[:, 384:512]
        decT = cst[:, 512:640]
        dq = cst[:, 640:768]

        def ncol(n):
            return {128: 0, 16: 1, 32: 2}[n]

        dma("sp", cst[:], cstin, writes=[rcst])
        op("dve", lambda h: h.tensor_copy(out=identb[:], in_=cst[:, 0:128]), reads=[rcst], writes=[rcb])
        op("dve", lambda h: h.tensor_copy(out=negUb[:], in_=cst[:, 128:256]), reads=[rcst], writes=[rcb])
        op("dve", lambda h: h.tensor_copy(out=negOb[:], in_=cst[:, 256:384]), reads=[rcst], writes=[rcb])
        op("dve", lambda h: h.memset(onesf[:], 1.0), writes=[rcb])
        op("dve", lambda h: h.memset(epst[:], EPS), writes=[rcb])
        op("dve", lambda h: h.memset(onet[:], 1.0), writes=[rcb])

        def bcast_row(dst, row_ap, n, scale=1.0):
            dma("sp", grow[0:1, 0:n], row_ap, writes=[rgrow])
            for c in range(0, n, 512):
                w = min(512, n - c)
                op("pe", lambda h: h.matmul(out=B[0][:, 0:w], lhsT=onesf[0:1, 0:128], rhs=grow[0:1, c:c + w],
                                            start=True, stop=True), reads=[rgrow, rcb], writes=[rB[0]])
                op("act", lambda h: h.activation(out=dst[:, c:c + w], in_=B[0][:, 0:w], func=AF.Copy, scale=scale),
                   reads=[rB[0]], writes=[rGt])

        def load_weights(L, ls):
            W = kb.sb("W%d" % L, [128, 8, WIN_COLS[L]], BF16, ls); rW = R()
            ngt = kb.sb("ngt%d" % L, [128, 8], F32, ls); rng = R()
            dma("sp", ngt[:], ngin[L], writes=[rng])
            wc = WIN_COLS[L]
            i = 0
            for kc in range(8):
                for c0 in range(0, wc, 1024):
                    w = min(1024, wc - c0)
                    w_ = wst[i % 2]; rw_ = rwst[i % 2]; i += 1
                    dma("sp", w_[:, 0:w], win[L][kc * 128:(kc + 1) * 128, c0:c0 + w], writes=[rw_])
                    op("dve", lambda h: h.tensor_scalar(out=W[:, kc, c0:c0 + w], in0=w_[:, 0:w], scalar1=ngt[:, kc:kc + 1],
                                                        scalar2=None, op0=ALU.mult), reads=[rw_, rng], writes=[rW])
            return W, rW

        def load_wout(L, ls):
            n = NCH[L]
            Wo = kb.sb("Wo%d" % L, [128, n, 1024], BF16, ls); rWo = R()
            for cc in range(n):
                w_ = wst[cc % 2]; rw_ = rwst[cc % 2]
                dma("sp", w_[:, 0:1024], wout[L][cc * 128:(cc + 1) * 128, :], writes=[rw_])
                op("pool", lambda h: h.tensor_copy(out=Wo[:, cc, :], in_=w_[:, 0:1024]), reads=[rw_], writes=[rWo])
            return Wo, rWo

        def compute_lam(L):
            dma("sp", lrow[0:1, 0:256], lamin[L], writes=[rlrow])
            for i in range(2):
                op("dve", lambda h: h.tensor_tensor(out=lrow[0:1, 128 * i:128 * i + 64], in0=lrow[0:1, 128 * i:128 * i + 64],
                                                    in1=lrow[0:1, 128 * i + 64:128 * i + 128], op=ALU.mult),
                   reads=[rlrow], writes=[rlrow])
                op("dve", lambda h: h.tensor_reduce(out=lrow[0:1, 256 + i:257 + i], in_=lrow[0:1, 128 * i:128 * i + 64],
                                                    axis=mybir.AxisListType.X, op=ALU.add), reads=[rlrow], writes=[rlrow])
            op("act", lambda h: h.activation(out=lrow[0:1, 258:260], in_=lrow[0:1, 256:258], func=AF.Exp),
               reads=[rlrow], writes=[rlrow])
            op("dve", lambda h: h.tensor_tensor(out=lrow[0:1, 260:261], in0=lrow[0:1, 258:259], in1=lrow[0:1, 259:260],
                                                op=ALU.subtract), reads=[rlrow], writes=[rlrow])
            op("dve", lambda h: h.tensor_scalar(out=lrow[0:1, 261:262], in0=lrow[0:1, 260:261], scalar1=-1.0,
                                                scalar2=-LAM_INIT[L], op0=ALU.mult, op1=ALU.add),
               reads=[rlrow], writes=[rlrow])
            op("pe", lambda h: h.matmul(out=B[0][:, 0:2], lhsT=onesf[0:1, 0:128], rhs=lrow[0:1, 260:262],
                                        start=True, stop=True), reads=[rlrow, rcb], writes=[rB[0]])
            op("act", lambda h: h.activation(out=lamt[:, 0:2], in_=B[0][:, 0:2], func=AF.Copy), reads=[rB[0]], writes=[rlam])

        def group_coll(L, G):
            kb.coll("AllGather", [bounce[L].ap()[G.idx].opt()], [gath[L].ap()[G.idx].opt()], [[0, 1, 2, 3], [4, 5, 6, 7]],
                    reads=[rbnc[L][G.idx]], writes=[rgath[L][G.idx]])

        def phase_a(L, G, Wo, rWo, ob=(0, 1)):
            src = xin if L <= 1 else xs
            n = NCH[L - 1] if L >= 1 else 0

            def loads(ti):
                ofs, nr = G.tiles[ti]
                tok0 = G.c0 + ofs
                dma("sp", xt[ti % 2][0:nr, :], src[tok0:tok0 + nr, :], reads=[rxs[G.idx]] if L >= 2 else [], writes=[rxt[ti % 2]])
                if L >= 1:
                    dma("sp", oga[ti % 2][:, 0:n, 0:nr], gath[L - 1].ap()[G.idx][:, ofs:ofs + nr].rearrange("(c p) n -> p c n", p=128),
                        reads=[rgath[L - 1][G.idx]], writes=[roga[ti % 2]])

            loads(0)
            for ti, (ofs, nr) in enumerate(G.tiles):
                tok0 = G.c0 + ofs
                x_ = xt[ti % 2]; rx_ = rxt[ti % 2]
                s_ = ss[ti % 2]; rs_ = rss[ti % 2]
                if ti + 1 < len(G.tiles):
                    loads(ti + 1)
                if L >= 1:
                    og_ = oga[ti % 2]; rog_ = roga[ti % 2]
                    for hf in range(2):
                        bk_ = ob[hf]
                        for cc in range(n):
                            op("pe", lambda h: h.matmul(out=B[bk_][0:nr, :], lhsT=og_[:, cc, 0:nr],
                                                        rhs=Wo[:, cc, hf * 512:(hf + 1) * 512], start=(cc == 0),
                                                        stop=(cc == n - 1)), reads=[rog_, rWo], writes=[rB[bk_]])
                        op("dve", lambda h: h.tensor_tensor(out=x_[0:nr, hf * 512:(hf + 1) * 512], in0=B[bk_][0:nr, :],
                                                            in1=x_[0:nr, hf * 512:(hf + 1) * 512], op=ALU.add),
                           reads=[rB[bk_], rx_], writes=[rx_])
                        yield
                    dma("sp", xs[tok0:tok0 + nr, :], x_[0:nr, :], reads=[rx_], writes=[rxs[G.idx]])
                else:
                    yield
                op("act", lambda h: h.activation(out=sqj[0:nr, :], in_=x_[0:nr, :], func=AF.Square,
                                                 accum_out=s_[0:nr, 0:1]), reads=[rx_], writes=[rsqj, rs_])
                op("act", lambda h: h.activation(out=s_[0:nr, 1:2], in_=s_[0:nr, 0:1], func=AF.Sqrt, scale=1.0 / 1024,
                                                 bias=epst[0:nr, 0:1]), reads=[rs_, rcb], writes=[rs_])
                op("dve", lambda h: h.reciprocal(out=s_[0:nr, 2:3], in_=s_[0:nr, 1:2]), reads=[rs_], writes=[rs_])
                op("dve", lambda h: h.tensor_scalar(out=xh[0:nr, :], in0=x_[0:nr, :], scalar1=s_[0:nr, 2:3],
                                                    scalar2=None, op0=ALU.mult), reads=[rx_, rs_], writes=[rxh])
                yield
                for kc in range(8):
                    op("pe", lambda h: h.transpose(out=TRb[:, kc, 0:nr], in_=xh[0:nr, kc * 128:(kc + 1) * 128],
                                                   identity=identb[0:nr, 0:nr]), reads=[rxh, rcb], writes=[rTR])
                op("act", lambda h: h.activation(out=xnT[:, :, ofs:ofs + nr], in_=TRb[:, :, 0:nr], func=AF.Copy),
                   reads=[rTR], writes=[rxnT])
                yield

        def final_pass(Wo, rWo):
            n = NCH[3]
            tl = [(G, ofs, nr) for G in groups for (ofs, nr) in G.tiles]

            def loads(i):
                G, ofs, nr = tl[i]
                tok0 = G.c0 + ofs
                dma("sp", xtf[i % 4][0:nr, :], xs[tok0:tok0 + nr, :], reads=[rxs[G.idx]], writes=[rxtf[i % 4]])
                dma("sp", ogaf[i % 4][:, 0:n, 0:nr], gath[3].ap()[G.idx][:, ofs:ofs + nr].rearrange("(c p) n -> p c n", p=128),
                    reads=[rgath[3][G.idx]], writes=[rogaf[i % 4]])

            loads(0)
            loads(1)
            for i, (G, ofs, nr) in enumerate(tl):
                tok0 = G.c0 + ofs
                if i + 2 < len(tl):
                    loads(i + 2)
                x_ = xtf[i % 4]; rx_ = rxtf[i % 4]; s_ = ssf[i % 4]; rs_ = rssf[i % 4]
                og_ = ogaf[i % 4]; rog_ = rogaf[i % 4]
                ob = ((0, 1), (2, 3), (4, 5))[i % 3]
                for hf in range(2):
                    bk_ = ob[hf]
                    for cc in range(n):
                        op("pe", lambda h: h.matmul(out=B[bk_][0:nr, :], lhsT=og_[:, cc, 0:nr],
                                                    rhs=Wo[:, cc, hf * 512:(hf + 1) * 512], start=(cc == 0),
                                                    stop=(cc == n - 1)), reads=[rog_, rWo], writes=[rB[bk_]])
                    op("dve", lambda h: h.tensor_tensor(out=x_[0:nr, hf * 512:(hf + 1) * 512], in0=B[bk_][0:nr, :],
                                                        in1=x_[0:nr, hf * 512:(hf + 1) * 512], op=ALU.add),
                       reads=[rB[bk_], rx_], writes=[rx_])
                op("act", lambda h: h.activation(out=sqj[0:nr, :], in_=x_[0:nr, :], func=AF.Square,
                                                 accum_out=s_[0:nr, 0:1]), reads=[rx_], writes=[rsqj, rs_])
                op("act", lambda h: h.activation(out=s_[0:nr, 1:2], in_=s_[0:nr, 0:1], func=AF.Sqrt, scale=1.0 / 1024,
                                                 bias=epst[0:nr, 0:1]), reads=[rs_, rcb], writes=[rs_])
                op("dve", lambda h: h.reciprocal(out=s_[0:nr, 2:3], in_=s_[0:nr, 1:2]), reads=[rs_], writes=[rs_])
                for hf in range(2):
                    eng = "dve" if hf == 0 else "pool"
                    if eng == "dve":
                        op("dve", lambda h: h.scalar_tensor_tensor(out=x_[0:nr, 0:512], in0=x_[0:nr, 0:512],
                                                                   scalar=s_[0:nr, 2:3], in1=GF[0][0:nr, :],
                                                                   op0=ALU.mult, op1=ALU.mult),
                           reads=[rx_, rs_, rGF], writes=[rx_])
                    else:
                        op("dve", lambda h: h.scalar_tensor_tensor(out=x_[0:nr, 512:1024], in0=x_[0:nr, 512:1024],
                                                                   scalar=s_[0:nr, 2:3], in1=GF[1][0:nr, :],
                                                                   op0=ALU.mult, op1=ALU.mult),
                           reads=[rx_, rs_, rGF], writes=[rx_])
                dma("sp", yo[tok0:tok0 + nr, :], x_[0:nr, :], reads=[rx_], writes=[ryo])

        def proj_fm(W, rW, wc0, bank, ncols):
            for kc in range(8):
                op("pe", lambda h: h.matmul(out=B[bank][:, 0:ncols], lhsT=W[:, kc, wc0:wc0 + 128],
                                            rhs=xnT[:, kc, 0:ncols], start=(kc == 0), stop=(kc == 7)),
                   reads=[rW, rxnT], writes=[rB[bank]])

        def proj_tm(W, rW, wc0, wn, bank, ofs, nr):
            for kc in range(8):
                op("pe", lambda h: h.matmul(out=B[bank][0:nr, 0:wn], lhsT=xnT[:, kc, ofs:ofs + nr],
                                            rhs=W[:, kc, wc0:wc0 + wn], start=(kc == 0), stop=(kc == 7)),
                   reads=[rW, rxnT], writes=[rB[bank]])

        def proj_attn(L, G, W, rW, KT, rKT, V, rV, qscale, is_da, bk):
            n = G.ncols
            par = G.idx % 2
            proj_fm(W, rW, 0, bk, n)
            op("act", lambda h: h.activation(out=qT[:, par, 0:n], in_=B[bk][:, 0:n], func=AF.Copy, scale=qscale),
               reads=[rB[bk]], writes=[rqTp[par]])
            yield
            proj_fm(W, rW, 128, bk, n)
            op("act", lambda h: h.activation(out=kst[:, 0:n], in_=B[bk][:, 0:n], func=AF.Copy), reads=[rB[bk]], writes=[rkst])
            dma("sp", kTo[L][:, G.c0:G.c0 + n], kst[:, 0:n], reads=[rkst], writes=[rko])
            op("pool", lambda h: h.tensor_copy(out=KT[:, G.c0:G.c0 + n], in_=kst[:, 0:n]), reads=[rkst], writes=[rKT])
            yield
            for ti, (ofs, nr) in enumerate(G.tiles):
                proj_tm(W, rW, 256, 128 if is_da else 256, bk, ofs, nr)
                v_ = vst[ti % 2]; rv_ = rvst[ti % 2]
                op("act", lambda h: h.activation(out=v_[0:nr, :], in_=B[bk][0:nr, 0:128], func=AF.Copy),
                   reads=[rB[bk]], writes=[rv_])
                dma("sp", vo[L][G.c0 + ofs:G.c0 + ofs + nr, :], v_[0:nr, :], reads=[rv_], writes=[rvo])
                op("pool", lambda h: h.tensor_copy(out=V[0:nr, G.vt0 + ti, 0:128], in_=v_[0:nr, :]), reads=[rv_], writes=[rV])
                if not is_da:
                    sgv = sg[0:nr, ti, par * 128:(par + 1) * 128]
                    op("act", lambda h: h.activation(out=sgv, in_=B[bk][0:nr, 128:256], func=AF.Silu),
                       reads=[rB[bk]], writes=[rsgp[par]])
                yield
            if is_da:
                proj_fm(W, rW, 384, bk, n)
                op("act", lambda h: h.activation(out=sg[:, par, 0:n], in_=B[bk][:, 0:n], func=AF.Silu),
                   reads=[rB[bk]], writes=[rsgp[par]])
                op("pool", lambda h: h.tensor_scalar(out=sg[:, par, 0:n], in0=sg[:, par, 0:n], scalar1=lamt[:, 3:4],
                                                     scalar2=None, op0=ALU.mult), reads=[rsgp[par], rlam], writes=[rsgp[par]])
                yield

        def key_tiles(G, KT, V, for_sb):
            kts = []
            if G.kind == "sample":
                for jj in range(NCB):
                    k = KT_(); k.nk = 128; k.shared = False
                    k.kT = [KT[:, qi * PAST + jj * 128: qi * PAST + (jj + 1) * 128] for qi in range(4)]
                    k.v = [V[:, qi * NCB + jj, :] for qi in range(4)]
                    k.active = [0, 1, 2, 3]; k.diag = []
                    kts.append(k)
                k = KT_(); k.nk = 32; k.shared = False
                k.kT = [KT[:, NP + 32 * qi: NP + 32 * qi + 32] for qi in range(4)]
                k.v = [V[:, 4 * NCB + qi, :] for qi in range(4)]
                k.active = [0, 1, 2, 3]; k.diag = [0, 1, 2, 3] if for_sb else []
                kts.append(k)
                return kts
            k = KT_(); k.nk = 16; k.shared = True; k.kT = [KT[:, 0:16]] * 4; k.v = [V[:, 0, :]] * 4
            k.active = list(range(len(G.tiles)))
            k.diag = [0] if (G.kind == "meta" and for_sb) else []
            kts.append(k)
            if G.kind == "frame":
                i0 = G.vt0
                for j in range(1, i0 + 4):
                    k = KT_(); k.nk = 128; k.shared = True
                    c = 16 + 128 * (j - 1)
                    k.kT = [KT[:, c:c + 128]] * 4; k.v = [V[:, j, :]] * 4
                    k.active = [qi for qi in range(4) if i0 + qi >= j]
                    k.diag = [j - i0] if j >= i0 else []
                    kts.append(k)
            return kts

        def da_b(L, G, KT, rKT, V, rV, gen, nch):
            kts = key_tiles(G, KT, V, False)
            par = G.idx % 2
            n = G.ncols
            rq_ = rqTp[par]; rsg_ = rsgp[par]
            OT = [B[2], B[3]]; rOT = [rB[2], rB[3]]
            Pacc = [TF[2], TF[3]]; rPa = [rTF[2], rTF[3]]
            units = [(m, k) for k in kts for m in range(2)]
            SBK = [B[0], B[5], B[6]]; rSBK = [rB[0], rB[5], rB[6]]
            PTS = [TH[0], TH[1], TH[2], TH[4]]; rPTS = [rTH[0], rTH[1], rTH[2], rTH[4]]

            def stS(u):
                m, k = units[u]; p0 = 64 * m; nk = k.nk
                cA = G.tiles[k.active[0]][0]; cE = G.ncols
                sb_ = SBK[u % 3]; rs_ = rSBK[u % 3]
                if k.shared:
                    op("pe", lambda h: h.matmul(out=sb_[0:nk, cA:cE], lhsT=k.kT[0][p0:p0 + 64, :],
                                                rhs=qT[p0:p0 + 64, par, cA:cE], start=True, stop=True),
                       reads=[rKT, rq_], writes=[rs_])
                else:
                    for qi in k.active:
                        ofs, nr = G.tiles[qi]
                        op("pe", lambda h: h.matmul(out=sb_[0:nk, ofs:ofs + nr], lhsT=k.kT[qi][p0:p0 + 64, :],
                                                    rhs=qT[p0:p0 + 64, par, ofs:ofs + nr], start=True, stop=True),
                           reads=[rKT, rq_], writes=[rs_])

            def stE(u):
                m, k = units[u]; nk = k.nk
                cA = G.tiles[k.active[0]][0]; cE = G.ncols
                sb_ = SBK[u % 3]; rs_ = rSBK[u % 3]
                pt = PTS[u % 4]; rpt = rPTS[u % 4]
                op("act", lambda h: h.activation(out=pt[0:nk, cA:cE], in_=sb_[0:nk, cA:cE], func=AF.Exp),
                   reads=[rs_], writes=[rpt])
                for qi in k.diag:
                    ofs, nr = G.tiles[qi]
                    op("pool", lambda h: h.memset(pt[64:128, ofs:ofs + 64], 0.0), writes=[rpt])
                op("pool", lambda h: h.tensor_tensor(out=Pacc[m][0:nk, cA:cE], in0=Pacc[m][0:nk, cA:cE], in1=pt[0:nk, cA:cE],
                                                     op=ALU.add), reads=[rpt, rPa[m]], writes=[rPa[m]])

            def stP(u):
                m, k = units[u]; nk = k.nk
                cA = G.tiles[k.active[0]][0]; cE = G.ncols
                pt = PTS[u % 4]; rpt = rPTS[u % 4]
                if k.shared:
                    op("pe", lambda h: h.matmul(out=OT[m][:, cA:cE], lhsT=k.v[0][0:nk, 0:128], rhs=pt[0:nk, cA:cE],
                                                start=False, stop=True, skip_group_check=True),
                       reads=[rpt, rV], writes=[rOT[m]])
                else:
                    for qi in k.active:
                        ofs, nr = G.tiles[qi]
                        op("pe", lambda h: h.matmul(out=OT[m][:, ofs:ofs + nr], lhsT=k.v[qi][0:nk, 0:128], rhs=pt[0:nk, ofs:ofs + nr],
                                                    start=False, stop=True, skip_group_check=True),
                           reads=[rpt, rV], writes=[rOT[m]])

            nu = len(units)
            per = -(-nch // max(1, nu))
            for st in range(nu + 2):
                if st < nu:
                    stS(st)
                if 0 <= st - 1 < nu:
                    stE(st - 1)
                if 0 <= st - 2 < nu:
                    stP(st - 2)
                if gen is not None:
                    for _ in range(per):
                        next(gen, None)
            if gen is not None:
                for _ in gen:
                    pass
            osb = [TF[0], TF[1]]; rosb = [rTF[0], rTF[1]]
            rdn = [TF[4], TF6]; rrdn = [rTF[4], rTF6]
            for m in range(2):
                op("act", lambda h: h.activation(out=osb[m][:, 0:n], in_=OT[m][:, 0:n], func=AF.Copy), reads=[rOT[m]], writes=[rosb[m]])
            for m in range(2):
                op("dve", lambda h: h.memset(OT[m][:], 0.0), writes=[rOT[m]])
            for m in range(2):
                op("pe", lambda h: h.matmul(out=B[4][:, 0:n], lhsT=onesf[:, :], rhs=Pacc[m][:, 0:n], start=True, stop=True),
                   reads=[rPa[m], rcb], writes=[rB[4]])
                op("dve", lambda h: h.reciprocal(out=rdn[m][:, 0:n], in_=B[4][:, 0:n]), reads=[rB[4]], writes=[rrdn[m]])
                op("pool", lambda h: h.memset(Pacc[m][:], 0.0), writes=[rPa[m]])
            for m in range(2):
                op("dve", lambda h: h.tensor_tensor(out=osb[m][:, 0:n], in0=osb[m][:, 0:n], in1=rdn[m][:, 0:n], op=ALU.mult),
                   reads=[rosb[m], rrdn[m]], writes=[rosb[m]])
            op("dve", lambda h: h.scalar_tensor_tensor(out=osb[0][:, 0:n], in0=osb[1][:, 0:n], scalar=lamt[:, 1:2], in1=osb[0][:, 0:n],
                                                       op0=ALU.mult, op1=ALU.add), reads=[rosb[0], rosb[1], rlam], writes=[rosb[0]])
            op("pool", lambda h: h.tensor_tensor(out=rdn[0][:, 0:n], in0=osb[0][:, 0:n], in1=osb[0][:, 0:n], op=ALU.mult),
               reads=[rosb[0]], writes=[rrdn[0]])
            op("pe", lambda h: h.matmul(out=B[4][:, 0:n], lhsT=onesf[:, :], rhs=rdn[0][:, 0:n], start=True, stop=True),
               reads=[rrdn[0], rcb], writes=[rB[4]])
            op("act", lambda h: h.activation(out=rdn[1][:, 0:n], in_=B[4][:, 0:n], func=AF.Sqrt, scale=1.0 / 128, bias=epst[:, 0:1]),
               reads=[rB[4], rcb], writes=[rrdn[1]])
            op("dve", lambda h: h.reciprocal(out=rdn[1][:, 0:n], in_=rdn[1][:, 0:n]), reads=[rrdn[1]], writes=[rrdn[1]])
            op("dve", lambda h: h.tensor_tensor(out=osb[0][:, 0:n], in0=osb[0][:, 0:n], in1=rdn[1][:, 0:n], op=ALU.mult),
               reads=[rosb[0], rrdn[1]], writes=[rosb[0]])
            op("dve", lambda h: h.tensor_tensor(out=ogT[:, 0, 0:n], in0=osb[0][:, 0:n], in1=sg[:, par, 0:n], op=ALU.mult),
               reads=[rosb[0], rsg_], writes=[rogT])
            dma("sp", bounce[L].ap()[G.idx][0:128, 0:n], ogT[:, 0, 0:n], reads=[rogT], writes=[rbnc[L][G.idx]])
            group_coll(L, G)

        def sb_b(L, G, KT, rKT, V, rV, gen, nch):
            kts = key_tiles(G, KT, V, True)[::-1]
            par = G.idx % 2
            rq_ = rqTp[par]; rsg_ = rsgp[par]
            Racc = TF[5]; rR = rTF[5]
            op("dve", lambda h: h.memset(B[6][:], 0.0), writes=[rB[6]])
            op("pool", lambda h: h.memset(Racc[:], 0.0), writes=[rR])
            nu = len(kts)
            EB = [TF[0], TF[1], TF[2], TF[3]]; rEB = [rTF[0], rTF[1], rTF[2], rTF[3]]
            XB = [TF[4], TF6]; rXB = [rTF[4], rTF6]

            def rng(k):
                return G.tiles[k.active[0]][0], G.ncols

            def sA(u):
                k = kts[u]; nk = k.nk; cA, cE = rng(k)
                zb = B[0]; rz = rB[0]
                if k.shared:
                    op("pe", lambda h: h.matmul(out=zb[0:nk, cA:cE], lhsT=k.kT[0], rhs=qT[:, par, cA:cE], start=True, stop=True),
                       reads=[rKT, rq_], writes=[rz])
                else:
                    for qi in k.active:
                        ofs, nr = G.tiles[qi]
                        op("pe", lambda h: h.matmul(out=zb[0:nk, ofs:ofs + nr], lhsT=k.kT[qi], rhs=qT[:, par, ofs:ofs + nr],
                                                    start=True, stop=True), reads=[rKT, rq_], writes=[rz])

            def sB(u):
                k = kts[u]; nk = k.nk; cA, cE = rng(k)
                zb = B[0]; rz = rB[0]
                E_ = EB[u % 4]; rE = rEB[u % 4]
                Lp = TH[u % 2]; rL = rTH[u % 2]
                op("act", lambda h: h.activation(out=E_[0:nk, cA:cE], in_=zb[0:nk, cA:cE], func=AF.Exp), reads=[rz], writes=[rE])
                for qi in k.diag:
                    ofs, nr = G.tiles[qi]
                    op("pool", lambda h: h.tensor_tensor(out=E_[0:nk, ofs:ofs + nr], in0=E_[0:nk, ofs:ofs + nr],
                                                         in1=Mstr[0:nk, 0:nr], op=ALU.mult), reads=[rE, rcst], writes=[rE])
                op("act", lambda h: h.activation(out=Lp[0:nk, cA:cE], in_=E_[0:nk, cA:cE], func=AF.Ln, scale=1.0,
                                                 bias=onet[0:nk, 0:1]), reads=[rE, rcb], writes=[rL])

            def sC(u):
                k = kts[u]; nk = k.nk; cA, cE = rng(k)
                Lp = TH[u % 2]; rL = rTH[u % 2]
                tb = B[2 + u % 2]; rt = rB[2 + u % 2]
                cb = B[4 + u % 2]; rc = rB[4 + u % 2]
                op("pe", lambda h: h.matmul(out=tb[0:nk, cA:cE], lhsT=negUb[0:nk, 0:nk], rhs=Lp[0:nk, cA:cE], start=True, stop=True),
                   reads=[rL, rcb], writes=[rt])
                if u != nu - 1:
                    op("pe", lambda h: h.matmul(out=cb[:, cA:cE], lhsT=negOb[0:nk, 0:128], rhs=Lp[0:nk, cA:cE], start=True, stop=True),
                       reads=[rL, rcb], writes=[rc])

            def sD(u):
                k = kts[u]; nk = k.nk; cA, cE = rng(k)
                E_ = EB[u % 4]; rE = rEB[u % 4]
                X_ = XB[u % 2]; rX = rXB[u % 2]
                tb = B[2 + u % 2]; rt = rB[2 + u % 2]
                cb = B[4 + u % 2]; rc = rB[4 + u % 2]
                A_ = TH[2 + u % 2]; rA = rTH[2 + u % 2]
                op("dve", lambda h: h.tensor_tensor(out=X_[0:nk, cA:cE], in0=tb[0:nk, cA:cE], in1=Racc[0:nk, cA:cE], op=ALU.add),
                   reads=[rt, rR], writes=[rX])
                if u != nu - 1:
                    op("dve", lambda h: h.tensor_tensor(out=Racc[:, cA:cE], in0=cb[:, cA:cE], in1=Racc[:, cA:cE], op=ALU.add),
                       reads=[rc, rR], writes=[rR])
                op("act", lambda h: h.activation(out=X_[0:nk, cA:cE], in_=X_[0:nk, cA:cE], func=AF.Exp), reads=[rX], writes=[rX])
                op("pool", lambda h: h.tensor_tensor(out=A_[0:nk, cA:cE], in0=E_[0:nk, cA:cE], in1=X_[0:nk, cA:cE], op=ALU.mult),
                   reads=[rE, rX], writes=[rA])

            def sE(u):
                k = kts[u]; nk = k.nk
                A_ = TH[2 + u % 2]; rA = rTH[2 + u % 2]
                for qi in k.active:
                    ofs, nr = G.tiles[qi]
                    op("pe", lambda h: h.matmul(out=B[6][0:nr, qi * 128:(qi + 1) * 128], lhsT=A_[0:nk, ofs:ofs + nr],
                                                rhs=k.v[qi][0:nk, 0:128], start=False, stop=True, skip_group_check=True),
                       reads=[rA, rV], writes=[rB[6]])

            per = -(-nch // max(1, nu))
            for st in range(nu + 4):
                if 0 <= st - 1 < nu:
                    sB(st - 1)
                if 0 <= st - 2 < nu:
                    sC(st - 2)
                if 0 <= st - 3 < nu:
                    sD(st - 3)
                if 0 <= st - 4 < nu:
                    sE(st - 4)
                if st < nu:
                    sA(st)
                if gen is not None:
                    for _ in range(per):
                        next(gen, None)
            if gen is not None:
                for _ in gen:
                    pass
            og = TH[4]
            for qi, (ofs, nr) in enumerate(G.tiles):
                op("dve", lambda h: h.tensor_tensor(out=og[0:nr, 0:128], in0=B[6][0:nr, qi * 128:(qi + 1) * 128],
                                                    in1=sg[0:nr, qi, par * 128:(par + 1) * 128], op=ALU.mult), reads=[rB[6], rsg_], writes=[rTH[4]])
                op("pe", lambda h: h.transpose(out=TRb[:, 0, 0:nr], in_=og[0:nr, 0:128], identity=identb[0:nr, 0:nr]),
                   reads=[rTH[4], rcb], writes=[rTR])
                op("act", lambda h: h.activation(out=ogT[:, 0, ofs:ofs + nr], in_=TRb[:, 0, 0:nr], func=AF.Copy),
                   reads=[rTR], writes=[rogT])
            dma("sp", bounce[L].ap()[G.idx][0:128, 0:G.ncols], ogT[:, 0, 0:G.ncols], reads=[rogT], writes=[rbnc[L][G.idx]])
            group_coll(L, G)

        def load_cache(L, KT, rKT, V, rV):
            for c in range(0, 4 * PAST, 1024):
                w_ = wst[(c // 1024) % 2]; rw_ = rwst[(c // 1024) % 2]
                dma("sp", w_[:, 0:1024], ckin[L][:, c:c + 1024], writes=[rw_])
                op("pool", lambda h: h.tensor_copy(out=KT[:, c:c + 1024], in_=w_[:, 0:1024]), reads=[rw_], writes=[rKT])
            i = 0
            for s in range(4):
                for j0 in range(0, NCB, 8):
                    nj = min(8, NCB - j0)
                    w_ = wst[i % 2]; rw_ = rwst[i % 2]; i += 1
                    dma("sp", w_[:, 0:nj * 128].rearrange("p (j d) -> p j d", d=128),
                        cvin[L][s, j0 * 128:(j0 + nj) * 128, :].rearrange("(j p) d -> p j d", p=128), writes=[rw_])
                    op("pool", lambda h: h.tensor_copy(out=V[:, s * NCB + j0:s * NCB + j0 + nj, 0:128],
                                                       in_=w_[:, 0:nj * 128].rearrange("p (j d) -> p j d", d=128)),
                       reads=[rw_], writes=[rV])

        if STOP == 1:
            kb.barrier()
            return nc
        CKN[0] = 0
        try:
          for L in range(4):
              kind = L % 3
              with contextlib.ExitStack() as ls:
                  kb.barrier()
                  W, rW = load_weights(L, ls)
                  if STOP == 2:
                      kb.barrier()
                      return nc
                  Wo = rWo = None
                  if L >= 1:
                      Wo, rWo = load_wout(L - 1, ls)
                      if L == 2:
                          ck('wout L2')
                  if kind in (0, 2):
                      KT = kb.sb("KT%d" % L, [128, NT], BF16, ls); rKT = R()
                      V = kb.sb("V%d" % L, [128, NVT, 130], BF16, ls); rV = R()
                      op("pool", lambda h: h.memset(V[:], 1.0), writes=[rV])
                      if L == 2:
                          ck('vmemset L2')
                      if kind == 0:
                          compute_lam(L)
                          dma("sp", lamt[:, 2:3], subg[L].rearrange("o d -> d o"), writes=[rlam])
                          op("dve", lambda h: h.tensor_scalar(out=lamt[:, 3:4], in0=lamt[:, 2:3], scalar1=1.0 - LAM_INIT[L],
                                                              scalar2=None, op0=ALU.mult), reads=[rlam], writes=[rlam])
                          for b_ in (2, 3):
                              op("dve", lambda h: h.memset(B[b_][:], 0.0), writes=[rB[b_]])
                          for i_ in (2, 3):
                              op("pool", lambda h: h.memset(TF[i_][:], 0.0), writes=[rTF[i_]])
                      def prep(G_):
                          if G_.kind == "sample":
                              load_cache(L, KT, rKT, V, rV)
                          yield from phase_a(L, G_, Wo, rWo, ob=(1, 1))
                          yield from proj_attn(L, G_, W, rW, KT, rKT, V, rV, (64 if kind == 0 else 128) ** -0.5, kind == 0, 1)

                      for _ in prep(groups[0]):
                          pass
                      for gi, G in enumerate(groups):
                          nxt = groups[gi + 1] if gi + 1 < len(groups) else None
                          gen = prep(nxt) if (nxt is not None and nxt.kind != "sample") else None
                          nch = 5 * len(nxt.tiles) + 2 if gen is not None else 0
                          if kind == 0:
                              da_b(L, G, KT, rKT, V, rV, gen, nch)
                          else:
                              sb_b(L, G, KT, rKT, V, rV, gen, nch)
                          ck('phaseB L%d G%d' % (L, G.idx))
                          if nxt is not None and nxt.kind == "sample":
                              for _ in prep(nxt):
                                  pass
                  else:
                      S = kb.sb("S", [128, 2, 512], F32, ls); rS = R()
                      Sb = kb.sb("Sb", [128, 2, 512], BF16, ls); rSb = R()
                      qT2 = kb.sb("qT2", [128, 2, 512], BF16, ls)
                      qTr = [qT, qT2]; rqTr = [R(), R()]
                      kTr = [kb.sb("kTr%d" % i, [128, 2, 512], BF16, ls) for i in range(2)]; rkTr = [R(), R()]
                      vr = [kb.sb("vr%d" % i, [128, 4, 512], BF16, ls) for i in range(2)]; rvr = [R(), R()]
                      sg2 = kb.sb("sg2", [128, 4, 512], F32, ls)
                      sgr = [sg, sg2]; rsgr = [R(), R()]
                      cs = kb.sb("cs", [128, 512], F32, ls); sn = kb.sb("sn", [128, 512], F32, ls); rcs = R(); rsn = R()
                      qd = kb.sb("qd", [128, 2, 128], BF16, ls); rqd = R()
                      kd = kb.sb("kd", [128, 2, 128], BF16, ls); rkd = R()
                      sTm = kb.sb("sTm", [128, 128], BF16, ls); rsT = R()
                      og4 = kb.sb("og4", [128, 512], BF16, ls); rog4 = R()
                      bcast_row(Gt, gn1, 512, 1.0)
                      op("dve", lambda h: h.memset(S[:], 0.0), writes=[rS])
                      op("dve", lambda h: h.memset(Sb[:], 0.0), writes=[rSb])

                      def prep_ret(G_):
                          n = G_.ncols; par = G_.idx % 2
                          yield from phase_a(L, G_, Wo, rWo, ob=(4, 5))
                          dma("sp", cs[:, 0:n], cosin[:, G_.c0:G_.c0 + n], writes=[rcs])
                          dma("sp", sn[:, 0:n], sinin[:, G_.c0:G_.c0 + n], writes=[rsn])
                          for which, dst, rdst, sc in ((0, qTr[par], rqTr[par], 1.0), (1, kTr[par], rkTr[par], 1.0 / 16)):
                              b0 = 4; b1 = 5
                              proj_fm(W, rW, 256 * which, b0, n)
                              yield
                              proj_fm(W, rW, 256 * which + 128, b1, n)
                              yield
                              t0, t1, t2, t3 = TF[0], TF[1], TF[2], TF[3]
                              op("dve", lambda h: h.scalar_tensor_tensor(out=t0[:, 0:n], in0=B[b0][:, 0:n], scalar=sc, in1=cs[:, 0:n],
                                                                         op0=ALU.mult, op1=ALU.mult), reads=[rB[b0], rcs], writes=[rTF[0]])
                              op("dve", lambda h: h.scalar_tensor_tensor(out=t1[:, 0:n], in0=B[b1][:, 0:n], scalar=sc, in1=sn[:, 0:n],
                                                                         op0=ALU.mult, op1=ALU.mult), reads=[rB[b1], rsn], writes=[rTF[1]])
                              op("pool", lambda h: h.tensor_tensor(out=dst[:, 0, 0:n], in0=t0[:, 0:n], in1=t1[:, 0:n], op=ALU.subtract),
                                 reads=[rTF[0], rTF[1]], writes=[rdst])
                              op("dve", lambda h: h.scalar_tensor_tensor(out=t2[:, 0:n], in0=B[b0][:, 0:n], scalar=sc, in1=sn[:, 0:n],
                                                                         op0=ALU.mult, op1=ALU.mult), reads=[rB[b0], rsn], writes=[rTF[2]])
                              op("dve", lambda h: h.scalar_tensor_tensor(out=t3[:, 0:n], in0=B[b1][:, 0:n], scalar=sc, in1=cs[:, 0:n],
                                                                         op0=ALU.mult, op1=ALU.mult), reads=[rB[b1], rcs], writes=[rTF[3]])
                              op("pool", lambda h: h.tensor_tensor(out=dst[:, 1, 0:n], in0=t2[:, 0:n], in1=t3[:, 0:n], op=ALU.add),
                                 reads=[rTF[2], rTF[3]], writes=[rdst])
                              yield
                          for ti, (ofs, nr) in enumerate(G_.tiles):
                              proj_tm(W, rW, 512, 512, 6, ofs, nr)
                              op("act", lambda h: h.activation(out=vr[par][0:nr, ti, :], in_=B[6][0:nr, :], func=AF.Copy),
                                 reads=[rB[6]], writes=[rvr[par]])
                              yield
                              bk = 4 + ti % 2
                              proj_tm(W, rW, 1024, 512, bk, ofs, nr)
                              op("act", lambda h: h.activation(out=sgr[par][0:nr, ti, :], in_=B[bk][0:nr, :], func=AF.Silu),
                                 reads=[rB[bk]], writes=[rsgr[par]])
                              op("pool", lambda h: h.tensor_tensor(out=sgr[par][0:nr, ti, :], in0=sgr[par][0:nr, ti, :], in1=Gt[0:nr, :], op=ALU.mult),
                                 reads=[rsgr[par], rGt], writes=[rsgr[par]])
                              yield

                      def pump(gen, k):
                          if gen is not None:
                              for _ in range(k):
                                  next(gen, None)

                      for _ in prep_ret(groups[0]):
                          pass
                      for gi, G in enumerate(groups):
                          n = G.ncols; par = G.idx % 2
                          nxt = groups[gi + 1] if gi + 1 < len(groups) else None
                          gen = prep_ret(nxt) if nxt is not None else None
                          nch = (4 * len(nxt.tiles) + 6 + 2 * len(nxt.tiles)) if nxt is not None else 0
                          pk = -(-nch // (4 * len(G.tiles)))
                          qT_ = qTr[par]; rq_ = rqTr[par]; kT_ = kTr[par]; rk_ = rkTr[par]
                          vr_ = vr[par]; rv_ = rvr[par]; sg_ = sgr[par]; rsg_ = rsgr[par]
                          for ti, (ofs, nr) in enumerate(G.tiles):
                              cn = ncol(nr)
                              if G.kind == "sample":
                                  dma("sp", S[:], st1[ti].rearrange("(c p) e -> p c e", p=128), writes=[rS])
                                  op("act", lambda h: h.activation(out=Sb[:], in_=S[:], func=AF.Copy), reads=[rS], writes=[rSb])
                              for c in range(2):
                                  op("pe", lambda h: h.matmul(out=B[0][0:nr, 0:nr], lhsT=kT_[:, c, ofs:ofs + nr], rhs=qT_[:, c, ofs:ofs + nr],
                                                              start=(c == 0), stop=(c == 1)), reads=[rk_, rq_], writes=[rB[0]])
                              op("dve", lambda h: h.tensor_tensor(out=sTm[0:nr, 0:nr], in0=B[0][0:nr, 0:nr], in1=decT[0:nr, 0:nr], op=ALU.mult),
                                 reads=[rB[0], rcst], writes=[rsT])
                              for c in range(2):
                                  op("pool", lambda h: h.tensor_tensor(out=qd[:, c, 0:nr], in0=qT_[:, c, ofs:ofs + nr], in1=dq[:, 0:nr], op=ALU.mult),
                                     reads=[rq_, rcst], writes=[rqd])
                                  op("pe", lambda h: h.transpose(out=TRb[0:nr, c, :], in_=kT_[:, c, ofs:ofs + nr], identity=identb[:, :]),
                                     reads=[rk_, rcb], writes=[rTR])
                              op("dve", lambda h: h.tensor_scalar(out=kd[0:nr, :, :], in0=TRb[0:nr, 0:2, :], scalar1=cst[0:nr, 768 + cn:769 + cn],
                                                                  scalar2=None, op0=ALU.mult), reads=[rTR, rcst], writes=[rkd])
                              pump(gen, pk)
                              op("pe", lambda h: h.matmul(out=B[1][0:nr, :], lhsT=sTm[0:nr, 0:nr], rhs=vr_[0:nr, ti, :], start=True, stop=False),
                                 reads=[rsT, rv_], writes=[rB[1]])
                              for c in range(2):
                                  op("pe", lambda h: h.matmul(out=B[1][0:nr, :], lhsT=qd[:, c, 0:nr], rhs=Sb[:, c, :], start=False, stop=(c == 1)),
                                     reads=[rqd, rSb], writes=[rB[1]])
                              for c in range(2):
                                  op("pe", lambda h: h.matmul(out=B[2 + c][:, :], lhsT=kd[0:nr, c, :], rhs=vr_[0:nr, ti, :], start=True, stop=True),
                                     reads=[rkd, rv_], writes=[rB[2 + c]])
                                  op("dve", lambda h: h.scalar_tensor_tensor(out=S[:, c, :], in0=S[:, c, :], scalar=cst[:, 771 + cn:772 + cn],
                                                                             in1=B[2 + c][:, :], op0=ALU.mult, op1=ALU.add),
                                     reads=[rS, rcst, rB[2 + c]], writes=[rS])
                              op("act", lambda h: h.activation(out=Sb[:], in_=S[:], func=AF.Copy), reads=[rS], writes=[rSb])
                              pump(gen, pk)
                              if G.kind == "sample":
                                  dma("sp", rets[ti].rearrange("(c p) e -> p c e", p=128), S[:], reads=[rS], writes=[ryo])
                              op("act", lambda h: h.activation(out=sqj[0:nr, 0:512], in_=B[1][0:nr, :], func=AF.Square, accum_out=rd[0:nr, 0:1]),
                                 reads=[rB[1]], writes=[rsqj, rrd])
                              op("act", lambda h: h.activation(out=rd[0:nr, 1:2], in_=rd[0:nr, 0:1], func=AF.Sqrt, scale=1.0 / 512,
                                                               bias=epst[0:nr, 0:1]), reads=[rrd, rcb], writes=[rrd])
                              op("dve", lambda h: h.reciprocal(out=rd[0:nr, 2:3], in_=rd[0:nr, 1:2]), reads=[rrd], writes=[rrd])
                              op("dve", lambda h: h.scalar_tensor_tensor(out=og4[0:nr, :], in0=B[1][0:nr, :], scalar=rd[0:nr, 2:3],
                                                                         in1=sg_[0:nr, ti, :], op0=ALU.mult, op1=ALU.mult),
                                 reads=[rB[1], rrd, rsg_], writes=[rog4])
                              pump(gen, pk)
                              for e in range(4):
                                  op("pe", lambda h: h.transpose(out=TRb[:, 4 + e, 0:nr], in_=og4[0:nr, e * 128:(e + 1) * 128],
                                                                 identity=identb[0:nr, 0:nr]), reads=[rog4, rcb], writes=[rTR])
                              op("act", lambda h: h.activation(out=ogT[:, 0:4, ofs:ofs + nr], in_=TRb[:, 4:8, 0:nr], func=AF.Copy),
                                 reads=[rTR], writes=[rogT])
                              pump(gen, pk)
                          if G.idx == NGF:
                              dma("sp", retp.rearrange("(c p) e -> p c e", p=128), S[:], reads=[rS], writes=[ryo])
                          dma("sp", bounce[L].ap()[G.idx][:, 0:n].rearrange("(c p) n -> p c n", p=128), ogT[:, 0:4, 0:n],
                              reads=[rogT], writes=[rbnc[L][G.idx]])
                          group_coll(L, G)
                          if gen is not None:
                              for _ in gen:
                                  pass
                  ck('endlayer L%d' % L)
                  kb.barrier()

        except StopBuild:
            kb.barrier()
            return nc
        with contextlib.ExitStack() as ls:
            kb.barrier()
            Wo, rWo = load_wout(3, ls)
            GF = [kb.sb("GF%d" % i, [128, 512], F32, ls) for i in range(2)]; rGF = rGt
            tcnt = [0]
            xtf = [kb.sb("xtf%d" % i, [128, 1024], F32, ls) for i in range(4)]; rxtf = [R() for _ in range(4)]
            ssf = [kb.sb("ssf%d" % i, [128, 4], F32, ls) for i in range(4)]; rssf = [R() for _ in range(4)]
            ogaf = [kb.sb("ogaf%d" % i, [128, 4, 128], BF16, ls) for i in range(4)]; rogaf = [R() for _ in range(4)]
            bcast_row(GF[0], ngf[:, 0:512], 512)
            bcast_row(GF[1], ngf[:, 512:1024], 512)
            final_pass(Wo, rWo)
            kb.barrier()
    return nc


_CACHE = {}


def host_consts(h, SEQ, PAST):
    NP = 16 + SEQ
    NT = NP + 128
    cst = np.zeros((128, 776), np.float32)
    idx = np.arange(128)
    cst[:, 0:128] = np.eye(128, dtype=np.float32)
    cst[:, 128:256] = -(idx[:, None] >= idx[None, :]).astype(np.float32)
    cst[:, 256:384] = -1.0
    cst[:, 384:512] = (idx[:, None] < idx[None, :]).astype(np.float32)
    lg = np.log(np.float32(1.0) - np.float32(2.0) ** np.float32(-5.0 - h)).astype(np.float32)
    rel = (idx[None, :] - idx[:, None]).astype(np.float32)
    cst[:, 512:640] = np.where(rel >= 0, np.exp(np.maximum(rel, 0.0) * lg), 0.0).astype(np.float32)
    cst[:, 640:768] = np.exp((idx.astype(np.float32) + 1.0) * lg)[None, :]
    for ci, n in enumerate((128, 16, 32)):
        col = np.where(idx < n, np.exp((n - 1.0 - idx.astype(np.float32)) * lg), 0.0)
        cst[:, 768 + ci] = col
        cst[:, 771 + ci] = np.exp(np.float32(n) * lg)
    pos = np.concatenate([np.arange(NP, dtype=np.float32) - 16.0,
                          np.float32(PAST) + (np.arange(128) % 32).astype(np.float32)]).astype(np.float32)
    inv = (np.float32(10000.0) ** (-np.linspace(0.0, 1.0, 128, dtype=np.float32))).astype(np.float32)
    ang = (inv[:, None] * pos[None, :]).astype(np.float32)
    return cst, np.cos(ang).astype(np.float32), np.sin(ang).astype(np.float32)


def make_in_maps(inp, SEQ, PAST):
    f = lambda a: np.ascontiguousarray(np.asarray(a, dtype=np.float32))
    w_in = [inp["w_in_0"], inp["w_in_1"], inp["w_in_2"], inp["w_in_3"]]
    w_out = [inp["w_out_0"], inp["w_out_1"], inp["w_out_2"], inp["w_out_3"]]
    norm_g = [inp["norm_g_0"], inp["norm_g_1"], inp["norm_g_2"], inp["norm_g_3"]]
    lam = {0: (inp["lam_q1_0"], inp["lam_k1_0"], inp["lam_q2_0"], inp["lam_k2_0"]),
           3: (inp["lam_q1_3"], inp["lam_k1_3"], inp["lam_q2_3"], inp["lam_k2_3"])}
    subln = {0: inp["subln_g_0"], 3: inp["subln_g_3"]}
    cache_k = {0: inp["cache_k_l0"], 2: inp["cache_k_l2"], 3: inp["cache_k_l3"]}
    cache_v = {0: inp["cache_v_l0"], 2: inp["cache_v_l2"], 3: inp["cache_v_l3"]}
    in_maps = []
    for c in range(8):
        b, h = c // 4, c % 4
        m = {}
        m["xin"] = f(np.concatenate([inp["meta_tokens"], inp["x_prompt"][b],
                                     np.asarray(inp["x_sample"][4 * b:4 * b + 4]).reshape(128, 1024)], 0))
        for L in range(4):
            w = np.asarray(w_in[L])
            if L == 1:
                cols = [w[:, h * 256:(h + 1) * 256], w[:, 1024 + h * 256:1024 + (h + 1) * 256],
                        w[:, 2048 + h * 512:2048 + (h + 1) * 512], w[:, 4096 + h * 512:4096 + (h + 1) * 512]]
            else:
                cols = [w[:, i * 512 + h * 128:i * 512 + (h + 1) * 128] for i in range(4)]
            m["win%d" % L] = f(np.concatenate(cols, 1))
            m["ng%d" % L] = f(np.asarray(norm_g[L]).reshape(8, 128).T)
            m["wout%d" % L] = f(w_out[L])
        for L in (0, 3):
            m["lam%d" % L] = f(np.concatenate([np.asarray(a) for a in lam[L]]).reshape(1, 256))
            m["subg%d" % L] = f(np.asarray(subln[L]).reshape(1, 128))
            ck = np.asarray(cache_k[L])[4 * b:4 * b + 4, :, 2 * h:2 * h + 2, :]
            m["ck%d" % L] = f(ck.transpose(2, 3, 0, 1).reshape(128, 4 * PAST))
            m["cv%d" % L] = f(np.asarray(cache_v[L])[4 * b:4 * b + 4, :, h, :])
        ck = np.asarray(cache_k[2])[4 * b:4 * b + 4, :, h, :]
        m["ck2"] = f(ck.transpose(2, 0, 1).reshape(128, 4 * PAST))
        m["cv2"] = f(np.asarray(cache_v[2])[4 * b:4 * b + 4, :, h, :])
        m["gn1"] = f(np.asarray(inp["gn_g_1"])[h].reshape(1, 512))
        m["ngf"] = f(np.asarray(inp["norm_g_final"]).reshape(1, 1024))
        m["st1"] = f(np.asarray(inp["state_ret_l1"])[4 * b:4 * b + 4, h])
        cst, cs, sn = host_consts(h, SEQ, PAST)
        m["cst"] = cst; m["cos"] = cs; m["sin"] = sn
        in_maps.append(m)
    return in_maps


def assemble(res, SEQ, PAST):
    NP = 16 + SEQ
    y_p = np.zeros((2, SEQ, 1024), np.float32)
    y_s = np.zeros((8, 32, 1024), np.float32)
    kv = {}
    for L in (0, 2, 3):
        nm, dh = (8, 64) if L != 2 else (4, 128)
        kv[L] = [np.zeros((2, NP, nm, dh), np.float32), np.zeros((2, NP, 4, 128), np.float32),
                 np.zeros((8, 32, nm, dh), np.float32), np.zeros((8, 32, 4, 128), np.float32)]
    ret_p = np.zeros((2, 4, 256, 512), np.float32)
    ret_s = np.zeros((8, 4, 256, 512), np.float32)
    for c in range(8):
        b, h = c // 4, c % 4
        r = res[c]
        if h == 0:
            y = np.asarray(r["y"])
            y_p[b] = y[16:NP]
            y_s[4 * b:4 * b + 4] = y[NP:].reshape(4, 32, 1024)
        for L in (0, 2, 3):
            kT = np.asarray(r["kT%d" % L]); v = np.asarray(r["v%d" % L])
            if L != 2:
                kk = kT.reshape(2, 64, -1).transpose(2, 0, 1)
                kv[L][0][b, :, 2 * h:2 * h + 2, :] = kk[:NP]
                kv[L][2][4 * b:4 * b + 4, :, 2 * h:2 * h + 2, :] = kk[NP:].reshape(4, 32, 2, 64)
            else:
                kk = kT.T
                kv[L][0][b, :, h, :] = kk[:NP]
                kv[L][2][4 * b:4 * b + 4, :, h, :] = kk[NP:].reshape(4, 32, 128)
            kv[L][1][b, :, h, :] = v[:NP]
            kv[L][3][4 * b:4 * b + 4, :, h, :] = v[NP:].reshape(4, 32, 128)
        ret_p[b, h] = np.asarray(r["retp"])
        ret_s[4 * b:4 * b + 4, h] = np.asarray(r["rets"])
    return (y_p, y_s, kv[0][0], kv[0][1], kv[0][2], kv[0][3], ret_p, ret_s,
            kv[2][0], kv[2][1], kv[2][2], kv[2][3], kv[3][0], kv[3][1], kv[3][2], kv[3][3])


def kernel(**inputs):
    seq = int(np.asarray(inputs["x_prompt"]).shape[1])
    past = int(np.asarray(inputs["cache_k_l0"]).shape[1])
    key = (seq, past)
    if key not in _CACHE:
        _CACHE[key] = build(seq, past)
    nc = _CACHE[key]
    in_maps = make_in_maps(inputs, seq, past)
    res = run_bass_kernel_spmd(nc, in_maps, core_ids=list(range(8)))
    return assemble(res.results, seq, past)
```

```python
import contextlib
import math
import numpy as np
import concourse.bass as bass
import concourse.mybir as mybir
from concourse.bass_utils import run_bass_kernel_spmd

F32 = mybir.dt.float32
BF16 = mybir.dt.bfloat16
AF = mybir.ActivationFunctionType
ALU = mybir.AluOpType

SEQ = 16384
PAST = 4096
EPS = 1e-6
STOP = 0


class R:
    __slots__ = ("w", "rd")

    def __init__(self):
        self.w = None
        self.rd = {}


class KB:
    EPOCH = 30000
    NDMA = 40

    def __init__(self, nc):
        self.nc = nc
        self.es = contextlib.ExitStack()
        self.h = {"pe": nc.tensor, "act": nc.scalar, "dve": nc.vector, "pool": nc.gpsimd, "sp": nc.sync}
        self.sem = {}
        self.cnt = {}
        self.seen = {e: {} for e in self.h}
        self.nsem = 0
        for e in self.h:
            self._new_epoch(e)
        self.dsem = [self._mksem("d%d" % i) for i in range(self.NDMA)]
        self.duse = [0] * self.NDMA
        self.dnext = 0
        self.csems = []
        self.ninst = 0
        self.stopped = False
        KBREF[0] = self

    def _mksem(self, name):
        self.nsem += 1
        return self.es.enter_context(self.nc.semaphore("s%s_%d" % (name, self.nsem)))

    def _new_epoch(self, e):
        self.sem[e] = self._mksem(e)
        self.cnt[e] = 0

    def sb(self, name, shape, dt, es=None):
        return (es or self.es).enter_context(self.nc.sbuf_tensor("s_" + name, list(shape), dt))

    def ps(self, name, shape, dt):
        return self.es.enter_context(self.nc.psum_tensor("p_" + name, list(shape), dt))

    def _deps(self, eng, reads, writes, is_dma):
        need = {}

        def add(t, war=False):
            if t is None:
                return
            s, v, te = t
            if (not is_dma) and te == eng and (war or eng == "pe"):
                return
            k = id(s)
            if k not in need or need[k][1] < v:
                need[k] = (s, v)

        for r in reads:
            add(r.w)
        for r in writes:
            add(r.w)
            for t in r.rd.values():
                add(t, True)
        seen = self.seen[eng]
        for k, (s, v) in need.items():
            if seen.get(k, 0) < v:
                self.h[eng].wait_ge(s, v)
                seen[k] = v

    def op(self, eng, fn, reads=(), writes=()):
        if self.stopped:
            return None
        self._deps(eng, reads, writes, False)
        if self.cnt[eng] >= self.EPOCH:
            self._new_epoch(eng)
        inst = fn(self.h[eng])
        self.cnt[eng] += 1
        self.ninst += 1
        inst.then_inc(self.sem[eng], 1)
        tok = (self.sem[eng], self.cnt[eng], eng)
        for r in writes:
            r.w = tok
            r.rd = {}
        for r in reads:
            r.rd[eng] = tok
        return inst

    def _dtok(self, q):
        i = self.dnext
        self.dnext = (self.dnext + 1) % self.NDMA
        s = self.dsem[i]
        pv = 16 * self.duse[i]
        if pv and self.seen[q].get(id(s), 0) < pv:
            self.h[q].wait_ge(s, pv)
            self.seen[q][id(s)] = pv
        self.duse[i] += 1
        return i, s, 16 * self.duse[i]

    def dma(self, q, out, in_, reads=(), writes=()):
        if self.stopped:
            return None
        self._deps(q, reads, writes, True)
        i, s, v = self._dtok(q)
        inst = self.h[q].dma_start(out=out, in_=in_)
        inst.then_inc(s, 16)
        self.ninst += 1
        tok = (s, v, None)
        for r in writes:
            r.w = tok
            r.rd = {}
        for r in reads:
            r.rd[("d", i)] = tok
        return inst

    def coll(self, kind, ins, outs, groups, reads=(), writes=()):
        if self.stopped:
            return None
        q = "pool"
        self._deps(q, reads, writes, True)
        if not self.csems:
            self.csems.append(self._mksem("cc"))
            self.ccnt = 0
        s = self.csems[0]
        inst = self.nc.gpsimd.collective_compute(kind, ALU.bypass, replica_groups=groups, ins=ins, outs=outs)
        inst.then_inc(s, 1)
        self.ccnt += 1
        tok = (s, self.ccnt, None)
        for r in writes:
            r.w = tok
            r.rd = {}
        for r in reads:
            r.rd[("c", 0)] = tok

    def barrier(self):
        if self.stopped:
            return
        toks = [(self.sem[e], self.cnt[e]) for e in ("pe", "act", "dve", "pool") if self.cnt[e]]
        toks += [(s, 16 * self.duse[i]) for i, s in enumerate(self.dsem) if self.duse[i]]
        toks += [(s, self.ccnt) for s in self.csems]
        for e in self.h:
            for s, v in toks:
                if s is self.sem[e] and e != "sp":
                    continue
                if self.seen[e].get(id(s), 0) < v:
                    self.h[e].wait_ge(s, v)
                    self.seen[e][id(s)] = v


class StopBuild(Exception):
    pass


CKN = [0]
KBREF = [None]


def ck(tag):
    CKN[0] += 1
    if STOP >= 10 and CKN[0] == STOP - 9:
        print('STOP at', tag)
        KBREF[0].barrier()
        KBREF[0].stopped = True


class Grp:
    pass


class KT_:
    pass


def build(SEQ, PAST):
    NF = SEQ // 128
    NGF = NF // 4
    NP = 16 + SEQ
    NT = NP + 128
    NCB = PAST // 128
    NVT = max(NF + 1, 4 * NCB + 4)
    assert 4 * PAST <= NP
    NCH = [4, 16, 4, 4]
    WIN_COLS = [512, 1536, 512, 512]
    LAM_INIT = {0: 0.8 - 0.6 * math.exp(0.0), 3: 0.8 - 0.6 * math.exp(-0.9)}

    nc = bass.Bass("TRN2", target_bir_lowering=False, dynamic_dma_scratch_size=4096)

    def din(name, shape, dt=F32):
        return nc.dram_tensor(name, list(shape), dt, kind="ExternalInput").ap()

    def dout(name, shape, dt=F32):
        return nc.dram_tensor(name, list(shape), dt, kind="ExternalOutput").ap()

    xin = din("xin", [NT, 1024])
    win = [din("win%d" % L, [1024, WIN_COLS[L]]) for L in range(4)]
    ngin = [din("ng%d" % L, [128, 8]) for L in range(4)]
    wout = [din("wout%d" % L, [NCH[L] * 128, 1024]) for L in range(4)]
    lamin = {L: din("lam%d" % L, [1, 256]) for L in (0, 3)}
    subg = {L: din("subg%d" % L, [1, 128]) for L in (0, 3)}
    gn1 = din("gn1", [1, 512])
    ngf = din("ngf", [1, 1024])
    ckin = {L: din("ck%d" % L, [128, 4 * PAST]) for L in (0, 2, 3)}
    cvin = {L: din("cv%d" % L, [4, PAST, 128]) for L in (0, 2, 3)}
    st1 = din("st1", [4, 256, 512])
    cstin = din("cst", [128, 776])
    cosin = din("cos", [128, NT])
    sinin = din("sin", [128, NT])

    yo = dout("y", [NT, 1024])
    kTo = {L: dout("kT%d" % L, [128, NT]) for L in (0, 2, 3)}
    vo = {L: dout("v%d" % L, [NT, 128]) for L in (0, 2, 3)}
    retp = dout("retp", [256, 512])
    rets = dout("rets", [4, 256, 512])

    xs = nc.dram_tensor("xs", [NT, 1024], F32).ap()
    NGRP = NGF + 2
    bounce = [nc.dram_tensor("bnc%d" % L, [NGRP, NCH[L] // 4 * 128, 512], BF16) for L in range(4)]
    gath = [nc.dram_tensor("gth%d" % L, [NGRP, NCH[L] * 128, 512], BF16) for L in range(4)]

    kb = KB(nc)
    op = kb.op
    dma = kb.dma

    groups = []
    g = Grp(); g.idx = 0; g.c0 = 0; g.tiles = [(0, 16)]; g.kind = "meta"; g.vt0 = 0; groups.append(g)
    for i in range(NGF):
        g = Grp(); g.idx = 1 + i; g.c0 = 16 + 512 * i; g.tiles = [(128 * t, 128) for t in range(4)]
        g.kind = "frame"; g.vt0 = 1 + 4 * i; groups.append(g)
    g = Grp(); g.idx = 1 + NGF; g.c0 = NP; g.tiles = [(32 * t, 32) for t in range(4)]; g.kind = "sample"
    g.vt0 = 4 * NCB; groups.append(g)
    for g in groups:
        g.ncols = sum(n for _, n in g.tiles)

    with kb.es:
        cst = kb.sb("cst", [128, 776], F32); rcst = R()
        identb = kb.sb("identb", [128, 128], BF16)
        negUb = kb.sb("negUb", [128, 128], BF16)
        negOb = kb.sb("negOb", [128, 128], BF16)
        onesf = kb.sb("onesf", [128, 128], F32)
        rcb = R()
        epst = kb.sb("epst", [128, 1], F32)
        onet = kb.sb("onet", [128, 1], F32)
        xt = [kb.sb("xt%d" % i, [128, 1024], F32) for i in range(2)]; rxt = [R(), R()]
        sqj = kb.sb("sqj", [128, 1024], BF16); rsqj = R()
        xh = kb.sb("xh", [128, 1024], BF16); rxh = R()
        ss = [kb.sb("ss%d" % i, [128, 4], F32) for i in range(2)]; rss = [R(), R()]
        xnT = kb.sb("xnT", [128, 8, 512], BF16); rxnT = R()
        qT = kb.sb("qT", [128, 2, 512], BF16); rqT = R(); rqTp = [R(), R()]; rsgp = [R(), R()]
        kst = kb.sb("kst", [128, 512], F32); rkst = R()
        vst = [kb.sb("vst%d" % i, [128, 128], F32) for i in range(2)]; rvst = [R(), R()]
        ogT = kb.sb("ogT", [128, 4, 512], BF16); rogT = R()
        oga = [kb.sb("oga%d" % i, [128, 16, 128], BF16) for i in range(2)]; roga = [R(), R()]
        TF = [kb.sb("tf%d" % i, [128, 512], F32) for i in range(6)]; rTF = [R() for _ in range(6)]
        TF6 = kb.sb("tf6", [128, 512], F32); rTF6 = R()
        TH = [kb.sb("th%d" % i, [128, 512], BF16) for i in range(5)]; rTH = [R() for _ in range(5)]
        wst = [kb.sb("wst%d" % i, [128, 1024], F32) for i in range(2)]; rwst = [R(), R()]
        sg = kb.sb("sg", [128, 4, 512], F32); rsg = R()
        rd = kb.sb("rd", [128, 8], F32); rrd = R()
        lamt = kb.sb("lamt", [128, 8], F32); rlam = R()
        lrow = kb.sb("lrow", [1, 264], F32); rlrow = R()
        Gt = kb.sb("Gt", [128, 512], F32); rGt = R()
        grow = wst[0]; rgrow = rwst[0]
        B = [kb.ps("pb%d" % i, [128, 512], F32) for i in range(7)]; rB = [R() for _ in range(7)]
        TRb = kb.ps("trb", [128, 8, 128], BF16); rTR = R()
        rxs = [R() for _ in groups]
        rgath = [[R() for _ in groups] for _ in range(4)]
        rbnc = [[R() for _ in groups] for _ in range(4)]
        rko = R(); rvo = R(); ryo = R()

        ident = cst[:, 0:128]
        Mstr = cst[:, 384:512]
        decT = cst[:, 512:640]
        dq = cst[:, 640:768]

        def ncol(n):
            return {128: 0, 16: 1, 32: 2}[n]

        dma("sp", cst[:], cstin, writes=[rcst])
        op("dve", lambda h: h.tensor_copy(out=identb[:], in_=cst[:, 0:128]), reads=[rcst], writes=[rcb])
        op("dve", lambda h: h.tensor_copy(out=negUb[:], in_=cst[:, 128:256]), reads=[rcst], writes=[rcb])
        op("dve", lambda h: h.tensor_copy(out=negOb[:], in_=cst[:, 256:384]), reads=[rcst], writes=[rcb])
        op("dve", lambda h: h.memset(onesf[:], 1.0), writes=[rcb])
        op("dve", lambda h: h.memset(epst[:], EPS), writes=[rcb])
        op("dve", lambda h: h.memset(onet[:], 1.0), writes=[rcb])

        def bcast_row(dst, row_ap, n, scale=1.0):
            dma("sp", grow[0:1, 0:n], row_ap, writes=[rgrow])
            for c in range(0, n, 512):
                w = min(512, n - c)
                op("pe", lambda h: h.matmul(out=B[0][:, 0:w], lhsT=onesf[0:1, 0:128], rhs=grow[0:1, c:c + w],
                                            start=True, stop=True), reads=[rgrow, rcb], writes=[rB[0]])
                op("act", lambda h: h.activation(out=dst[:, c:c + w], in_=B[0][:, 0:w], func=AF.Copy, scale=scale),
                   reads=[rB[0]], writes=[rGt])

        def load_weights(L, ls):
            W = kb.sb("W%d" % L, [128, 8, WIN_COLS[L]], BF16, ls); rW = R()
            ngt = kb.sb("ngt%d" % L, [128, 8], F32, ls); rng = R()
            dma("sp", ngt[:], ngin[L], writes=[rng])
            wc = WIN_COLS[L]
            i = 0
            for kc in range(8):
                for c0 in range(0, wc, 1024):
                    w = min(1024, wc - c0)
                    w_ = wst[i % 2]; rw_ = rwst[i % 2]; i += 1
                    dma("sp", w_[:, 0:w], win[L][kc * 128:(kc + 1) * 128, c0:c0 + w], writes=[rw_])
                    op("dve", lambda h: h.tensor_scalar(out=W[:, kc, c0:c0 + w], in0=w_[:, 0:w], scalar1=ngt[:, kc:kc + 1],
                                                        scalar2=None, op0=ALU.mult), reads=[rw_, rng], writes=[rW])
            return W, rW

        def load_wout(L, ls):
            n = NCH[L]
            Wo = kb.sb("Wo%d" % L, [128, n, 1024], BF16, ls); rWo = R()
            for cc in range(n):
                w_ = wst[cc % 2]; rw_ = rwst[cc % 2]
                dma("sp", w_[:, 0:1024], wout[L][cc * 128:(cc + 1) * 128, :], writes=[rw_])
                op("pool", lambda h: h.tensor_copy(out=Wo[:, cc, :], in_=w_[:, 0:1024]), reads=[rw_], writes=[rWo])
            return Wo, rWo

        def compute_lam(L):
            dma("sp", lrow[0:1, 0:256], lamin[L], writes=[rlrow])
            for i in range(2):
                op("dve", lambda h: h.tensor_tensor(out=lrow[0:1, 128 * i:128 * i + 64], in0=lrow[0:1, 128 * i:128 * i + 64],
                                                    in1=lrow[0:1, 128 * i + 64:128 * i + 128], op=ALU.mult),
                   reads=[rlrow], writes=[rlrow])
                op("dve", lambda h: h.tensor_reduce(out=lrow[0:1, 256 + i:257 + i], in_=lrow[0:1, 128 * i:128 * i + 64],
                                                    axis=mybir.AxisListType.X, op=ALU.add), reads=[rlrow], writes=[rlrow])
            op("act", lambda h: h.activation(out=lrow[0:1, 258:260], in_=lrow[0:1, 256:258], func=AF.Exp),
               reads=[rlrow], writes=[rlrow])
            op("dve", lambda h: h.tensor_tensor(out=lrow[0:1, 260:261], in0=lrow[0:1, 258:259], in1=lrow[0:1, 259:260],
                                                op=ALU.subtract), reads=[rlrow], writes=[rlrow])
            op("dve", lambda h: h.tensor_scalar(out=lrow[0:1, 261:262], in0=lrow[0:1, 260:261], scalar1=-1.0,
                                                scalar2=-LAM_INIT[L], op0=ALU.mult, op1=ALU.add),
               reads=[rlrow], writes=[rlrow])
            op("pe", lambda h: h.matmul(out=B[0][:, 0:2], lhsT=onesf[0:1, 0:128], rhs=lrow[0:1, 260:262],
                                        start=True, stop=True), reads=[rlrow, rcb], writes=[rB[0]])
            op("act", lambda h: h.activation(out=lamt[:, 0:2], in_=B[0][:, 0:2], func=AF.Copy), reads=[rB[0]], writes=[rlam])

        def group_coll(L, G):
            kb.coll("AllGather", [bounce[L].ap()[G.idx].opt()], [gath[L].ap()[G.idx].opt()], [[0, 1, 2, 3], [4, 5, 6, 7]],
                    reads=[rbnc[L][G.idx]], writes=[rgath[L][G.idx]])

        def phase_a(L, G, Wo, rWo, ob=(0, 1)):
            src = xin if L <= 1 else xs
            n = NCH[L - 1] if L >= 1 else 0

            def loads(ti):
                ofs, nr = G.tiles[ti]
                tok0 = G.c0 + ofs
                dma("sp", xt[ti % 2][0:nr, :], src[tok0:tok0 + nr, :], reads=[rxs[G.idx]] if L >= 2 else [], writes=[rxt[ti % 2]])
                if L >= 1:
                    dma("sp", oga[ti % 2][:, 0:n, 0:nr], gath[L - 1].ap()[G.idx][:, ofs:ofs + nr].rearrange("(c p) n -> p c n", p=128),
                        reads=[rgath[L - 1][G.idx]], writes=[roga[ti % 2]])

            loads(0)
            for ti, (ofs, nr) in enumerate(G.tiles):
                tok0 = G.c0 + ofs
                x_ = xt[ti % 2]; rx_ = rxt[ti % 2]
                s_ = ss[ti % 2]; rs_ = rss[ti % 2]
                if ti + 1 < len(G.tiles):
                    loads(ti + 1)
                if L >= 1:
                    og_ = oga[ti % 2]; rog_ = roga[ti % 2]
                    for hf in range(2):
                        bk_ = ob[hf]
                        for cc in range(n):
                            op("pe", lambda h: h.matmul(out=B[bk_][0:nr, :], lhsT=og_[:, cc, 0:nr],
                                                        rhs=Wo[:, cc, hf * 512:(hf + 1) * 512], start=(cc == 0),
                                                        stop=(cc == n - 1)), reads=[rog_, rWo], writes=[rB[bk_]])
                        op("dve", lambda h: h.tensor_tensor(out=x_[0:nr, hf * 512:(hf + 1) * 512], in0=B[bk_][0:nr, :],
                                                            in1=x_[0:nr, hf * 512:(hf + 1) * 512], op=ALU.add),
                           reads=[rB[bk_], rx_], writes=[rx_])
                        yield
                    dma("sp", xs[tok0:tok0 + nr, :], x_[0:nr, :], reads=[rx_], writes=[rxs[G.idx]])
                else:
                    yield
                op("act", lambda h: h.activation(out=sqj[0:nr, :], in_=x_[0:nr, :], func=AF.Square,
                                                 accum_out=s_[0:nr, 0:1]), reads=[rx_], writes=[rsqj, rs_])
                op("act", lambda h: h.activation(out=s_[0:nr, 1:2], in_=s_[0:nr, 0:1], func=AF.Sqrt, scale=1.0 / 1024,
                                                 bias=epst[0:nr, 0:1]), reads=[rs_, rcb], writes=[rs_])
                op("dve", lambda h: h.reciprocal(out=s_[0:nr, 2:3], in_=s_[0:nr, 1:2]), reads=[rs_], writes=[rs_])
                op("dve", lambda h: h.tensor_scalar(out=xh[0:nr, :], in0=x_[0:nr, :], scalar1=s_[0:nr, 2:3],
                                                    scalar2=None, op0=ALU.mult), reads=[rx_, rs_], writes=[rxh])
                yield
                for kc in range(8):
                    op("pe", lambda h: h.transpose(out=TRb[:, kc, 0:nr], in_=xh[0:nr, kc * 128:(kc + 1) * 128],
                                                   identity=identb[0:nr, 0:nr]), reads=[rxh, rcb], writes=[rTR])
                op("act", lambda h: h.activation(out=xnT[:, :, ofs:ofs + nr], in_=TRb[:, :, 0:nr], func=AF.Copy),
                   reads=[rTR], writes=[rxnT])
                yield

        def final_pass(Wo, rWo):
            n = NCH[3]
            tl = [(G, ofs, nr) for G in groups for (ofs, nr) in G.tiles]

            def loads(i):
                G, ofs, nr = tl[i]
                tok0 = G.c0 + ofs
                dma("sp", xtf[i % 4][0:nr, :], xs[tok0:tok0 + nr, :], reads=[rxs[G.idx]], writes=[rxtf[i % 4]])
                dma("sp", ogaf[i % 4][:, 0:n, 0:nr], gath[3].ap()[G.idx][:, ofs:ofs + nr].rearrange("(c p) n -> p c n", p=128),
                    reads=[rgath[3][G.idx]], writes=[rogaf[i % 4]])

            loads(0)
            loads(1)
            for i, (G, ofs, nr) in enumerate(tl):
                tok0 = G.c0 + ofs
                if i + 2 < len(tl):
                    loads(i + 2)
                x_ = xtf[i % 4]; rx_ = rxtf[i % 4]; s_ = ssf[i % 4]; rs_ = rssf[i % 4]
                og_ = ogaf[i % 4]; rog_ = rogaf[i % 4]
                ob = ((0, 1), (2, 3), (4, 5))[i % 3]
                for hf in range(2):
                    bk_ = ob[hf]
                    for cc in range(n):
                        op("pe", lambda h: h.matmul(out=B[bk_][0:nr, :], lhsT=og_[:, cc, 0:nr],
                                                    rhs=Wo[:, cc, hf * 512:(hf + 1) * 512], start=(cc == 0),
                                                    stop=(cc == n - 1)), reads=[rog_, rWo], writes=[rB[bk_]])
                    op("dve", lambda h: h.tensor_tensor(out=x_[0:nr, hf * 512:(hf + 1) * 512], in0=B[bk_][0:nr, :],
                                                        in1=x_[0:nr, hf * 512:(hf + 1) * 512], op=ALU.add),
                       reads=[rB[bk_], rx_], writes=[rx_])
                op("act", lambda h: h.activation(out=sqj[0:nr, :], in_=x_[0:nr, :], func=AF.Square,
                                                 accum_out=s_[0:nr, 0:1]), reads=[rx_], writes=[rsqj, rs_])
                op("act", lambda h: h.activation(out=s_[0:nr, 1:2], in_=s_[0:nr, 0:1], func=AF.Sqrt, scale=1.0 / 1024,
                                                 bias=epst[0:nr, 0:1]), reads=[rs_, rcb], writes=[rs_])
                op("dve", lambda h: h.reciprocal(out=s_[0:nr, 2:3], in_=s_[0:nr, 1:2]), reads=[rs_], writes=[rs_])
                for hf in range(2):
                    eng = "dve" if hf == 0 else "pool"
                    if eng == "dve":
                        op("dve", lambda h: h.scalar_tensor_tensor(out=x_[0:nr, 0:512], in0=x_[0:nr, 0:512],
                                                                   scalar=s_[0:nr, 2:3], in1=GF[0][0:nr, :],
                                                                   op0=ALU.mult, op1=ALU.mult),
                           reads=[rx_, rs_, rGF], writes=[rx_])
                    else:
                        op("dve", lambda h: h.scalar_tensor_tensor(out=x_[0:nr, 512:1024], in0=x_[0:nr, 512:1024],
                                                                   scalar=s_[0:nr, 2:3], in1=GF[1][0:nr, :],
                                                                   op0=ALU.mult, op1=ALU.mult),
                           reads=[rx_, rs_, rGF], writes=[rx_])
                dma("sp", yo[tok0:tok0 + nr, :], x_[0:nr, :], reads=[rx_], writes=[ryo])

        def proj_fm(W, rW, wc0, bank, ncols):
            for kc in range(8):
                op("pe", lambda h: h.matmul(out=B[bank][:, 0:ncols], lhsT=W[:, kc, wc0:wc0 + 128],
                                            rhs=xnT[:, kc, 0:ncols], start=(kc == 0), stop=(kc == 7)),
                   reads=[rW, rxnT], writes=[rB[bank]])

        def proj_tm(W, rW, wc0, wn, bank, ofs, nr):
            for kc in range(8):
                op("pe", lambda h: h.matmul(out=B[bank][0:nr, 0:wn], lhsT=xnT[:, kc, ofs:ofs + nr],
                                            rhs=W[:, kc, wc0:wc0 + wn], start=(kc == 0), stop=(kc == 7)),
                   reads=[rW, rxnT], writes=[rB[bank]])

        def proj_attn(L, G, W, rW, KT, rKT, V, rV, qscale, is_da, bk):
            n = G.ncols
            par = G.idx % 2
            proj_fm(W, rW, 0, bk, n)
            op("act", lambda h: h.activation(out=qT[:, par, 0:n], in_=B[bk][:, 0:n], func=AF.Copy, scale=qscale),
               reads=[rB[bk]], writes=[rqTp[par]])
            yield
            proj_fm(W, rW, 128, bk, n)
            op("act", lambda h: h.activation(out=kst[:, 0:n], in_=B[bk][:, 0:n], func=AF.Copy), reads=[rB[bk]], writes=[rkst])
            dma("sp", kTo[L][:, G.c0:G.c0 + n], kst[:, 0:n], reads=[rkst], writes=[rko])
            op("pool", lambda h: h.tensor_copy(out=KT[:, G.c0:G.c0 + n], in_=kst[:, 0:n]), reads=[rkst], writes=[rKT])
            yield
            for ti, (ofs, nr) in enumerate(G.tiles):
                proj_tm(W, rW, 256, 128 if is_da else 256, bk, ofs, nr)
                v_ = vst[ti % 2]; rv_ = rvst[ti % 2]
                op("act", lambda h: h.activation(out=v_[0:nr, :], in_=B[bk][0:nr, 0:128], func=AF.Copy),
                   reads=[rB[bk]], writes=[rv_])
                dma("sp", vo[L][G.c0 + ofs:G.c0 + ofs + nr, :], v_[0:nr, :], reads=[rv_], writes=[rvo])
                op("pool", lambda h: h.tensor_copy(out=V[0:nr, G.vt0 + ti, 0:128], in_=v_[0:nr, :]), reads=[rv_], writes=[rV])
                if not is_da:
                    sgv = sg[0:nr, ti, par * 128:(par + 1) * 128]
                    op("act", lambda h: h.activation(out=sgv, in_=B[bk][0:nr, 128:256], func=AF.Silu),
                       reads=[rB[bk]], writes=[rsgp[par]])
                yield
            if is_da:
                proj_fm(W, rW, 384, bk, n)
                op("act", lambda h: h.activation(out=sg[:, par, 0:n], in_=B[bk][:, 0:n], func=AF.Silu),
                   reads=[rB[bk]], writes=[rsgp[par]])
                op("pool", lambda h: h.tensor_scalar(out=sg[:, par, 0:n], in0=sg[:, par, 0:n], scalar1=lamt[:, 3:4],
                                                     scalar2=None, op0=ALU.mult), reads=[rsgp[par], rlam], writes=[rsgp[par]])
                yield

        def key_tiles(G, KT, V, for_sb):
            kts = []
            if G.kind == "sample":
                for jj in range(NCB):
                    k = KT_(); k.nk = 128; k.shared = False
                    k.kT = [KT[:, qi * PAST + jj * 128: qi * PAST + (jj + 1) * 128] for qi in range(4)]
                    k.v = [V[:, qi * NCB + jj, :] for qi in range(4)]
                    k.active = [0, 1, 2, 3]; k.diag = []
                    kts.append(k)
                k = KT_(); k.nk = 32; k.shared = False
                k.kT = [KT[:, NP + 32 * qi: NP + 32 * qi + 32] for qi in range(4)]
                k.v = [V[:, 4 * NCB + qi, :] for qi in range(4)]
                k.active = [0, 1, 2, 3]; k.diag = [0, 1, 2, 3] if for_sb else []
                kts.append(k)
                return kts
            k = KT_(); k.nk = 16; k.shared = True; k.kT = [KT[:, 0:16]] * 4; k.v = [V[:, 0, :]] * 4
            k.active = list(range(len(G.tiles)))
            k.diag = [0] if (G.kind == "meta" and for_sb) else []
            kts.append(k)
            if G.kind == "frame":
                i0 = G.vt0
                for j in range(1, i0 + 4):
                    k = KT_(); k.nk = 128; k.shared = True
                    c = 16 + 128 * (j - 1)
                    k.kT = [KT[:, c:c + 128]] * 4; k.v = [V[:, j, :]] * 4
                    k.active = [qi for qi in range(4) if i0 + qi >= j]
                    k.diag = [j - i0] if j >= i0 else []
                    kts.append(k)
            return kts

        def da_b(L, G, KT, rKT, V, rV, gen, nch):
            kts = key_tiles(G, KT, V, False)
            par = G.idx % 2
            n = G.ncols
            rq_ = rqTp[par]; rsg_ = rsgp[par]
            OT = [B[2], B[3]]; rOT = [rB[2], rB[3]]
            Pacc = [TF[2], TF[3]]; rPa = [rTF[2], rTF[3]]
            units = [(m, k) for k in kts for m in range(2)]
            SBK = [B[0], B[5], B[6]]; rSBK = [rB[0], rB[5], rB[6]]
            PTS = [TH[0], TH[1], TH[2], TH[4]]; rPTS = [rTH[0], rTH[1], rTH[2], rTH[4]]

            def stS(u):
                m, k = units[u]; p0 = 64 * m; nk = k.nk
                cA = G.tiles[k.active[0]][0]; cE = G.ncols
                sb_ = SBK[u % 3]; rs_ = rSBK[u % 3]
                if k.shared:
                    op("pe", lambda h: h.matmul(out=sb_[0:nk, cA:cE], lhsT=k.kT[0][p0:p0 + 64, :],
                                                rhs=qT[p0:p0 + 64, par, cA:cE], start=True, stop=True),
                       reads=[rKT, rq_], writes=[rs_])
                else:
                    for qi in k.active:
                        ofs, nr = G.tiles[qi]
                        op("pe", lambda h: h.matmul(out=sb_[0:nk, ofs:ofs + nr], lhsT=k.kT[qi][p0:p0 + 64, :],
                                                    rhs=qT[p0:p0 + 64, par, ofs:ofs + nr], start=True, stop=True),
                           reads=[rKT, rq_], writes=[rs_])

            def stE(u):
                m, k = units[u]; nk = k.nk
                cA = G.tiles[k.active[0]][0]; cE = G.ncols
                sb_ = SBK[u % 3]; rs_ = rSBK[u % 3]
                pt = PTS[u % 4]; rpt = rPTS[u % 4]
                op("act", lambda h: h.activation(out=pt[0:nk, cA:cE], in_=sb_[0:nk, cA:cE], func=AF.Exp),
                   reads=[rs_], writes=[rpt])
                for qi in k.diag:
                    ofs, nr = G.tiles[qi]
                    op("pool", lambda h: h.memset(pt[64:128, ofs:ofs + 64], 0.0), writes=[rpt])
                op("dve", lambda h: h.tensor_tensor(out=Pacc[m][0:nk, cA:cE], in0=Pacc[m][0:nk, cA:cE], in1=pt[0:nk, cA:cE],
                                                    op=ALU.add), reads=[rpt, rPa[m]], writes=[rPa[m]])

            def stP(u):
                m, k = units[u]; nk = k.nk
                cA = G.tiles[k.active[0]][0]; cE = G.ncols
                pt = PTS[u % 4]; rpt = rPTS[u % 4]
                if k.shared:
                    op("pe", lambda h: h.matmul(out=OT[m][:, cA:cE], lhsT=k.v[0][0:nk, 0:128], rhs=pt[0:nk, cA:cE],
                                                start=False, stop=True, skip_group_check=True),
                       reads=[rpt, rV], writes=[rOT[m]])
                else:
                    for qi in k.active:
                        ofs, nr = G.tiles[qi]
                        op("pe", lambda h: h.matmul(out=OT[m][:, ofs:ofs + nr], lhsT=k.v[qi][0:nk, 0:128], rhs=pt[0:nk, ofs:ofs + nr],
                                                    start=False, stop=True, skip_group_check=True),
                           reads=[rpt, rV], writes=[rOT[m]])

            nu = len(units)
            per = -(-nch // max(1, nu))
            for st in range(nu + 2):
                if st < nu:
                    stS(st)
                if 0 <= st - 1 < nu:
                    stE(st - 1)
                if 0 <= st - 2 < nu:
                    stP(st - 2)
                if gen is not None:
                    for _ in range(per):
                        next(gen, None)
            if gen is not None:
                for _ in gen:
                    pass
            osb = [TF[0], TF[1]]; rosb = [rTF[0], rTF[1]]
            rdn = [TF[4], TF6]; rrdn = [rTF[4], rTF6]
            for m in range(2):
                op("act", lambda h: h.activation(out=osb[m][:, 0:n], in_=OT[m][:, 0:n], func=AF.Copy), reads=[rOT[m]], writes=[rosb[m]])
            for m in range(2):
                op("dve", lambda h: h.memset(OT[m][:], 0.0), writes=[rOT[m]])
            for m in range(2):
                op("pe", lambda h: h.matmul(out=B[4][:, 0:n], lhsT=onesf[:, :], rhs=Pacc[m][:, 0:n], start=True, stop=True),
                   reads=[rPa[m], rcb], writes=[rB[4]])
                op("dve", lambda h: h.reciprocal(out=rdn[m][:, 0:n], in_=B[4][:, 0:n]), reads=[rB[4]], writes=[rrdn[m]])
                op("pool", lambda h: h.memset(Pacc[m][:], 0.0), writes=[rPa[m]])
            for m in range(2):
                op("dve", lambda h: h.tensor_tensor(out=osb[m][:, 0:n], in0=osb[m][:, 0:n], in1=rdn[m][:, 0:n], op=ALU.mult),
                   reads=[rosb[m], rrdn[m]], writes=[rosb[m]])
            op("dve", lambda h: h.scalar_tensor_tensor(out=osb[0][:, 0:n], in0=osb[1][:, 0:n], scalar=lamt[:, 1:2], in1=osb[0][:, 0:n],
                                                       op0=ALU.mult, op1=ALU.add), reads=[rosb[0], rosb[1], rlam], writes=[rosb[0]])
            op("pool", lambda h: h.tensor_tensor(out=rdn[0][:, 0:n], in0=osb[0][:, 0:n], in1=osb[0][:, 0:n], op=ALU.mult),
               reads=[rosb[0]], writes=[rrdn[0]])
            op("pe", lambda h: h.matmul(out=B[4][:, 0:n], lhsT=onesf[:, :], rhs=rdn[0][:, 0:n], start=True, stop=True),
               reads=[rrdn[0], rcb], writes=[rB[4]])
            op("act", lambda h: h.activation(out=rdn[1][:, 0:n], in_=B[4][:, 0:n], func=AF.Sqrt, scale=1.0 / 128, bias=epst[:, 0:1]),
               reads=[rB[4], rcb], writes=[rrdn[1]])
            op("dve", lambda h: h.reciprocal(out=rdn[1][:, 0:n], in_=rdn[1][:, 0:n]), reads=[rrdn[1]], writes=[rrdn[1]])
            op("dve", lambda h: h.tensor_tensor(out=osb[0][:, 0:n], in0=osb[0][:, 0:n], in1=rdn[1][:, 0:n], op=ALU.mult),
               reads=[rosb[0], rrdn[1]], writes=[rosb[0]])
            op("dve", lambda h: h.tensor_tensor(out=ogT[:, 0, 0:n], in0=osb[0][:, 0:n], in1=sg[:, par, 0:n], op=ALU.mult),
               reads=[rosb[0], rsg_], writes=[rogT])
            dma("sp", bounce[L].ap()[G.idx][0:128, 0:n], ogT[:, 0, 0:n], reads=[rogT], writes=[rbnc[L][G.idx]])
            group_coll(L, G)

        def sb_b(L, G, KT, rKT, V, rV, gen, nch):
            kts = key_tiles(G, KT, V, True)[::-1]
            par = G.idx % 2
            rq_ = rqTp[par]; rsg_ = rsgp[par]
            Racc = TF[5]; rR = rTF[5]
            op("dve", lambda h: h.memset(B[6][:], 0.0), writes=[rB[6]])
            op("pool", lambda h: h.memset(Racc[:], 0.0), writes=[rR])
            nu = len(kts)
            EB = [TF[0], TF[1], TF[2], TF[3]]; rEB = [rTF[0], rTF[1], rTF[2], rTF[3]]
            XB = [TF[4], TF6]; rXB = [rTF[4], rTF6]

            def rng(k):
                return G.tiles[k.active[0]][0], G.ncols

            def sA(u):
                k = kts[u]; nk = k.nk; cA, cE = rng(k)
                zb = B[0]; rz = rB[0]
                if k.shared:
                    op("pe", lambda h: h.matmul(out=zb[0:nk, cA:cE], lhsT=k.kT[0], rhs=qT[:, par, cA:cE], start=True, stop=True),
                       reads=[rKT, rq_], writes=[rz])
                else:
                    for qi in k.active:
                        ofs, nr = G.tiles[qi]
                        op("pe", lambda h: h.matmul(out=zb[0:nk, ofs:ofs + nr], lhsT=k.kT[qi], rhs=qT[:, par, ofs:ofs + nr],
                                                    start=True, stop=True), reads=[rKT, rq_], writes=[rz])

            def sB(u):
                k = kts[u]; nk = k.nk; cA, cE = rng(k)
                zb = B[0]; rz = rB[0]
                E_ = EB[u % 4]; rE = rEB[u % 4]
                Lp = TH[u % 2]; rL = rTH[u % 2]
                op("act", lambda h: h.activation(out=E_[0:nk, cA:cE], in_=zb[0:nk, cA:cE], func=AF.Exp), reads=[rz], writes=[rE])
                for qi in k.diag:
                    ofs, nr = G.tiles[qi]
                    op("pool", lambda h: h.tensor_tensor(out=E_[0:nk, ofs:ofs + nr], in0=E_[0:nk, ofs:ofs + nr],
                                                         in1=Mstr[0:nk, 0:nr], op=ALU.mult), reads=[rE, rcst], writes=[rE])
                op("act", lambda h: h.activation(out=Lp[0:nk, cA:cE], in_=E_[0:nk, cA:cE], func=AF.Ln, scale=1.0,
                                                 bias=onet[0:nk, 0:1]), reads=[rE, rcb], writes=[rL])

            def sC(u):
                k = kts[u]; nk = k.nk; cA, cE = rng(k)
                Lp = TH[u % 2]; rL = rTH[u % 2]
                tb = B[2 + u % 2]; rt = rB[2 + u % 2]
                cb = B[4 + u % 2]; rc = rB[4 + u % 2]
                op("pe", lambda h: h.matmul(out=tb[0:nk, cA:cE], lhsT=negUb[0:nk, 0:nk], rhs=Lp[0:nk, cA:cE], start=True, stop=True),
                   reads=[rL, rcb], writes=[rt])
                if u != nu - 1:
                    op("pe", lambda h: h.matmul(out=cb[:, cA:cE], lhsT=negOb[0:nk, 0:128], rhs=Lp[0:nk, cA:cE], start=True, stop=True),
                       reads=[rL, rcb], writes=[rc])

            def sD(u):
                k = kts[u]; nk = k.nk; cA, cE = rng(k)
                E_ = EB[u % 4]; rE = rEB[u % 4]
                X_ = XB[u % 2]; rX = rXB[u % 2]
                tb = B[2 + u % 2]; rt = rB[2 + u % 2]
                cb = B[4 + u % 2]; rc = rB[4 + u % 2]
                A_ = TH[2 + u % 2]; rA = rTH[2 + u % 2]
                op("dve", lambda h: h.tensor_tensor(out=X_[0:nk, cA:cE], in0=tb[0:nk, cA:cE], in1=Racc[0:nk, cA:cE], op=ALU.add),
                   reads=[rt, rR], writes=[rX])
                if u != nu - 1:
                    op("dve", lambda h: h.tensor_tensor(out=Racc[:, cA:cE], in0=cb[:, cA:cE], in1=Racc[:, cA:cE], op=ALU.add),
                       reads=[rc, rR], writes=[rR])
                op("act", lambda h: h.activation(out=X_[0:nk, cA:cE], in_=X_[0:nk, cA:cE], func=AF.Exp), reads=[rX], writes=[rX])
                op("pool", lambda h: h.tensor_tensor(out=A_[0:nk, cA:cE], in0=E_[0:nk, cA:cE], in1=X_[0:nk, cA:cE], op=ALU.mult),
                   reads=[rE, rX], writes=[rA])

            def sE(u):
                k = kts[u]; nk = k.nk
                A_ = TH[2 + u % 2]; rA = rTH[2 + u % 2]
                for qi in k.active:
                    ofs, nr = G.tiles[qi]
                    op("pe", lambda h: h.matmul(out=B[6][0:nr, qi * 128:(qi + 1) * 128], lhsT=A_[0:nk, ofs:ofs + nr],
                                                rhs=k.v[qi][0:nk, 0:128], start=False, stop=True, skip_group_check=True),
                       reads=[rA, rV], writes=[rB[6]])

            per = -(-nch // max(1, nu))
            for st in range(nu + 4):
                if 0 <= st - 1 < nu:
                    sB(st - 1)
                if 0 <= st - 2 < nu:
                    sC(st - 2)
                if 0 <= st - 3 < nu:
                    sD(st - 3)
                if 0 <= st - 4 < nu:
                    sE(st - 4)
                if st < nu:
                    sA(st)
                if gen is not None:
                    for _ in range(per):
                        next(gen, None)
            if gen is not None:
                for _ in gen:
                    pass
            og = TH[4]
            for qi, (ofs, nr) in enumerate(G.tiles):
                op("dve", lambda h: h.tensor_tensor(out=og[0:nr, 0:128], in0=B[6][0:nr, qi * 128:(qi + 1) * 128],
                                                    in1=sg[0:nr, qi, par * 128:(par + 1) * 128], op=ALU.mult), reads=[rB[6], rsg_], writes=[rTH[4]])
                op("pe", lambda h: h.transpose(out=TRb[:, 0, 0:nr], in_=og[0:nr, 0:128], identity=identb[0:nr, 0:nr]),
                   reads=[rTH[4], rcb], writes=[rTR])
                op("act", lambda h: h.activation(out=ogT[:, 0, ofs:ofs + nr], in_=TRb[:, 0, 0:nr], func=AF.Copy),
                   reads=[rTR], writes=[rogT])
            dma("sp", bounce[L].ap()[G.idx][0:128, 0:G.ncols], ogT[:, 0, 0:G.ncols], reads=[rogT], writes=[rbnc[L][G.idx]])
            group_coll(L, G)

        def load_cache(L, KT, rKT, V, rV):
            for c in range(0, 4 * PAST, 1024):
                w_ = wst[(c // 1024) % 2]; rw_ = rwst[(c // 1024) % 2]
                dma("sp", w_[:, 0:1024], ckin[L][:, c:c + 1024], writes=[rw_])
                op("pool", lambda h: h.tensor_copy(out=KT[:, c:c + 1024], in_=w_[:, 0:1024]), reads=[rw_], writes=[rKT])
            i = 0
            for s in range(4):
                for j0 in range(0, NCB, 8):
                    nj = min(8, NCB - j0)
                    w_ = wst[i % 2]; rw_ = rwst[i % 2]; i += 1
                    dma("sp", w_[:, 0:nj * 128].rearrange("p (j d) -> p j d", d=128),
                        cvin[L][s, j0 * 128:(j0 + nj) * 128, :].rearrange("(j p) d -> p j d", p=128), writes=[rw_])
                    op("pool", lambda h: h.tensor_copy(out=V[:, s * NCB + j0:s * NCB + j0 + nj, 0:128],
                                                       in_=w_[:, 0:nj * 128].rearrange("p (j d) -> p j d", d=128)),
                       reads=[rw_], writes=[rV])

        if STOP == 1:
            kb.barrier()
            return nc
        CKN[0] = 0
        try:
          for L in range(4):
              kind = L % 3
              with contextlib.ExitStack() as ls:
                  kb.barrier()
                  W, rW = load_weights(L, ls)
                  if STOP == 2:
                      kb.barrier()
                      return nc
                  Wo = rWo = None
                  if L >= 1:
                      Wo, rWo = load_wout(L - 1, ls)
                      if L == 2:
                          ck('wout L2')
                  if kind in (0, 2):
                      KT = kb.sb("KT%d" % L, [128, NT], BF16, ls); rKT = R()
                      V = kb.sb("V%d" % L, [128, NVT, 130], BF16, ls); rV = R()
                      op("pool", lambda h: h.memset(V[:], 1.0), writes=[rV])
                      if L == 2:
                          ck('vmemset L2')
                      if kind == 0:
                          compute_lam(L)
                          dma("sp", lamt[:, 2:3], subg[L].rearrange("o d -> d o"), writes=[rlam])
                          op("dve", lambda h: h.tensor_scalar(out=lamt[:, 3:4], in0=lamt[:, 2:3], scalar1=1.0 - LAM_INIT[L],
                                                              scalar2=None, op0=ALU.mult), reads=[rlam], writes=[rlam])
                          for b_ in (2, 3):
                              op("dve", lambda h: h.memset(B[b_][:], 0.0), writes=[rB[b_]])
                          for i_ in (2, 3):
                              op("pool", lambda h: h.memset(TF[i_][:], 0.0), writes=[rTF[i_]])
                      def prep(G_):
                          if G_.kind == "sample":
                              load_cache(L, KT, rKT, V, rV)
                          yield from phase_a(L, G_, Wo, rWo, ob=(1, 1))
                          yield from proj_attn(L, G_, W, rW, KT, rKT, V, rV, (64 if kind == 0 else 128) ** -0.5, kind == 0, 1)

                      for _ in prep(groups[0]):
                          pass
                      for gi, G in enumerate(groups):
                          nxt = groups[gi + 1] if gi + 1 < len(groups) else None
                          gen = prep(nxt) if (nxt is not None and nxt.kind != "sample") else None
                          nch = 5 * len(nxt.tiles) + 2 if gen is not None else 0
                          if kind == 0:
                              da_b(L, G, KT, rKT, V, rV, gen, nch)
                          else:
                              sb_b(L, G, KT, rKT, V, rV, gen, nch)
                          ck('phaseB L%d G%d' % (L, G.idx))
                          if nxt is not None and nxt.kind == "sample":
                              for _ in prep(nxt):
                                  pass
                  else:
                      S = kb.sb("S", [128, 2, 512], F32, ls); rS = R()
                      Sb = kb.sb("Sb", [128, 2, 512], BF16, ls); rSb = R()
                      qT2 = kb.sb("qT2", [128, 2, 512], BF16, ls)
                      qTr = [qT, qT2]; rqTr = [R(), R()]
                      kTr = [kb.sb("kTr%d" % i, [128, 2, 512], BF16, ls) for i in range(2)]; rkTr = [R(), R()]
                      vr = [kb.sb("vr%d" % i, [128, 4, 512], BF16, ls) for i in range(2)]; rvr = [R(), R()]
                      sg2 = kb.sb("sg2", [128, 4, 512], F32, ls)
                      sgr = [sg, sg2]; rsgr = [R(), R()]
                      cs = kb.sb("cs", [128, 512], F32, ls); sn = kb.sb("sn", [128, 512], F32, ls); rcs = R(); rsn = R()
                      qd = kb.sb("qd", [128, 2, 128], BF16, ls); rqd = R()
                      kd = kb.sb("kd", [128, 2, 128], BF16, ls); rkd = R()
                      sTm = kb.sb("sTm", [128, 128], BF16, ls); rsT = R()
                      og4 = kb.sb("og4", [128, 512], BF16, ls); rog4 = R()
                      bcast_row(Gt, gn1, 512, 1.0)
                      op("dve", lambda h: h.memset(S[:], 0.0), writes=[rS])
                      op("dve", lambda h: h.memset(Sb[:], 0.0), writes=[rSb])

                      def prep_ret(G_):
                          n = G_.ncols; par = G_.idx % 2
                          yield from phase_a(L, G_, Wo, rWo, ob=(4, 5))
                          dma("sp", cs[:, 0:n], cosin[:, G_.c0:G_.c0 + n], writes=[rcs])
                          dma("sp", sn[:, 0:n], sinin[:, G_.c0:G_.c0 + n], writes=[rsn])
                          for which, dst, rdst, sc in ((0, qTr[par], rqTr[par], 1.0), (1, kTr[par], rkTr[par], 1.0 / 16)):
                              b0 = 4; b1 = 5
                              proj_fm(W, rW, 256 * which, b0, n)
                              yield
                              proj_fm(W, rW, 256 * which + 128, b1, n)
                              yield
                              t0, t1, t2, t3 = TF[0], TF[1], TF[2], TF[3]
                              op("dve", lambda h: h.scalar_tensor_tensor(out=t0[:, 0:n], in0=B[b0][:, 0:n], scalar=sc, in1=cs[:, 0:n],
                                                                         op0=ALU.mult, op1=ALU.mult), reads=[rB[b0], rcs], writes=[rTF[0]])
                              op("dve", lambda h: h.scalar_tensor_tensor(out=t1[:, 0:n], in0=B[b1][:, 0:n], scalar=sc, in1=sn[:, 0:n],
                                                                         op0=ALU.mult, op1=ALU.mult), reads=[rB[b1], rsn], writes=[rTF[1]])
                              op("pool", lambda h: h.tensor_tensor(out=dst[:, 0, 0:n], in0=t0[:, 0:n], in1=t1[:, 0:n], op=ALU.subtract),
                                 reads=[rTF[0], rTF[1]], writes=[rdst])
                              op("dve", lambda h: h.scalar_tensor_tensor(out=t2[:, 0:n], in0=B[b0][:, 0:n], scalar=sc, in1=sn[:, 0:n],
                                                                         op0=ALU.mult, op1=ALU.mult), reads=[rB[b0], rsn], writes=[rTF[2]])
                              op("dve", lambda h: h.scalar_tensor_tensor(out=t3[:, 0:n], in0=B[b1][:, 0:n], scalar=sc, in1=cs[:, 0:n],
                                                                         op0=ALU.mult, op1=ALU.mult), reads=[rB[b1], rcs], writes=[rTF[3]])
                              op("pool", lambda h: h.tensor_tensor(out=dst[:, 1, 0:n], in0=t2[:, 0:n], in1=t3[:, 0:n], op=ALU.add),
                                 reads=[rTF[2], rTF[3]], writes=[rdst])
                              yield
                          for ti, (ofs, nr) in enumerate(G_.tiles):
                              proj_tm(W, rW, 512, 512, 6, ofs, nr)
                              op("act", lambda h: h.activation(out=vr[par][0:nr, ti, :], in_=B[6][0:nr, :], func=AF.Copy),
                                 reads=[rB[6]], writes=[rvr[par]])
                              yield
                              bk = 4 + ti % 2
                              proj_tm(W, rW, 1024, 512, bk, ofs, nr)
                              op("act", lambda h: h.activation(out=sgr[par][0:nr, ti, :], in_=B[bk][0:nr, :], func=AF.Silu),
                                 reads=[rB[bk]], writes=[rsgr[par]])
                              op("pool", lambda h: h.tensor_tensor(out=sgr[par][0:nr, ti, :], in0=sgr[par][0:nr, ti, :], in1=Gt[0:nr, :], op=ALU.mult),
                                 reads=[rsgr[par], rGt], writes=[rsgr[par]])
                              yield

                      def pump(gen, k):
                          if gen is not None:
                              for _ in range(k):
                                  next(gen, None)

                      for _ in prep_ret(groups[0]):
                          pass
                      for gi, G in enumerate(groups):
                          n = G.ncols; par = G.idx % 2
                          nxt = groups[gi + 1] if gi + 1 < len(groups) else None
                          gen = prep_ret(nxt) if nxt is not None else None
                          nch = (4 * len(nxt.tiles) + 6 + 2 * len(nxt.tiles)) if nxt is not None else 0
                          pk = -(-nch // (4 * len(G.tiles)))
                          qT_ = qTr[par]; rq_ = rqTr[par]; kT_ = kTr[par]; rk_ = rkTr[par]
                          vr_ = vr[par]; rv_ = rvr[par]; sg_ = sgr[par]; rsg_ = rsgr[par]
                          for ti, (ofs, nr) in enumerate(G.tiles):
                              cn = ncol(nr)
                              if G.kind == "sample":
                                  dma("sp", S[:], st1[ti].rearrange("(c p) e -> p c e", p=128), writes=[rS])
                                  op("act", lambda h: h.activation(out=Sb[:], in_=S[:], func=AF.Copy), reads=[rS], writes=[rSb])
                              for c in range(2):
                                  op("pe", lambda h: h.matmul(out=B[0][0:nr, 0:nr], lhsT=kT_[:, c, ofs:ofs + nr], rhs=qT_[:, c, ofs:ofs + nr],
                                                              start=(c == 0), stop=(c == 1)), reads=[rk_, rq_], writes=[rB[0]])
                              op("dve", lambda h: h.tensor_tensor(out=sTm[0:nr, 0:nr], in0=B[0][0:nr, 0:nr], in1=decT[0:nr, 0:nr], op=ALU.mult),
                                 reads=[rB[0], rcst], writes=[rsT])
                              for c in range(2):
                                  op("pool", lambda h: h.tensor_tensor(out=qd[:, c, 0:nr], in0=qT_[:, c, ofs:ofs + nr], in1=dq[:, 0:nr], op=ALU.mult),
                                     reads=[rq_, rcst], writes=[rqd])
                                  op("pe", lambda h: h.transpose(out=TRb[0:nr, c, :], in_=kT_[:, c, ofs:ofs + nr], identity=identb[:, :]),
                                     reads=[rk_, rcb], writes=[rTR])
                              op("dve", lambda h: h.tensor_scalar(out=kd[0:nr, :, :], in0=TRb[0:nr, 0:2, :], scalar1=cst[0:nr, 768 + cn:769 + cn],
                                                                  scalar2=None, op0=ALU.mult), reads=[rTR, rcst], writes=[rkd])
                              pump(gen, pk)
                              op("pe", lambda h: h.matmul(out=B[1][0:nr, :], lhsT=sTm[0:nr, 0:nr], rhs=vr_[0:nr, ti, :], start=True, stop=False),
                                 reads=[rsT, rv_], writes=[rB[1]])
                              for c in range(2):
                                  op("pe", lambda h: h.matmul(out=B[1][0:nr, :], lhsT=qd[:, c, 0:nr], rhs=Sb[:, c, :], start=False, stop=(c == 1)),
                                     reads=[rqd, rSb], writes=[rB[1]])
                              for c in range(2):
                                  op("pe", lambda h: h.matmul(out=B[2 + c][:, :], lhsT=kd[0:nr, c, :], rhs=vr_[0:nr, ti, :], start=True, stop=True),
                                     reads=[rkd, rv_], writes=[rB[2 + c]])
                                  op("dve", lambda h: h.scalar_tensor_tensor(out=S[:, c, :], in0=S[:, c, :], scalar=cst[:, 771 + cn:772 + cn],
                                                                             in1=B[2 + c][:, :], op0=ALU.mult, op1=ALU.add),
                                     reads=[rS, rcst, rB[2 + c]], writes=[rS])
                              op("act", lambda h: h.activation(out=Sb[:], in_=S[:], func=AF.Copy), reads=[rS], writes=[rSb])
                              pump(gen, pk)
                              if G.kind == "sample":
                                  dma("sp", rets[ti].rearrange("(c p) e -> p c e", p=128), S[:], reads=[rS], writes=[ryo])
                              op("act", lambda h: h.activation(out=sqj[0:nr, 0:512], in_=B[1][0:nr, :], func=AF.Square, accum_out=rd[0:nr, 0:1]),
                                 reads=[rB[1]], writes=[rsqj, rrd])
                              op("act", lambda h: h.activation(out=rd[0:nr, 1:2], in_=rd[0:nr, 0:1], func=AF.Sqrt, scale=1.0 / 512,
                                                               bias=epst[0:nr, 0:1]), reads=[rrd, rcb], writes=[rrd])
                              op("dve", lambda h: h.reciprocal(out=rd[0:nr, 2:3], in_=rd[0:nr, 1:2]), reads=[rrd], writes=[rrd])
                              op("dve", lambda h: h.scalar_tensor_tensor(out=og4[0:nr, :], in0=B[1][0:nr, :], scalar=rd[0:nr, 2:3],
                                                                         in1=sg_[0:nr, ti, :], op0=ALU.mult, op1=ALU.mult),
                                 reads=[rB[1], rrd, rsg_], writes=[rog4])
                              pump(gen, pk)
                              for e in range(4):
                                  op("pe", lambda h: h.transpose(out=TRb[:, 4 + e, 0:nr], in_=og4[0:nr, e * 128:(e + 1) * 128],
                                                                 identity=identb[0:nr, 0:nr]), reads=[rog4, rcb], writes=[rTR])
                              op("act", lambda h: h.activation(out=ogT[:, 0:4, ofs:ofs + nr], in_=TRb[:, 4:8, 0:nr], func=AF.Copy),
                                 reads=[rTR], writes=[rogT])
                              pump(gen, pk)
                          if G.idx == NGF:
                              dma("sp", retp.rearrange("(c p) e -> p c e", p=128), S[:], reads=[rS], writes=[ryo])
                          dma("sp", bounce[L].ap()[G.idx][:, 0:n].rearrange("(c p) n -> p c n", p=128), ogT[:, 0:4, 0:n],
                              reads=[rogT], writes=[rbnc[L][G.idx]])
                          group_coll(L, G)
                          if gen is not None:
                              for _ in gen:
                                  pass
                  ck('endlayer L%d' % L)
                  kb.barrier()

        except StopBuild:
            kb.barrier()
            return nc
        with contextlib.ExitStack() as ls:
            kb.barrier()
            Wo, rWo = load_wout(3, ls)
            GF = [kb.sb("GF%d" % i, [128, 512], F32, ls) for i in range(2)]; rGF = rGt
            tcnt = [0]
            xtf = [kb.sb("xtf%d" % i, [128, 1024], F32, ls) for i in range(4)]; rxtf = [R() for _ in range(4)]
            ssf = [kb.sb("ssf%d" % i, [128, 4], F32, ls) for i in range(4)]; rssf = [R() for _ in range(4)]
            ogaf = [kb.sb("ogaf%d" % i, [128, 4, 128], BF16, ls) for i in range(4)]; rogaf = [R() for _ in range(4)]
            bcast_row(GF[0], ngf[:, 0:512], 512)
            bcast_row(GF[1], ngf[:, 512:1024], 512)
            final_pass(Wo, rWo)
            kb.barrier()
    return nc


_CACHE = {}


def host_consts(h, SEQ, PAST):
    NP = 16 + SEQ
    NT = NP + 128
    cst = np.zeros((128, 776), np.float32)
    idx = np.arange(128)
    cst[:, 0:128] = np.eye(128, dtype=np.float32)
    cst[:, 128:256] = -(idx[:, None] >= idx[None, :]).astype(np.float32)
    cst[:, 256:384] = -1.0
    cst[:, 384:512] = (idx[:, None] < idx[None, :]).astype(np.float32)
    lg = np.log(np.float32(1.0) - np.float32(2.0) ** np.float32(-5.0 - h)).astype(np.float32)
    rel = (idx[None, :] - idx[:, None]).astype(np.float32)
    cst[:, 512:640] = np.where(rel >= 0, np.exp(np.maximum(rel, 0.0) * lg), 0.0).astype(np.float32)
    cst[:, 640:768] = np.exp((idx.astype(np.float32) + 1.0) * lg)[None, :]
    for ci, n in enumerate((128, 16, 32)):
        col = np.where(idx < n, np.exp((n - 1.0 - idx.astype(np.float32)) * lg), 0.0)
        cst[:, 768 + ci] = col
        cst[:, 771 + ci] = np.exp(np.float32(n) * lg)
    pos = np.concatenate([np.arange(NP, dtype=np.float32) - 16.0,
                          np.float32(PAST) + (np.arange(128) % 32).astype(np.float32)]).astype(np.float32)
    inv = (np.float32(10000.0) ** (-np.linspace(0.0, 1.0, 128, dtype=np.float32))).astype(np.float32)
    ang = (inv[:, None] * pos[None, :]).astype(np.float32)
    return cst, np.cos(ang).astype(np.float32), np.sin(ang).astype(np.float32)


def make_in_maps(inp, SEQ, PAST):
    f = lambda a: np.ascontiguousarray(np.asarray(a, dtype=np.float32))
    w_in = [inp["w_in_0"], inp["w_in_1"], inp["w_in_2"], inp["w_in_3"]]
    w_out = [inp["w_out_0"], inp["w_out_1"], inp["w_out_2"], inp["w_out_3"]]
    norm_g = [inp["norm_g_0"], inp["norm_g_1"], inp["norm_g_2"], inp["norm_g_3"]]
    lam = {0: (inp["lam_q1_0"], inp["lam_k1_0"], inp["lam_q2_0"], inp["lam_k2_0"]),
           3: (inp["lam_q1_3"], inp["lam_k1_3"], inp["lam_q2_3"], inp["lam_k2_3"])}
    subln = {0: inp["subln_g_0"], 3: inp["subln_g_3"]}
    cache_k = {0: inp["cache_k_l0"], 2: inp["cache_k_l2"], 3: inp["cache_k_l3"]}
    cache_v = {0: inp["cache_v_l0"], 2: inp["cache_v_l2"], 3: inp["cache_v_l3"]}
    in_maps = []
    for c in range(8):
        b, h = c // 4, c % 4
        m = {}
        m["xin"] = f(np.concatenate([inp["meta_tokens"], inp["x_prompt"][b],
                                     np.asarray(inp["x_sample"][4 * b:4 * b + 4]).reshape(128, 1024)], 0))
        for L in range(4):
            w = np.asarray(w_in[L])
            if L == 1:
                cols = [w[:, h * 256:(h + 1) * 256], w[:, 1024 + h * 256:1024 + (h + 1) * 256],
                        w[:, 2048 + h * 512:2048 + (h + 1) * 512], w[:, 4096 + h * 512:4096 + (h + 1) * 512]]
            else:
                cols = [w[:, i * 512 + h * 128:i * 512 + (h + 1) * 128] for i in range(4)]
            m["win%d" % L] = f(np.concatenate(cols, 1))
            m["ng%d" % L] = f(np.asarray(norm_g[L]).reshape(8, 128).T)
            m["wout%d" % L] = f(w_out[L])
        for L in (0, 3):
            m["lam%d" % L] = f(np.concatenate([np.asarray(a) for a in lam[L]]).reshape(1, 256))
            m["subg%d" % L] = f(np.asarray(subln[L]).reshape(1, 128))
            ck = np.asarray(cache_k[L])[4 * b:4 * b + 4, :, 2 * h:2 * h + 2, :]
            m["ck%d" % L] = f(ck.transpose(2, 3, 0, 1).reshape(128, 4 * PAST))
            m["cv%d" % L] = f(np.asarray(cache_v[L])[4 * b:4 * b + 4, :, h, :])
        ck = np.asarray(cache_k[2])[4 * b:4 * b + 4, :, h, :]
        m["ck2"] = f(ck.transpose(2, 0, 1).reshape(128, 4 * PAST))
        m["cv2"] = f(np.asarray(cache_v[2])[4 * b:4 * b + 4, :, h, :])
        m["gn1"] = f(np.asarray(inp["gn_g_1"])[h].reshape(1, 512))
        m["ngf"] = f(np.asarray(inp["norm_g_final"]).reshape(1, 1024))
        m["st1"] = f(np.asarray(inp["state_ret_l1"])[4 * b:4 * b + 4, h])
        cst, cs, sn = host_consts(h, SEQ, PAST)
        m["cst"] = cst; m["cos"] = cs; m["sin"] = sn
        in_maps.append(m)
    return in_maps


def assemble(res, SEQ, PAST):
    NP = 16 + SEQ
    y_p = np.zeros((2, SEQ, 1024), np.float32)
    y_s = np.zeros((8, 32, 1024), np.float32)
    kv = {}
    for L in (0, 2, 3):
        nm, dh = (8, 64) if L != 2 else (4, 128)
        kv[L] = [np.zeros((2, NP, nm, dh), np.float32), np.zeros((2, NP, 4, 128), np.float32),
                 np.zeros((8, 32, nm, dh), np.float32), np.zeros((8, 32, 4, 128), np.float32)]
    ret_p = np.zeros((2, 4, 256, 512), np.float32)
    ret_s = np.zeros((8, 4, 256, 512), np.float32)
    for c in range(8):
        b, h = c // 4, c % 4
        r = res[c]
        if h == 0:
            y = np.asarray(r["y"])
            y_p[b] = y[16:NP]
            y_s[4 * b:4 * b + 4] = y[NP:].reshape(4, 32, 1024)
        for L in (0, 2, 3):
            kT = np.asarray(r["kT%d" % L]); v = np.asarray(r["v%d" % L])
            if L != 2:
                kk = kT.reshape(2, 64, -1).transpose(2, 0, 1)
                kv[L][0][b, :, 2 * h:2 * h + 2, :] = kk[:NP]
                kv[L][2][4 * b:4 * b + 4, :, 2 * h:2 * h + 2, :] = kk[NP:].reshape(4, 32, 2, 64)
            else:
                kk = kT.T
                kv[L][0][b, :, h, :] = kk[:NP]
                kv[L][2][4 * b:4 * b + 4, :, h, :] = kk[NP:].reshape(4, 32, 128)
            kv[L][1][b, :, h, :] = v[:NP]
            kv[L][3][4 * b:4 * b + 4, :, h, :] = v[NP:].reshape(4, 32, 128)
        ret_p[b, h] = np.asarray(r["retp"])
        ret_s[4 * b:4 * b + 4, h] = np.asarray(r["rets"])
    return (y_p, y_s, kv[0][0], kv[0][1], kv[0][2], kv[0][3], ret_p, ret_s,
            kv[2][0], kv[2][1], kv[2][2], kv[2][3], kv[3][0], kv[3][1], kv[3][2], kv[3][3])


def kernel(**inputs):
    seq = int(np.asarray(inputs["x_prompt"]).shape[1])
    past = int(np.asarray(inputs["cache_k_l0"]).shape[1])
    key = (seq, past)
    if key not in _CACHE:
        _CACHE[key] = build(seq, past)
    nc = _CACHE[key]
    in_maps = make_in_maps(inputs, seq, past)
    res = run_bass_kernel_spmd(nc, in_maps, core_ids=list(range(8)))
    return assemble(res.results, seq, past)
```

```python
import contextlib
import math
import numpy as np
import concourse.bass as bass
import concourse.mybir as mybir
from concourse.bass_utils import run_bass_kernel_spmd

F32 = mybir.dt.float32
BF16 = mybir.dt.bfloat16
AF = mybir.ActivationFunctionType
ALU = mybir.AluOpType

SEQ = 16384
PAST = 4096
EPS = 1e-6
STOP = 0


class R:
    __slots__ = ("w", "rd")

    def __init__(self):
        self.w = None
        self.rd = {}


class KB:
    EPOCH = 30000
    NDMA = 40

    def __init__(self, nc):
        self.nc = nc
        self.es = contextlib.ExitStack()
        self.h = {"pe": nc.tensor, "act": nc.scalar, "dve": nc.vector, "pool": nc.gpsimd, "sp": nc.sync}
        self.sem = {}
        self.cnt = {}
        self.seen = {e: {} for e in self.h}
        self.nsem = 0
        for e in self.h:
            self._new_epoch(e)
        self.dsem = [self._mksem("d%d" % i) for i in range(self.NDMA)]
        self.duse = [0] * self.NDMA
        self.dnext = 0
        self.csems = []
        self.ninst = 0
        self.stopped = False
        KBREF[0] = self

    def _mksem(self, name):
        self.nsem += 1
        return self.es.enter_context(self.nc.semaphore("s%s_%d" % (name, self.nsem)))

    def _new_epoch(self, e):
        self.sem[e] = self._mksem(e)
        self.cnt[e] = 0

    def sb(self, name, shape, dt, es=None):
        return (es or self.es).enter_context(self.nc.sbuf_tensor("s_" + name, list(shape), dt))

    def ps(self, name, shape, dt):
        return self.es.enter_context(self.nc.psum_tensor("p_" + name, list(shape), dt))

    def _deps(self, eng, reads, writes, is_dma):
        need = {}

        def add(t, war=False):
            if t is None:
                return
            s, v, te = t
            if (not is_dma) and te == eng and (war or eng == "pe"):
                return
            k = id(s)
            if k not in need or need[k][1] < v:
                need[k] = (s, v)

        for r in reads:
            add(r.w)
        for r in writes:
            add(r.w)
            for t in r.rd.values():
                add(t, True)
        seen = self.seen[eng]
        for k, (s, v) in need.items():
            if seen.get(k, 0) < v:
                self.h[eng].wait_ge(s, v)
                seen[k] = v

    def op(self, eng, fn, reads=(), writes=()):
        if self.stopped:
            return None
        self._deps(eng, reads, writes, False)
        if self.cnt[eng] >= self.EPOCH:
            self._new_epoch(eng)
        inst = fn(self.h[eng])
        self.cnt[eng] += 1
        self.ninst += 1
        inst.then_inc(self.sem[eng], 1)
        tok = (self.sem[eng], self.cnt[eng], eng)
        for r in writes:
            r.w = tok
            r.rd = {}
        for r in reads:
            r.rd[eng] = tok
        return inst

    def _dtok(self, q):
        i = self.dnext
        self.dnext = (self.dnext + 1) % self.NDMA
        s = self.dsem[i]
        pv = 16 * self.duse[i]
        if pv and self.seen[q].get(id(s), 0) < pv:
            self.h[q].wait_ge(s, pv)
            self.seen[q][id(s)] = pv
        self.duse[i] += 1
        return i, s, 16 * self.duse[i]

    def dma(self, q, out, in_, reads=(), writes=()):
        if self.stopped:
            return None
        self._deps(q, reads, writes, True)
        i, s, v = self._dtok(q)
        inst = self.h[q].dma_start(out=out, in_=in_)
        inst.then_inc(s, 16)
        self.ninst += 1
        tok = (s, v, None)
        for r in writes:
            r.w = tok
            r.rd = {}
        for r in reads:
            r.rd[("d", i)] = tok
        return inst

    def coll(self, kind, ins, outs, groups, reads=(), writes=()):
        if self.stopped:
            return None
        q = "pool"
        self._deps(q, reads, writes, True)
        if not self.csems:
            self.csems.append(self._mksem("cc"))
            self.ccnt = 0
        s = self.csems[0]
        inst = self.nc.gpsimd.collective_compute(kind, ALU.bypass, replica_groups=groups, ins=ins, outs=outs)
        inst.then_inc(s, 1)
        self.ccnt += 1
        tok = (s, self.ccnt, None)
        for r in writes:
            r.w = tok
            r.rd = {}
        for r in reads:
            r.rd[("c", 0)] = tok

    def barrier(self):
        if self.stopped:
            return
        toks = [(self.sem[e], self.cnt[e]) for e in ("pe", "act", "dve", "pool") if self.cnt[e]]
        toks += [(s, 16 * self.duse[i]) for i, s in enumerate(self.dsem) if self.duse[i]]
        toks += [(s, self.ccnt) for s in self.csems]
        for e in self.h:
            for s, v in toks:
                if s is self.sem[e] and e != "sp":
                    continue
                if self.seen[e].get(id(s), 0) < v:
                    self.h[e].wait_ge(s, v)
                    self.seen[e][id(s)] = v


class StopBuild(Exception):
    pass


CKN = [0]
KBREF = [None]


def ck(tag):
    CKN[0] += 1
    if STOP >= 10 and CKN[0] == STOP - 9:
        print('STOP at', tag)
        KBREF[0].barrier()
        KBREF[0].stopped = True


class Grp:
    pass


class KT_:
    pass


def build(SEQ, PAST):
    NF = SEQ // 128
    NGF = NF // 4
    NP = 16 + SEQ
    NT = NP + 128
    NCB = PAST // 128
    NVT = max(NF + 1, 4 * NCB + 4)
    assert 4 * PAST <= NP
    NCH = [4, 16, 4, 4]
    WIN_COLS = [512, 1536, 512, 512]
    LAM_INIT = {0: 0.8 - 0.6 * math.exp(0.0), 3: 0.8 - 0.6 * math.exp(-0.9)}

    nc = bass.Bass("TRN2", target_bir_lowering=False, dynamic_dma_scratch_size=4096)

    def din(name, shape, dt=F32):
        return nc.dram_tensor(name, list(shape), dt, kind="ExternalInput").ap()

    def dout(name, shape, dt=F32):
        return nc.dram_tensor(name, list(shape), dt, kind="ExternalOutput").ap()

    xin = din("xin", [NT, 1024])
    win = [din("win%d" % L, [1024, WIN_COLS[L]]) for L in range(4)]
    ngin = [din("ng%d" % L, [128, 8]) for L in range(4)]
    wout = [din("wout%d" % L, [NCH[L] * 128, 1024]) for L in range(4)]
    lamin = {L: din("lam%d" % L, [1, 256]) for L in (0, 3)}
    subg = {L: din("subg%d" % L, [1, 128]) for L in (0, 3)}
    gn1 = din("gn1", [1, 512])
    ngf = din("ngf", [1, 1024])
    ckin = {L: din("ck%d" % L, [128, 4 * PAST]) for L in (0, 2, 3)}
    cvin = {L: din("cv%d" % L, [4, PAST, 128]) for L in (0, 2, 3)}
    st1 = din("st1", [4, 256, 512])
    cstin = din("cst", [128, 776])
    cosin = din("cos", [128, NT])
    sinin = din("sin", [128, NT])

    yo = dout("y", [NT, 1024])
    kTo = {L: dout("kT%d" % L, [128, NT]) for L in (0, 2, 3)}
    vo = {L: dout("v%d" % L, [NT, 128]) for L in (0, 2, 3)}
    retp = dout("retp", [256, 512])
    rets = dout("rets", [4, 256, 512])

    xs = nc.dram_tensor("xs", [NT, 1024], F32).ap()
    NGRP = NGF + 2
    bounce = [nc.dram_tensor("bnc%d" % L, [NGRP, NCH[L] // 4 * 128, 512], BF16) for L in range(4)]
    gath = [nc.dram_tensor("gth%d" % L, [NGRP, NCH[L] * 128, 512], BF16) for L in range(4)]

    kb = KB(nc)
    op = kb.op
    dma = kb.dma

    groups = []
    g = Grp(); g.idx = 0; g.c0 = 0; g.tiles = [(0, 16)]; g.kind = "meta"; g.vt0 = 0; groups.append(g)
    for i in range(NGF):
        g = Grp(); g.idx = 1 + i; g.c0 = 16 + 512 * i; g.tiles = [(128 * t, 128) for t in range(4)]
        g.kind = "frame"; g.vt0 = 1 + 4 * i; groups.append(g)
    g = Grp(); g.idx = 1 + NGF; g.c0 = NP; g.tiles = [(32 * t, 32) for t in range(4)]; g.kind = "sample"
    g.vt0 = 4 * NCB; groups.append(g)
    for g in groups:
        g.ncols = sum(n for _, n in g.tiles)

    with kb.es:
        cst = kb.sb("cst", [128, 776], F32); rcst = R()
        identb = kb.sb("identb", [128, 128], BF16)
        negUb = kb.sb("negUb", [128, 128], BF16)
        negOb = kb.sb("negOb", [128, 128], BF16)
        onesf = kb.sb("onesf", [128, 128], F32)
        rcb = R()
        epst = kb.sb("epst", [128, 1], F32)
        onet = kb.sb("onet", [128, 1], F32)
        xt = [kb.sb("xt%d" % i, [128, 1024], F32) for i in range(2)]; rxt = [R(), R()]
        sqj = kb.sb("sqj", [128, 1024], BF16); rsqj = R()
        xh = kb.sb("xh", [128, 1024], BF16); rxh = R()
        ss = [kb.sb("ss%d" % i, [128, 4], F32) for i in range(2)]; rss = [R(), R()]
        xnT = kb.sb("xnT", [128, 8, 512], BF16); rxnT = R()
        qT = kb.sb("qT", [128, 2, 512], BF16); rqT = R(); rqTp = [R(), R()]; rsgp = [R(), R()]
        kst = kb.sb("kst", [128, 512], F32); rkst = R()
        vst = [kb.sb("vst%d" % i, [128, 128], F32) for i in range(2)]; rvst = [R(), R()]
        ogT = kb.sb("ogT", [128, 4, 512], BF16); rogT = R()
        oga = [kb.sb("oga%d" % i, [128, 16, 128], BF16) for i in range(2)]; roga = [R(), R()]
        TF = [kb.sb("tf%d" % i, [128, 512], F32) for i in range(6)]; rTF = [R() for _ in range(6)]
        TF6 = kb.sb("tf6", [128, 512], F32); rTF6 = R()
        TH = [kb.sb("th%d" % i, [128, 512], BF16) for i in range(5)]; rTH = [R() for _ in range(5)]
        wst = [kb.sb("wst%d" % i, [128, 1024], F32) for i in range(2)]; rwst = [R(), R()]
        sg = kb.sb("sg", [128, 4, 512], F32); rsg = R()
        rd = kb.sb("rd", [128, 8], F32); rrd = R()
        lamt = kb.sb("lamt", [128, 8], F32); rlam = R()
        lrow = kb.sb("lrow", [1, 264], F32); rlrow = R()
        Gt = kb.sb("Gt", [128, 512], F32); rGt = R()
        grow = wst[0]; rgrow = rwst[0]
        B = [kb.ps("pb%d" % i, [128, 512], F32) for i in range(7)]; rB = [R() for _ in range(7)]
        TRb = kb.ps("trb", [128, 8, 128], BF16); rTR = R()
        rxs = [R() for _ in groups]
        rgath = [[R() for _ in groups] for _ in range(4)]
        rbnc = [[R() for _ in groups] for _ in range(4)]
        rko = R(); rvo = R(); ryo = R()

        ident = cst[:, 0:128]
        Mstr = cst[:, 384:512]
        decT = cst[:, 512:640]
        dq = cst[:, 640:768]

        def ncol(n):
            return {128: 0, 16: 1, 32: 2}[n]

        dma("sp", cst[:], cstin, writes=[rcst])
        op("dve", lambda h: h.tensor_copy(out=identb[:], in_=cst[:, 0:128]), reads=[rcst], writes=[rcb])
        op("dve", lambda h: h.tensor_copy(out=negUb[:], in_=cst[:, 128:256]), reads=[rcst], writes=[rcb])
        op("dve", lambda h: h.tensor_copy(out=negOb[:], in_=cst[:, 256:384]), reads=[rcst], writes=[rcb])
        op("dve", lambda h: h.memset(onesf[:], 1.0), writes=[rcb])
        op("dve", lambda h: h.memset(epst[:], EPS), writes=[rcb])
        op("dve", lambda h: h.memset(onet[:], 1.0), writes=[rcb])

        def bcast_row(dst, row_ap, n, scale=1.0):
            dma("sp", grow[0:1, 0:n], row_ap, writes=[rgrow])
            for c in range(0, n, 512):
                w = min(512, n - c)
                op("pe", lambda h: h.matmul(out=B[0][:, 0:w], lhsT=onesf[0:1, 0:128], rhs=grow[0:1, c:c + w],
                                            start=True, stop=True), reads=[rgrow, rcb], writes=[rB[0]])
                op("act", lambda h: h.activation(out=dst[:, c:c + w], in_=B[0][:, 0:w], func=AF.Copy, scale=scale),
                   reads=[rB[0]], writes=[rGt])

        def load_weights(L, ls):
            W = kb.sb("W%d" % L, [128, 8, WIN_COLS[L]], BF16, ls); rW = R()
            ngt = kb.sb("ngt%d" % L, [128, 8], F32, ls); rng = R()
            dma("sp", ngt[:], ngin[L], writes=[rng])
            wc = WIN_COLS[L]
            i = 0
            for kc in range(8):
                for c0 in range(0, wc, 1024):
                    w = min(1024, wc - c0)
                    w_ = wst[i % 2]; rw_ = rwst[i % 2]; i += 1
                    dma("sp", w_[:, 0:w], win[L][kc * 128:(kc + 1) * 128, c0:c0 + w], writes=[rw_])
                    op("dve", lambda h: h.tensor_scalar(out=W[:, kc, c0:c0 + w], in0=w_[:, 0:w], scalar1=ngt[:, kc:kc + 1],
                                                        scalar2=None, op0=ALU.mult), reads=[rw_, rng], writes=[rW])
            return W, rW

        def load_wout(L, ls):
            n = NCH[L]
            Wo = kb.sb("Wo%d" % L, [128, n, 1024], BF16, ls); rWo = R()
            for cc in range(n):
                w_ = wst[cc % 2]; rw_ = rwst[cc % 2]
                dma("sp", w_[:, 0:1024], wout[L][cc * 128:(cc + 1) * 128, :], writes=[rw_])
                op("pool", lambda h: h.tensor_copy(out=Wo[:, cc, :], in_=w_[:, 0:1024]), reads=[rw_], writes=[rWo])
            return Wo, rWo

        def compute_lam(L):
            dma("sp", lrow[0:1, 0:256], lamin[L], writes=[rlrow])
            for i in range(2):
                op("dve", lambda h: h.tensor_tensor(out=lrow[0:1, 128 * i:128 * i + 64], in0=lrow[0:1, 128 * i:128 * i + 64],
                                                    in1=lrow[0:1, 128 * i + 64:128 * i + 128], op=ALU.mult),
                   reads=[rlrow], writes=[rlrow])
                op("dve", lambda h: h.tensor_reduce(out=lrow[0:1, 256 + i:257 + i], in_=lrow[0:1, 128 * i:128 * i + 64],
                                                    axis=mybir.AxisListType.X, op=ALU.add), reads=[rlrow], writes=[rlrow])
            op("act", lambda h: h.activation(out=lrow[0:1, 258:260], in_=lrow[0:1, 256:258], func=AF.Exp),
               reads=[rlrow], writes=[rlrow])
            op("dve", lambda h: h.tensor_tensor(out=lrow[0:1, 260:261], in0=lrow[0:1, 258:259], in1=lrow[0:1, 259:260],
                                                op=ALU.subtract), reads=[rlrow], writes=[rlrow])
            op("dve", lambda h: h.tensor_scalar(out=lrow[0:1, 261:262], in0=lrow[0:1, 260:261], scalar1=-1.0,
                                                scalar2=-LAM_INIT[L], op0=ALU.mult, op1=ALU.add),
               reads=[rlrow], writes=[rlrow])
            op("pe", lambda h: h.matmul(out=B[0][:, 0:2], lhsT=onesf[0:1, 0:128], rhs=lrow[0:1, 260:262],
                                        start=True, stop=True), reads=[rlrow, rcb], writes=[rB[0]])
            op("act", lambda h: h.activation(out=lamt[:, 0:2], in_=B[0][:, 0:2], func=AF.Copy), reads=[rB[0]], writes=[rlam])

        def group_coll(L, G):
            kb.coll("AllGather", [bounce[L].ap()[G.idx].opt()], [gath[L].ap()[G.idx].opt()], [[0, 1, 2, 3], [4, 5, 6, 7]],
                    reads=[rbnc[L][G.idx]], writes=[rgath[L][G.idx]])

        def phase_a(L, G, Wo, rWo, ob=(0, 1)):
            src = xin if L <= 1 else xs
            n = NCH[L - 1] if L >= 1 else 0

            def loads(ti):
                ofs, nr = G.tiles[ti]
                tok0 = G.c0 + ofs
                dma("sp", xt[ti % 2][0:nr, :], src[tok0:tok0 + nr, :], reads=[rxs[G.idx]] if L >= 2 else [], writes=[rxt[ti % 2]])
                if L >= 1:
                    dma("sp", oga[ti % 2][:, 0:n, 0:nr], gath[L - 1].ap()[G.idx][:, ofs:ofs + nr].rearrange("(c p) n -> p c n", p=128),
                        reads=[rgath[L - 1][G.idx]], writes=[roga[ti % 2]])

            loads(0)
            for ti, (ofs, nr) in enumerate(G.tiles):
                tok0 = G.c0 + ofs
                x_ = xt[ti % 2]; rx_ = rxt[ti % 2]
                s_ = ss[ti % 2]; rs_ = rss[ti % 2]
                if ti + 1 < len(G.tiles):
                    loads(ti + 1)
                if L >= 1:
                    og_ = oga[ti % 2]; rog_ = roga[ti % 2]
                    for hf in range(2):
                        bk_ = ob[hf]
                        for cc in range(n):
                            op("pe", lambda h: h.matmul(out=B[bk_][0:nr, :], lhsT=og_[:, cc, 0:nr],
                                                        rhs=Wo[:, cc, hf * 512:(hf + 1) * 512], start=(cc == 0),
                                                        stop=(cc == n - 1)), reads=[rog_, rWo], writes=[rB[bk_]])
                        op("dve", lambda h: h.tensor_tensor(out=x_[0:nr, hf * 512:(hf + 1) * 512], in0=B[bk_][0:nr, :],
                                                            in1=x_[0:nr, hf * 512:(hf + 1) * 512], op=ALU.add),
                           reads=[rB[bk_], rx_], writes=[rx_])
                        yield
                    dma("sp", xs[tok0:tok0 + nr, :], x_[0:nr, :], reads=[rx_], writes=[rxs[G.idx]])
                else:
                    yield
                op("act", lambda h: h.activation(out=sqj[0:nr, :], in_=x_[0:nr, :], func=AF.Square,
                                                 accum_out=s_[0:nr, 0:1]), reads=[rx_], writes=[rsqj, rs_])
                op("act", lambda h: h.activation(out=s_[0:nr, 1:2], in_=s_[0:nr, 0:1], func=AF.Sqrt, scale=1.0 / 1024,
                                                 bias=epst[0:nr, 0:1]), reads=[rs_, rcb], writes=[rs_])
                op("dve", lambda h: h.reciprocal(out=s_[0:nr, 2:3], in_=s_[0:nr, 1:2]), reads=[rs_], writes=[rs_])
                op("dve", lambda h: h.tensor_scalar(out=xh[0:nr, :], in0=x_[0:nr, :], scalar1=s_[0:nr, 2:3],
                                                    scalar2=None, op0=ALU.mult), reads=[rx_, rs_], writes=[rxh])
                yield
                for kc in range(8):
                    op("pe", lambda h: h.transpose(out=TRb[:, kc, 0:nr], in_=xh[0:nr, kc * 128:(kc + 1) * 128],
                                                   identity=identb[0:nr, 0:nr]), reads=[rxh, rcb], writes=[rTR])
                op("act", lambda h: h.activation(out=xnT[:, :, ofs:ofs + nr], in_=TRb[:, :, 0:nr], func=AF.Copy),
                   reads=[rTR], writes=[rxnT])
                yield

        def final_pass(Wo, rWo):
            n = NCH[3]
            tl = [(G, ofs, nr) for G in groups for (ofs, nr) in G.tiles]

            def loads(i):
                G, ofs, nr = tl[i]
                tok0 = G.c0 + ofs
                dma("sp", xtf[i % 4][0:nr, :], xs[tok0:tok0 + nr, :], reads=[rxs[G.idx]], writes=[rxtf[i % 4]])
                dma("sp", ogaf[i % 4][:, 0:n, 0:nr], gath[3].ap()[G.idx][:, ofs:ofs + nr].rearrange("(c p) n -> p c n", p=128),
                    reads=[rgath[3][G.idx]], writes=[rogaf[i % 4]])

            loads(0)
            loads(1)
            for i, (G, ofs, nr) in enumerate(tl):
                tok0 = G.c0 + ofs
                if i + 2 < len(tl):
                    loads(i + 2)
                x_ = xtf[i % 4]; rx_ = rxtf[i % 4]; s_ = ssf[i % 4]; rs_ = rssf[i % 4]
                og_ = ogaf[i % 4]; rog_ = rogaf[i % 4]
                ob = ((0, 1), (2, 3), (4, 5))[i % 3]
                for hf in range(2):
                    bk_ = ob[hf]
                    for cc in range(n):
                        op("pe", lambda h: h.matmul(out=B[bk_][0:nr, :], lhsT=og_[:, cc, 0:nr],
                                                    rhs=Wo[:, cc, hf * 512:(hf + 1) * 512], start=(cc == 0),
                                                    stop=(cc == n - 1)), reads=[rog_, rWo], writes=[rB[bk_]])
                    op("dve", lambda h: h.tensor_tensor(out=x_[0:nr, hf * 512:(hf + 1) * 512], in0=B[bk_][0:nr, :],
                                                        in1=x_[0:nr, hf * 512:(hf + 1) * 512], op=ALU.add),
                       reads=[rB[bk_], rx_], writes=[rx_])
                op("act", lambda h: h.activation(out=sqj[0:nr, :], in_=x_[0:nr, :], func=AF.Square,
                                                 accum_out=s_[0:nr, 0:1]), reads=[rx_], writes=[rsqj, rs_])
                op("act", lambda h: h.activation(out=s_[0:nr, 1:2], in_=s_[0:nr, 0:1], func=AF.Sqrt, scale=1.0 / 1024,
                                                 bias=epst[0:nr, 0:1]), reads=[rs_, rcb], writes=[rs_])
                op("dve", lambda h: h.reciprocal(out=s_[0:nr, 2:3], in_=s_[0:nr, 1:2]), reads=[rs_], writes=[rs_])
                for hf in range(2):
                    eng = "dve" if hf == 0 else "pool"
                    if eng == "dve":
                        op("dve", lambda h: h.scalar_tensor_tensor(out=x_[0:nr, 0:512], in0=x_[0:nr, 0:512],
                                                                   scalar=s_[0:nr, 2:3], in1=GF[0][0:nr, :],
                                                                   op0=ALU.mult, op1=ALU.mult),
                           reads=[rx_, rs_, rGF], writes=[rx_])
                    else:
                        op("dve", lambda h: h.scalar_tensor_tensor(out=x_[0:nr, 512:1024], in0=x_[0:nr, 512:1024],
                                                                   scalar=s_[0:nr, 2:3], in1=GF[1][0:nr, :],
                                                                   op0=ALU.mult, op1=ALU.mult),
                           reads=[rx_, rs_, rGF], writes=[rx_])
                dma("sp", yo[tok0:tok0 + nr, :], x_[0:nr, :], reads=[rx_], writes=[ryo])

        def proj_fm(W, rW, wc0, bank, ncols):
            for kc in range(8):
                op("pe", lambda h: h.matmul(out=B[bank][:, 0:ncols], lhsT=W[:, kc, wc0:wc0 + 128],
                                            rhs=xnT[:, kc, 0:ncols], start=(kc == 0), stop=(kc == 7)),
                   reads=[rW, rxnT], writes=[rB[bank]])

        def proj_tm(W, rW, wc0, wn, bank, ofs, nr):
            for kc in range(8):
                op("pe", lambda h: h.matmul(out=B[bank][0:nr, 0:wn], lhsT=xnT[:, kc, ofs:ofs + nr],
                                            rhs=W[:, kc, wc0:wc0 + wn], start=(kc == 0), stop=(kc == 7)),
                   reads=[rW, rxnT], writes=[rB[bank]])

        def proj_attn(L, G, W, rW, KT, rKT, V, rV, qscale, is_da, bk):
            n = G.ncols
            par = G.idx % 2
            proj_fm(W, rW, 0, bk, n)
            op("act", lambda h: h.activation(out=qT[:, par, 0:n], in_=B[bk][:, 0:n], func=AF.Copy, scale=qscale),
               reads=[rB[bk]], writes=[rqTp[par]])
            yield
            proj_fm(W, rW, 128, bk, n)
            op("act", lambda h: h.activation(out=kst[:, 0:n], in_=B[bk][:, 0:n], func=AF.Copy), reads=[rB[bk]], writes=[rkst])
            dma("sp", kTo[L][:, G.c0:G.c0 + n], kst[:, 0:n], reads=[rkst], writes=[rko])
            op("pool", lambda h: h.tensor_copy(out=KT[:, G.c0:G.c0 + n], in_=kst[:, 0:n]), reads=[rkst], writes=[rKT])
            yield
            for ti, (ofs, nr) in enumerate(G.tiles):
                proj_tm(W, rW, 256, 256, bk, ofs, nr)
                v_ = vst[ti % 2]; rv_ = rvst[ti % 2]
                op("act", lambda h: h.activation(out=v_[0:nr, :], in_=B[bk][0:nr, 0:128], func=AF.Copy),
                   reads=[rB[bk]], writes=[rv_])
                dma("sp", vo[L][G.c0 + ofs:G.c0 + ofs + nr, :], v_[0:nr, :], reads=[rv_], writes=[rvo])
                op("pool", lambda h: h.tensor_copy(out=V[0:nr, G.vt0 + ti, 0:128], in_=v_[0:nr, :]), reads=[rv_], writes=[rV])
                sgv = sg[0:nr, ti, par * 128:(par + 1) * 128]
                op("act", lambda h: h.activation(out=sgv, in_=B[bk][0:nr, 128:256], func=AF.Silu),
                   reads=[rB[bk]], writes=[rsgp[par]])
                if is_da:
                    op("pool", lambda h: h.tensor_tensor(out=sgv, in0=sgv, in1=Gt[0:nr, 0:128], op=ALU.mult),
                       reads=[rsgp[par], rGt], writes=[rsgp[par]])
                yield

        def key_tiles(G, KT, V, for_sb):
            kts = []
            if G.kind == "sample":
                for jj in range(NCB):
                    k = KT_(); k.nk = 128; k.shared = False
                    k.kT = [KT[:, qi * PAST + jj * 128: qi * PAST + (jj + 1) * 128] for qi in range(4)]
                    k.v = [V[:, qi * NCB + jj, :] for qi in range(4)]
                    k.active = [0, 1, 2, 3]; k.diag = []
                    kts.append(k)
                k = KT_(); k.nk = 32; k.shared = False
                k.kT = [KT[:, NP + 32 * qi: NP + 32 * qi + 32] for qi in range(4)]
                k.v = [V[:, 4 * NCB + qi, :] for qi in range(4)]
                k.active = [0, 1, 2, 3]; k.diag = [0, 1, 2, 3] if for_sb else []
                kts.append(k)
                return kts
            k = KT_(); k.nk = 16; k.shared = True; k.kT = [KT[:, 0:16]] * 4; k.v = [V[:, 0, :]] * 4
            k.active = list(range(len(G.tiles)))
            k.diag = [0] if (G.kind == "meta" and for_sb) else []
            kts.append(k)
            if G.kind == "frame":
                i0 = G.vt0
                for j in range(1, i0 + 4):
                    k = KT_(); k.nk = 128; k.shared = True
                    c = 16 + 128 * (j - 1)
                    k.kT = [KT[:, c:c + 128]] * 4; k.v = [V[:, j, :]] * 4
                    k.active = [qi for qi in range(4) if i0 + qi >= j]
                    k.diag = [j - i0] if j >= i0 else []
                    kts.append(k)
            return kts

        def da_b(L, G, KT, rKT, V, rV, gen, nch):
            kts = key_tiles(G, KT, V, False)
            par = G.idx % 2
            rq_ = rqTp[par]; rsg_ = rsgp[par]

            def acc(a, nr):
                return B[2 + a // 3][0:nr, (a % 3) * 160:(a % 3) * 160 + 129], rB[2 + a // 3]

            for b_ in (2, 3, 4):
                op("dve", lambda h: h.memset(B[b_][:], 0.0), writes=[rB[b_]])
            units = [(m, k) for k in kts for m in range(2)]
            SBK = [B[0], B[5], B[6]]; rSBK = [rB[0], rB[5], rB[6]]
            PTS = [TH[0], TH[1], TH[2], TH[4]]; rPTS = [rTH[0], rTH[1], rTH[2], rTH[4]]

            def stS(u):
                m, k = units[u]; p0 = 64 * m; nk = k.nk
                cA = G.tiles[k.active[0]][0]; cE = G.ncols
                sb_ = SBK[u % 3]; rs_ = rSBK[u % 3]
                if k.shared:
                    op("pe", lambda h: h.matmul(out=sb_[0:nk, cA:cE], lhsT=k.kT[0][p0:p0 + 64, :],
                                                rhs=qT[p0:p0 + 64, par, cA:cE], start=True, stop=True),
                       reads=[rKT, rq_], writes=[rs_])
                else:
                    for qi in k.active:
                        ofs, nr = G.tiles[qi]
                        op("pe", lambda h: h.matmul(out=sb_[0:nk, ofs:ofs + nr], lhsT=k.kT[qi][p0:p0 + 64, :],
                                                    rhs=qT[p0:p0 + 64, par, ofs:ofs + nr], start=True, stop=True),
                           reads=[rKT, rq_], writes=[rs_])

            def stE(u):
                m, k = units[u]; nk = k.nk
                cA = G.tiles[k.active[0]][0]; cE = G.ncols
                sb_ = SBK[u % 3]; rs_ = rSBK[u % 3]
                pt = PTS[u % 4]; rpt = rPTS[u % 4]
                op("act", lambda h: h.activation(out=pt[0:nk, cA:cE], in_=sb_[0:nk, cA:cE], func=AF.Exp),
                   reads=[rs_], writes=[rpt])
                for qi in k.diag:
                    ofs, nr = G.tiles[qi]
                    op("pool", lambda h: h.memset(pt[64:128, ofs:ofs + 64], 0.0), writes=[rpt])

            def stP(u):
                m, k = units[u]; nk = k.nk
                pt = PTS[u % 4]; rpt = rPTS[u % 4]
                for qi in k.active:
                    ofs, nr = G.tiles[qi]
                    oa, roa = acc(m * 4 + qi, nr)
                    op("pe", lambda h: h.matmul(out=oa, lhsT=pt[0:nk, ofs:ofs + nr], rhs=k.v[qi][0:nk, 0:129],
                                                start=False, stop=True, skip_group_check=True),
                       reads=[rpt, rV], writes=[roa])

            nu = len(units)
            npair = nu // 2
            per = -(-nch // max(1, npair))
            for p_ in range(npair + 2):
                for u in (2 * p_ - 2, 2 * p_ - 1):
                    if 0 <= u < nu:
                        stE(u)
                for u in (2 * p_, 2 * p_ + 1):
                    if u < nu:
                        stS(u)
                for u in (2 * p_ - 4, 2 * p_ - 3):
                    if 0 <= u < nu:
                        stP(u)
                if gen is not None:
                    for _ in range(per):
                        next(gen, None)
            if gen is not None:
                for _ in gen:
                    pass
            o1 = TF[0]; dd = TF[1]; og = TH[3]
            for qi, (ofs, nr) in enumerate(G.tiles):
                O0, r0 = acc(qi, nr)
                O1, r1 = acc(4 + qi, nr)
                op("dve", lambda h: h.reciprocal(out=rd[0:nr, 0:1], in_=O0[:, 128:129]), reads=[r0], writes=[rrd])
                op("dve", lambda h: h.reciprocal(out=rd[0:nr, 1:2], in_=O1[:, 128:129]), reads=[r1], writes=[rrd])
                op("dve", lambda h: h.tensor_tensor(out=rd[0:nr, 2:3], in0=rd[0:nr, 1:2], in1=lamt[0:nr, 1:2], op=ALU.mult),
                   reads=[rrd, rlam], writes=[rrd])
                op("dve", lambda h: h.tensor_scalar(out=o1[0:nr, 0:128], in0=O0[:, 0:128], scalar1=rd[0:nr, 0:1],
                                                    scalar2=None, op0=ALU.mult), reads=[r0, rrd], writes=[rTF[0]])
                op("dve", lambda h: h.scalar_tensor_tensor(out=dd[0:nr, 0:128], in0=O1[:, 0:128], scalar=rd[0:nr, 2:3],
                                                           in1=o1[0:nr, 0:128], op0=ALU.mult, op1=ALU.add),
                   reads=[r1, rrd, rTF[0]], writes=[rTF[1]])
                op("act", lambda h: h.activation(out=sqj[0:nr, 0:128], in_=dd[0:nr, 0:128], func=AF.Square,
                                                 accum_out=rd[0:nr, 3:4]), reads=[rTF[1]], writes=[rsqj, rrd])
                op("act", lambda h: h.activation(out=rd[0:nr, 4:5], in_=rd[0:nr, 3:4], func=AF.Sqrt, scale=1.0 / 128,
                                                 bias=epst[0:nr, 0:1]), reads=[rrd, rcb], writes=[rrd])
                op("dve", lambda h: h.reciprocal(out=rd[0:nr, 5:6], in_=rd[0:nr, 4:5]), reads=[rrd], writes=[rrd])
                op("dve", lambda h: h.scalar_tensor_tensor(out=og[0:nr, 0:128], in0=dd[0:nr, 0:128], scalar=rd[0:nr, 5:6],
                                                           in1=sg[0:nr, qi, par * 128:(par + 1) * 128], op0=ALU.mult, op1=ALU.mult),
                   reads=[rTF[1], rrd, rsg_], writes=[rTH[3]])
                op("pe", lambda h: h.transpose(out=TRb[:, 0, 0:nr], in_=og[0:nr, 0:128], identity=identb[0:nr, 0:nr]),
                   reads=[rTH[3], rcb], writes=[rTR])
                op("act", lambda h: h.activation(out=ogT[:, 0, ofs:ofs + nr], in_=TRb[:, 0, 0:nr], func=AF.Copy),
                   reads=[rTR], writes=[rogT])
            dma("sp", bounce[L].ap()[G.idx][0:128, 0:G.ncols], ogT[:, 0, 0:G.ncols], reads=[rogT], writes=[rbnc[L][G.idx]])
            group_coll(L, G)

        def sb_b(L, G, KT, rKT, V, rV, gen, nch):
            kts = key_tiles(G, KT, V, True)[::-1]
            par = G.idx % 2
            rq_ = rqTp[par]; rsg_ = rsgp[par]
            Racc = TF[5]; rR = rTF[5]
            op("dve", lambda h: h.memset(B[6][:], 0.0), writes=[rB[6]])
            op("pool", lambda h: h.memset(Racc[:], 0.0), writes=[rR])
            nu = len(kts)
            EB = [TF[0], TF[1], TF[2], TF[3]]; rEB = [rTF[0], rTF[1], rTF[2], rTF[3]]
            XB = [TF[4], TF6]; rXB = [rTF[4], rTF6]

            def rng(k):
                return G.tiles[k.active[0]][0], G.ncols

            def sA(u):
                k = kts[u]; nk = k.nk; cA, cE = rng(k)
                zb = B[0]; rz = rB[0]
                if k.shared:
                    op("pe", lambda h: h.matmul(out=zb[0:nk, cA:cE], lhsT=k.kT[0], rhs=qT[:, par, cA:cE], start=True, stop=True),
                       reads=[rKT, rq_], writes=[rz])
                else:
                    for qi in k.active:
                        ofs, nr = G.tiles[qi]
                        op("pe", lambda h: h.matmul(out=zb[0:nk, ofs:ofs + nr], lhsT=k.kT[qi], rhs=qT[:, par, ofs:ofs + nr],
                                                    start=True, stop=True), reads=[rKT, rq_], writes=[rz])

            def sB(u):
                k = kts[u]; nk = k.nk; cA, cE = rng(k)
                zb = B[0]; rz = rB[0]
                E_ = EB[u % 4]; rE = rEB[u % 4]
                Lp = TH[u % 2]; rL = rTH[u % 2]
                op("act", lambda h: h.activation(out=E_[0:nk, cA:cE], in_=zb[0:nk, cA:cE], func=AF.Exp), reads=[rz], writes=[rE])
                for qi in k.diag:
                    ofs, nr = G.tiles[qi]
                    op("pool", lambda h: h.tensor_tensor(out=E_[0:nk, ofs:ofs + nr], in0=E_[0:nk, ofs:ofs + nr],
                                                         in1=Mstr[0:nk, 0:nr], op=ALU.mult), reads=[rE, rcst], writes=[rE])

            def sB2(u):
                k = kts[u]; nk = k.nk; cA, cE = rng(k)
                E_ = EB[u % 4]; rE = rEB[u % 4]
                Lp = TH[u % 2]; rL = rTH[u % 2]
                op("act", lambda h: h.activation(out=Lp[0:nk, cA:cE], in_=E_[0:nk, cA:cE], func=AF.Ln, scale=1.0,
                                                 bias=onet[0:nk, 0:1]), reads=[rE, rcb], writes=[rL])

            def sC(u):
                k = kts[u]; nk = k.nk; cA, cE = rng(k)
                Lp = TH[u % 2]; rL = rTH[u % 2]
                tb = B[2 + u % 2]; rt = rB[2 + u % 2]
                cb = B[4 + u % 2]; rc = rB[4 + u % 2]
                op("pe", lambda h: h.matmul(out=tb[0:nk, cA:cE], lhsT=negUb[0:nk, 0:nk], rhs=Lp[0:nk, cA:cE], start=True, stop=True),
                   reads=[rL, rcb], writes=[rt])
                if u != nu - 1:
                    op("pe", lambda h: h.matmul(out=cb[:, cA:cE], lhsT=negOb[0:nk, 0:128], rhs=Lp[0:nk, cA:cE], start=True, stop=True),
                       reads=[rL, rcb], writes=[rc])

            def sD(u):
                k = kts[u]; nk = k.nk; cA, cE = rng(k)
                E_ = EB[u % 4]; rE = rEB[u % 4]
                X_ = XB[u % 2]; rX = rXB[u % 2]
                tb = B[2 + u % 2]; rt = rB[2 + u % 2]
                cb = B[4 + u % 2]; rc = rB[4 + u % 2]
                A_ = TH[2 + u % 2]; rA = rTH[2 + u % 2]
                op("dve", lambda h: h.tensor_tensor(out=X_[0:nk, cA:cE], in0=tb[0:nk, cA:cE], in1=Racc[0:nk, cA:cE], op=ALU.add),
                   reads=[rt, rR], writes=[rX])
                if u != nu - 1:
                    op("dve", lambda h: h.tensor_tensor(out=Racc[:, cA:cE], in0=cb[:, cA:cE], in1=Racc[:, cA:cE], op=ALU.add),
                       reads=[rc, rR], writes=[rR])
                op("act", lambda h: h.activation(out=X_[0:nk, cA:cE], in_=X_[0:nk, cA:cE], func=AF.Exp), reads=[rX], writes=[rX])
                op("pool", lambda h: h.tensor_tensor(out=A_[0:nk, cA:cE], in0=E_[0:nk, cA:cE], in1=X_[0:nk, cA:cE], op=ALU.mult),
                   reads=[rE, rX], writes=[rA])

            def sE(u):
                k = kts[u]; nk = k.nk
                A_ = TH[2 + u % 2]; rA = rTH[2 + u % 2]
                for qi in k.active:
                    ofs, nr = G.tiles[qi]
                    op("pe", lambda h: h.matmul(out=B[6][0:nr, qi * 128:(qi + 1) * 128], lhsT=A_[0:nk, ofs:ofs + nr],
                                                rhs=k.v[qi][0:nk, 0:128], start=False, stop=True, skip_group_check=True),
                       reads=[rA, rV], writes=[rB[6]])

            per = -(-nch // max(1, nu))
            for st in range(nu + 4):
                if 0 <= st - 1 < nu:
                    sB(st - 1)
                if 0 <= st - 2 < nu:
                    sC(st - 2)
                if 0 <= st - 3 < nu:
                    sD(st - 3)
                if 0 <= st - 1 < nu:
                    sB2(st - 1)
                if 0 <= st - 4 < nu:
                    sE(st - 4)
                if st < nu:
                    sA(st)
                if gen is not None:
                    for _ in range(per):
                        next(gen, None)
            if gen is not None:
                for _ in gen:
                    pass
            og = TH[4]
            for qi, (ofs, nr) in enumerate(G.tiles):
                op("dve", lambda h: h.tensor_tensor(out=og[0:nr, 0:128], in0=B[6][0:nr, qi * 128:(qi + 1) * 128],
                                                    in1=sg[0:nr, qi, par * 128:(par + 1) * 128], op=ALU.mult), reads=[rB[6], rsg_], writes=[rTH[4]])
                op("pe", lambda h: h.transpose(out=TRb[:, 0, 0:nr], in_=og[0:nr, 0:128], identity=identb[0:nr, 0:nr]),
                   reads=[rTH[4], rcb], writes=[rTR])
                op("act", lambda h: h.activation(out=ogT[:, 0, ofs:ofs + nr], in_=TRb[:, 0, 0:nr], func=AF.Copy),
                   reads=[rTR], writes=[rogT])
            dma("sp", bounce[L].ap()[G.idx][0:128, 0:G.ncols], ogT[:, 0, 0:G.ncols], reads=[rogT], writes=[rbnc[L][G.idx]])
            group_coll(L, G)

        def load_cache(L, KT, rKT, V, rV):
            for c in range(0, 4 * PAST, 1024):
                w_ = wst[(c // 1024) % 2]; rw_ = rwst[(c // 1024) % 2]
                dma("sp", w_[:, 0:1024], ckin[L][:, c:c + 1024], writes=[rw_])
                op("pool", lambda h: h.tensor_copy(out=KT[:, c:c + 1024], in_=w_[:, 0:1024]), reads=[rw_], writes=[rKT])
            i = 0
            for s in range(4):
                for j0 in range(0, NCB, 8):
                    nj = min(8, NCB - j0)
                    w_ = wst[i % 2]; rw_ = rwst[i % 2]; i += 1
                    dma("sp", w_[:, 0:nj * 128].rearrange("p (j d) -> p j d", d=128),
                        cvin[L][s, j0 * 128:(j0 + nj) * 128, :].rearrange("(j p) d -> p j d", p=128), writes=[rw_])
                    op("pool", lambda h: h.tensor_copy(out=V[:, s * NCB + j0:s * NCB + j0 + nj, 0:128],
                                                       in_=w_[:, 0:nj * 128].rearrange("p (j d) -> p j d", d=128)),
                       reads=[rw_], writes=[rV])

        if STOP == 1:
            kb.barrier()
            return nc
        CKN[0] = 0
        try:
          for L in range(4):
              kind = L % 3
              with contextlib.ExitStack() as ls:
                  kb.barrier()
                  W, rW = load_weights(L, ls)
                  if STOP == 2:
                      kb.barrier()
                      return nc
                  Wo = rWo = None
                  if L >= 1:
                      Wo, rWo = load_wout(L - 1, ls)
                      if L == 2:
                          ck('wout L2')
                  if kind in (0, 2):
                      KT = kb.sb("KT%d" % L, [128, NT], BF16, ls); rKT = R()
                      V = kb.sb("V%d" % L, [128, NVT, 130], BF16, ls); rV = R()
                      op("pool", lambda h: h.memset(V[:], 1.0), writes=[rV])
                      if L == 2:
                          ck('vmemset L2')
                      if kind == 0:
                          compute_lam(L)
                          bcast_row(Gt, subg[L], 128, 1.0 - LAM_INIT[L])
                      def prep(G_):
                          if G_.kind == "sample":
                              load_cache(L, KT, rKT, V, rV)
                          yield from phase_a(L, G_, Wo, rWo, ob=(1, 1))
                          yield from proj_attn(L, G_, W, rW, KT, rKT, V, rV, (64 if kind == 0 else 128) ** -0.5, kind == 0, 1)

                      for _ in prep(groups[0]):
                          pass
                      for gi, G in enumerate(groups):
                          nxt = groups[gi + 1] if gi + 1 < len(groups) else None
                          gen = prep(nxt) if (nxt is not None and nxt.kind != "sample") else None
                          nch = 5 * len(nxt.tiles) + 2 if gen is not None else 0
                          if kind == 0:
                              da_b(L, G, KT, rKT, V, rV, gen, nch)
                          else:
                              sb_b(L, G, KT, rKT, V, rV, gen, nch)
                          ck('phaseB L%d G%d' % (L, G.idx))
                          if nxt is not None and nxt.kind == "sample":
                              for _ in prep(nxt):
                                  pass
                  else:
                      S = kb.sb("S", [128, 2, 512], F32, ls); rS = R()
                      Sb = kb.sb("Sb", [128, 2, 512], BF16, ls); rSb = R()
                      qT2 = kb.sb("qT2", [128, 2, 512], BF16, ls)
                      qTr = [qT, qT2]; rqTr = [R(), R()]
                      kTr = [kb.sb("kTr%d" % i, [128, 2, 512], BF16, ls) for i in range(2)]; rkTr = [R(), R()]
                      vr = [kb.sb("vr%d" % i, [128, 4, 512], BF16, ls) for i in range(2)]; rvr = [R(), R()]
                      sg2 = kb.sb("sg2", [128, 4, 512], F32, ls)
                      sgr = [sg, sg2]; rsgr = [R(), R()]
                      cs = kb.sb("cs", [128, 512], F32, ls); sn = kb.sb("sn", [128, 512], F32, ls); rcs = R(); rsn = R()
                      qd = kb.sb("qd", [128, 2, 128], BF16, ls); rqd = R()
                      kd = kb.sb("kd", [128, 2, 128], BF16, ls); rkd = R()
                      sTm = kb.sb("sTm", [128, 128], BF16, ls); rsT = R()
                      og4 = kb.sb("og4", [128, 512], BF16, ls); rog4 = R()
                      bcast_row(Gt, gn1, 512, 1.0)
                      op("dve", lambda h: h.memset(S[:], 0.0), writes=[rS])
                      op("dve", lambda h: h.memset(Sb[:], 0.0), writes=[rSb])

                      def prep_ret(G_):
                          n = G_.ncols; par = G_.idx % 2
                          yield from phase_a(L, G_, Wo, rWo, ob=(4, 5))
                          dma("sp", cs[:, 0:n], cosin[:, G_.c0:G_.c0 + n], writes=[rcs])
                          dma("sp", sn[:, 0:n], sinin[:, G_.c0:G_.c0 + n], writes=[rsn])
                          for which, dst, rdst, sc in ((0, qTr[par], rqTr[par], 1.0), (1, kTr[par], rkTr[par], 1.0 / 16)):
                              b0 = 4; b1 = 5
                              proj_fm(W, rW, 256 * which, b0, n)
                              yield
                              proj_fm(W, rW, 256 * which + 128, b1, n)
                              yield
                              t0, t1, t2, t3 = TF[0], TF[1], TF[2], TF[3]
                              op("dve", lambda h: h.scalar_tensor_tensor(out=t0[:, 0:n], in0=B[b0][:, 0:n], scalar=sc, in1=cs[:, 0:n],
                                                                         op0=ALU.mult, op1=ALU.mult), reads=[rB[b0], rcs], writes=[rTF[0]])
                              op("dve", lambda h: h.scalar_tensor_tensor(out=t1[:, 0:n], in0=B[b1][:, 0:n], scalar=sc, in1=sn[:, 0:n],
                                                                         op0=ALU.mult, op1=ALU.mult), reads=[rB[b1], rsn], writes=[rTF[1]])
                              op("pool", lambda h: h.tensor_tensor(out=dst[:, 0, 0:n], in0=t0[:, 0:n], in1=t1[:, 0:n], op=ALU.subtract),
                                 reads=[rTF[0], rTF[1]], writes=[rdst])
                              op("dve", lambda h: h.scalar_tensor_tensor(out=t2[:, 0:n], in0=B[b0][:, 0:n], scalar=sc, in1=sn[:, 0:n],
                                                                         op0=ALU.mult, op1=ALU.mult), reads=[rB[b0], rsn], writes=[rTF[2]])
                              op("dve", lambda h: h.scalar_tensor_tensor(out=t3[:, 0:n], in0=B[b1][:, 0:n], scalar=sc, in1=cs[:, 0:n],
                                                                         op0=ALU.mult, op1=ALU.mult), reads=[rB[b1], rcs], writes=[rTF[3]])
                              op("pool", lambda h: h.tensor_tensor(out=dst[:, 1, 0:n], in0=t2[:, 0:n], in1=t3[:, 0:n], op=ALU.add),
                                 reads=[rTF[2], rTF[3]], writes=[rdst])
                              yield
                          for ti, (ofs, nr) in enumerate(G_.tiles):
                              proj_tm(W, rW, 512, 512, 6, ofs, nr)
                              op("act", lambda h: h.activation(out=vr[par][0:nr, ti, :], in_=B[6][0:nr, :], func=AF.Copy),
                                 reads=[rB[6]], writes=[rvr[par]])
                              yield
                              bk = 4 + ti % 2
                              proj_tm(W, rW, 1024, 512, bk, ofs, nr)
                              op("act", lambda h: h.activation(out=sgr[par][0:nr, ti, :], in_=B[bk][0:nr, :], func=AF.Silu),
                                 reads=[rB[bk]], writes=[rsgr[par]])
                              op("pool", lambda h: h.tensor_tensor(out=sgr[par][0:nr, ti, :], in0=sgr[par][0:nr, ti, :], in1=Gt[0:nr, :], op=ALU.mult),
                                 reads=[rsgr[par], rGt], writes=[rsgr[par]])
                              yield

                      def pump(gen, k):
                          if gen is not None:
                              for _ in range(k):
                                  next(gen, None)

                      for _ in prep_ret(groups[0]):
                          pass
                      for gi, G in enumerate(groups):
                          n = G.ncols; par = G.idx % 2
                          nxt = groups[gi + 1] if gi + 1 < len(groups) else None
                          gen = prep_ret(nxt) if nxt is not None else None
                          nch = (4 * len(nxt.tiles) + 6 + 2 * len(nxt.tiles)) if nxt is not None else 0
                          pk = -(-nch // (4 * len(G.tiles)))
                          qT_ = qTr[par]; rq_ = rqTr[par]; kT_ = kTr[par]; rk_ = rkTr[par]
                          vr_ = vr[par]; rv_ = rvr[par]; sg_ = sgr[par]; rsg_ = rsgr[par]
                          for ti, (ofs, nr) in enumerate(G.tiles):
                              cn = ncol(nr)
                              if G.kind == "sample":
                                  dma("sp", S[:], st1[ti].rearrange("(c p) e -> p c e", p=128), writes=[rS])
                                  op("act", lambda h: h.activation(out=Sb[:], in_=S[:], func=AF.Copy), reads=[rS], writes=[rSb])
                              for c in range(2):
                                  op("pe", lambda h: h.matmul(out=B[0][0:nr, 0:nr], lhsT=kT_[:, c, ofs:ofs + nr], rhs=qT_[:, c, ofs:ofs + nr],
                                                              start=(c == 0), stop=(c == 1)), reads=[rk_, rq_], writes=[rB[0]])
                              op("dve", lambda h: h.tensor_tensor(out=sTm[0:nr, 0:nr], in0=B[0][0:nr, 0:nr], in1=decT[0:nr, 0:nr], op=ALU.mult),
                                 reads=[rB[0], rcst], writes=[rsT])
                              for c in range(2):
                                  op("pool", lambda h: h.tensor_tensor(out=qd[:, c, 0:nr], in0=qT_[:, c, ofs:ofs + nr], in1=dq[:, 0:nr], op=ALU.mult),
                                     reads=[rq_, rcst], writes=[rqd])
                                  op("pe", lambda h: h.transpose(out=TRb[0:nr, c, :], in_=kT_[:, c, ofs:ofs + nr], identity=identb[:, :]),
                                     reads=[rk_, rcb], writes=[rTR])
                              op("dve", lambda h: h.tensor_scalar(out=kd[0:nr, :, :], in0=TRb[0:nr, 0:2, :], scalar1=cst[0:nr, 768 + cn:769 + cn],
                                                                  scalar2=None, op0=ALU.mult), reads=[rTR, rcst], writes=[rkd])
                              pump(gen, pk)
                              op("pe", lambda h: h.matmul(out=B[1][0:nr, :], lhsT=sTm[0:nr, 0:nr], rhs=vr_[0:nr, ti, :], start=True, stop=False),
                                 reads=[rsT, rv_], writes=[rB[1]])
                              for c in range(2):
                                  op("pe", lambda h: h.matmul(out=B[1][0:nr, :], lhsT=qd[:, c, 0:nr], rhs=Sb[:, c, :], start=False, stop=(c == 1)),
                                     reads=[rqd, rSb], writes=[rB[1]])
                              for c in range(2):
                                  op("pe", lambda h: h.matmul(out=B[2 + c][:, :], lhsT=kd[0:nr, c, :], rhs=vr_[0:nr, ti, :], start=True, stop=True),
                                     reads=[rkd, rv_], writes=[rB[2 + c]])
                                  op("dve", lambda h: h.scalar_tensor_tensor(out=S[:, c, :], in0=S[:, c, :], scalar=cst[:, 771 + cn:772 + cn],
                                                                             in1=B[2 + c][:, :], op0=ALU.mult, op1=ALU.add),
                                     reads=[rS, rcst, rB[2 + c]], writes=[rS])
                              op("act", lambda h: h.activation(out=Sb[:], in_=S[:], func=AF.Copy), reads=[rS], writes=[rSb])
                              pump(gen, pk)
                              if G.kind == "sample":
                                  dma("sp", rets[ti].rearrange("(c p) e -> p c e", p=128), S[:], reads=[rS], writes=[ryo])
                              op("act", lambda h: h.activation(out=sqj[0:nr, 0:512], in_=B[1][0:nr, :], func=AF.Square, accum_out=rd[0:nr, 0:1]),
                                 reads=[rB[1]], writes=[rsqj, rrd])
                              op("act", lambda h: h.activation(out=rd[0:nr, 1:2], in_=rd[0:nr, 0:1], func=AF.Sqrt, scale=1.0 / 512,
                                                               bias=epst[0:nr, 0:1]), reads=[rrd, rcb], writes=[rrd])
                              op("dve", lambda h: h.reciprocal(out=rd[0:nr, 2:3], in_=rd[0:nr, 1:2]), reads=[rrd], writes=[rrd])
                              op("dve", lambda h: h.scalar_tensor_tensor(out=og4[0:nr, :], in0=B[1][0:nr, :], scalar=rd[0:nr, 2:3],
                                                                         in1=sg_[0:nr, ti, :], op0=ALU.mult, op1=ALU.mult),
                                 reads=[rB[1], rrd, rsg_], writes=[rog4])
                              pump(gen, pk)
                              for e in range(4):
                                  op("pe", lambda h: h.transpose(out=TRb[:, 4 + e, 0:nr], in_=og4[0:nr, e * 128:(e + 1) * 128],
                                                                 identity=identb[0:nr, 0:nr]), reads=[rog4, rcb], writes=[rTR])
                              op("act", lambda h: h.activation(out=ogT[:, 0:4, ofs:ofs + nr], in_=TRb[:, 4:8, 0:nr], func=AF.Copy),
                                 reads=[rTR], writes=[rogT])
                              pump(gen, pk)
                          if G.idx == NGF:
                              dma("sp", retp.rearrange("(c p) e -> p c e", p=128), S[:], reads=[rS], writes=[ryo])
                          dma("sp", bounce[L].ap()[G.idx][:, 0:n].rearrange("(c p) n -> p c n", p=128), ogT[:, 0:4, 0:n],
                              reads=[rogT], writes=[rbnc[L][G.idx]])
                          group_coll(L, G)
                          if gen is not None:
                              for _ in gen:
                                  pass
                  ck('endlayer L%d' % L)
                  kb.barrier()

        except StopBuild:
            kb.barrier()
            return nc
        with contextlib.ExitStack() as ls:
            kb.barrier()
            Wo, rWo = load_wout(3, ls)
            GF = [kb.sb("GF%d" % i, [128, 512], F32, ls) for i in range(2)]; rGF = rGt
            tcnt = [0]
            xtf = [kb.sb("xtf%d" % i, [128, 1024], F32, ls) for i in range(4)]; rxtf = [R() for _ in range(4)]
            ssf = [kb.sb("ssf%d" % i, [128, 4], F32, ls) for i in range(4)]; rssf = [R() for _ in range(4)]
            ogaf = [kb.sb("ogaf%d" % i, [128, 4, 128], BF16, ls) for i in range(4)]; rogaf = [R() for _ in range(4)]
            bcast_row(GF[0], ngf[:, 0:512], 512)
            bcast_row(GF[1], ngf[:, 512:1024], 512)
            final_pass(Wo, rWo)
            kb.barrier()
    return nc


_CACHE = {}


def host_consts(h, SEQ, PAST):
    NP = 16 + SEQ
    NT = NP + 128
    cst = np.zeros((128, 776), np.float32)
    idx = np.arange(128)
    cst[:, 0:128] = np.eye(128, dtype=np.float32)
    cst[:, 128:256] = -(idx[:, None] >= idx[None, :]).astype(np.float32)
    cst[:, 256:384] = -1.0
    cst[:, 384:512] = (idx[:, None] < idx[None, :]).astype(np.float32)
    lg = np.log(np.float32(1.0) - np.float32(2.0) ** np.float32(-5.0 - h)).astype(np.float32)
    rel = (idx[None, :] - idx[:, None]).astype(np.float32)
    cst[:, 512:640] = np.where(rel >= 0, np.exp(np.maximum(rel, 0.0) * lg), 0.0).astype(np.float32)
    cst[:, 640:768] = np.exp((idx.astype(np.float32) + 1.0) * lg)[None, :]
    for ci, n in enumerate((128, 16, 32)):
        col = np.where(idx < n, np.exp((n - 1.0 - idx.astype(np.float32)) * lg), 0.0)
        cst[:, 768 + ci] = col
        cst[:, 771 + ci] = np.exp(np.float32(n) * lg)
    pos = np.concatenate([np.arange(NP, dtype=np.float32) - 16.0,
                          np.float32(PAST) + (np.arange(128) % 32).astype(np.float32)]).astype(np.float32)
    inv = (np.float32(10000.0) ** (-np.linspace(0.0, 1.0, 128, dtype=np.float32))).astype(np.float32)
    ang = (inv[:, None] * pos[None, :]).astype(np.float32)
    return cst, np.cos(ang).astype(np.float32), np.sin(ang).astype(np.float32)


def make_in_maps(inp, SEQ, PAST):
    f = lambda a: np.ascontiguousarray(np.asarray(a, dtype=np.float32))
    w_in = [inp["w_in_0"], inp["w_in_1"], inp["w_in_2"], inp["w_in_3"]]
    w_out = [inp["w_out_0"], inp["w_out_1"], inp["w_out_2"], inp["w_out_3"]]
    norm_g = [inp["norm_g_0"], inp["norm_g_1"], inp["norm_g_2"], inp["norm_g_3"]]
    lam = {0: (inp["lam_q1_0"], inp["lam_k1_0"], inp["lam_q2_0"], inp["lam_k2_0"]),
           3: (inp["lam_q1_3"], inp["lam_k1_3"], inp["lam_q2_3"], inp["lam_k2_3"])}
    subln = {0: inp["subln_g_0"], 3: inp["subln_g_3"]}
    cache_k = {0: inp["cache_k_l0"], 2: inp["cache_k_l2"], 3: inp["cache_k_l3"]}
    cache_v = {0: inp["cache_v_l0"], 2: inp["cache_v_l2"], 3: inp["cache_v_l3"]}
    in_maps = []
    for c in range(8):
        b, h = c // 4, c % 4
        m = {}
        m["xin"] = f(np.concatenate([inp["meta_tokens"], inp["x_prompt"][b],
                                     np.asarray(inp["x_sample"][4 * b:4 * b + 4]).reshape(128, 1024)], 0))
        for L in range(4):
            w = np.asarray(w_in[L])
            if L == 1:
                cols = [w[:, h * 256:(h + 1) * 256], w[:, 1024 + h * 256:1024 + (h + 1) * 256],
                        w[:, 2048 + h * 512:2048 + (h + 1) * 512], w[:, 4096 + h * 512:4096 + (h + 1) * 512]]
            else:
                cols = [w[:, i * 512 + h * 128:i * 512 + (h + 1) * 128] for i in range(4)]
            m["win%d" % L] = f(np.concatenate(cols, 1))
            m["ng%d" % L] = f(np.asarray(norm_g[L]).reshape(8, 128).T)
            m["wout%d" % L] = f(w_out[L])
        for L in (0, 3):
            m["lam%d" % L] = f(np.concatenate([np.asarray(a) for a in lam[L]]).reshape(1, 256))
            m["subg%d" % L] = f(np.asarray(subln[L]).reshape(1, 128))
            ck = np.asarray(cache_k[L])[4 * b:4 * b + 4, :, 2 * h:2 * h + 2, :]
            m["ck%d" % L] = f(ck.transpose(2, 3, 0, 1).reshape(128, 4 * PAST))
            m["cv%d" % L] = f(np.asarray(cache_v[L])[4 * b:4 * b + 4, :, h, :])
        ck = np.asarray(cache_k[2])[4 * b:4 * b + 4, :, h, :]
        m["ck2"] = f(ck.transpose(2, 0, 1).reshape(128, 4 * PAST))
        m["cv2"] = f(np.asarray(cache_v[2])[4 * b:4 * b + 4, :, h, :])
        m["gn1"] = f(np.asarray(inp["gn_g_1"])[h].reshape(1, 512))
        m["ngf"] = f(np.asarray(inp["norm_g_final"]).reshape(1, 1024))
        m["st1"] = f(np.asarray(inp["state_ret_l1"])[4 * b:4 * b + 4, h])
        cst, cs, sn = host_consts(h, SEQ, PAST)
        m["cst"] = cst; m["cos"] = cs; m["sin"] = sn
        in_maps.append(m)
    return in_maps


def assemble(res, SEQ, PAST):
    NP = 16 + SEQ
    y_p = np.zeros((2, SEQ, 1024), np.float32)
    y_s = np.zeros((8, 32, 1024), np.float32)
    kv = {}
    for L in (0, 2, 3):
        nm, dh = (8, 64) if L != 2 else (4, 128)
        kv[L] = [np.zeros((2, NP, nm, dh), np.float32), np.zeros((2, NP, 4, 128), np.float32),
                 np.zeros((8, 32, nm, dh), np.float32), np.zeros((8, 32, 4, 128), np.float32)]
    ret_p = np.zeros((2, 4, 256, 512), np.float32)
    ret_s = np.zeros((8, 4, 256, 512), np.float32)
    for c in range(8):
        b, h = c // 4, c % 4
        r = res[c]
        if h == 0:
            y = np.asarray(r["y"])
            y_p[b] = y[16:NP]
            y_s[4 * b:4 * b + 4] = y[NP:].reshape(4, 32, 1024)
        for L in (0, 2, 3):
            kT = np.asarray(r["kT%d" % L]); v = np.asarray(r["v%d" % L])
            if L != 2:
                kk = kT.reshape(2, 64, -1).transpose(2, 0, 1)
                kv[L][0][b, :, 2 * h:2 * h + 2, :] = kk[:NP]
                kv[L][2][4 * b:4 * b + 4, :, 2 * h:2 * h + 2, :] = kk[NP:].reshape(4, 32, 2, 64)
            else:
                kk = kT.T
                kv[L][0][b, :, h, :] = kk[:NP]
                kv[L][2][4 * b:4 * b + 4, :, h, :] = kk[NP:].reshape(4, 32, 128)
            kv[L][1][b, :, h, :] = v[:NP]
            kv[L][3][4 * b:4 * b + 4, :, h, :] = v[NP:].reshape(4, 32, 128)
        ret_p[b, h] = np.asarray(r["retp"])
        ret_s[4 * b:4 * b + 4, h] = np.asarray(r["rets"])
    return (y_p, y_s, kv[0][0], kv[0][1], kv[0][2], kv[0][3], ret_p, ret_s,
            kv[2][0], kv[2][1], kv[2][2], kv[2][3], kv[3][0], kv[3][1], kv[3][2], kv[3][3])


def kernel(**inputs):
    seq = int(np.asarray(inputs["x_prompt"]).shape[1])
    past = int(np.asarray(inputs["cache_k_l0"]).shape[1])
    key = (seq, past)
    if key not in _CACHE:
        _CACHE[key] = build(seq, past)
    nc = _CACHE[key]
    in_maps = make_in_maps(inputs, seq, past)
    res = run_bass_kernel_spmd(nc, in_maps, core_ids=list(range(8)))
    return assemble(res.results, seq, past)
```
